# Optimizing a Trainium2 kernel written in Bass

```python
import math
import jax, jax.numpy as jnp
from jax import lax
import numpy as np

D_MODEL = 1024
BATCH = 16
SEQ = 2048
DEPTH = 4

CHUNK = 64
Q_BLOCK = 128
HEAD_DIM = 64
H_SB = 8
H_DIFF = 4
H_CH = 8
N_PAST_CHUNKS = 8
BAND = (N_PAST_CHUNKS + 1) * CHUNK
REL_CLIP = 128
W_SB = H_SB * HEAD_DIM
W_DIFF = H_DIFF * 2 * HEAD_DIM
W_CH = H_CH * HEAD_DIM
N_BRANCH = 3
D_FF = int(math.ceil(8 * D_MODEL / 3 / 256)) * 256
RMS_EPS = 1e-6
_SIZES = (W_SB, W_SB, W_SB, W_DIFF, W_DIFF, W_DIFF, W_CH, W_CH, W_CH, N_BRANCH * D_MODEL)
N_IN = int(sum(_SIZES))
_SPLIT_POINTS = tuple(int(v) for v in np.cumsum(_SIZES)[:-1])

kernel_name = "hybrid_stickbreak_diff_chunkrel_gated"


def alibi_slopes(n):
    return np.asarray([2.0 ** (-8.0 * (i + 1) / n) for i in range(n)], dtype=np.float32)


def rms_norm(x, g):
    xf = x.astype(jnp.float32)
    y = xf * lax.rsqrt(jnp.mean(xf * xf, axis=-1, keepdims=True) + RMS_EPS)
    return (y * g.astype(jnp.float32)).astype(x.dtype)


def to_heads(t, n_heads):
    b, s, _ = t.shape
    return t.reshape(b, s, n_heads, -1).transpose(0, 2, 1, 3)


def merge_heads(t):
    b, h, s, d = t.shape
    return t.transpose(0, 2, 1, 3).reshape(b, s, h * d)


def stick_breaking_attention(q, k, v):
    b, h, s, d = q.shape
    nqb = s // Q_BLOCK
    scale = d ** -0.5
    kpos = jnp.arange(s)
    qb = q.reshape(b, h, nqb, Q_BLOCK, d).transpose(2, 0, 1, 3, 4)

    def block(args):
        q_blk, i = args
        qpos = i * Q_BLOCK + jnp.arange(Q_BLOCK)
        z = jnp.einsum('bhqd,bhkd->bhqk', q_blk, k, preferred_element_type=jnp.float32) * scale
        strict = kpos[None, :] < qpos[:, None]
        log_beta = jax.nn.log_sigmoid(z)
        log_1m = jnp.where(strict, jax.nn.log_sigmoid(-z), 0.0)
        after = lax.cumsum(log_1m, axis=3, reverse=True) - log_1m
        w = jnp.where(strict, jnp.exp(log_beta + after), 0.0)
        return jnp.einsum('bhqk,bhkd->bhqd', w.astype(v.dtype), v)

    out = lax.map(block, (qb, jnp.arange(nqb)))
    return out.transpose(1, 2, 0, 3, 4).reshape(b, h, s, d)


def diff_attention(q, k, v, lam, slopes):
    b, h, _, s, d = q.shape
    nqb = s // Q_BLOCK
    scale = d ** -0.5
    kpos = jnp.arange(s)
    qb = q.reshape(b, h, 2, nqb, Q_BLOCK, d).transpose(3, 0, 1, 2, 4, 5)

    def block(args):
        q_blk, i = args
        qpos = i * Q_BLOCK + jnp.arange(Q_BLOCK)
        allowed = (kpos[None, :] // CHUNK) <= (qpos[:, None] // CHUNK)
        dist = jnp.abs(qpos[:, None] - kpos[None, :]).astype(jnp.float32)
        bias = -slopes[:, None, None] * dist[None]
        logits = jnp.einsum('bhmqd,bhmkd->bhmqk', q_blk, k, preferred_element_type=jnp.float32) * scale
        logits = jnp.where(allowed[None, None, None], logits + bias[None, :, None], -jnp.inf)
        p = jax.nn.softmax(logits, axis=-1)
        w = p[:, :, 0] - lam * p[:, :, 1]
        return jnp.einsum('bhqk,bhkd->bhqd', w.astype(v.dtype), v)

    out = lax.map(block, (qb, jnp.arange(nqb)))
    return out.transpose(1, 2, 0, 3, 4).reshape(b, h, s, 2 * d)


def chunked_rel_attention(q, k, v, rel_bias):
    b, h, s, d = q.shape
    nc = s // CHUNK
    pad = N_PAST_CHUNKS * CHUNK
    scale = d ** -0.5
    kp = jnp.pad(k, ((0, 0), (0, 0), (pad, 0), (0, 0)))
    vp = jnp.pad(v, ((0, 0), (0, 0), (pad, 0), (0, 0)))
    qc = q.reshape(b, h, nc, CHUNK, d).transpose(2, 0, 1, 3, 4)
    koff = jnp.arange(BAND)
    rel = (pad + jnp.arange(CHUNK))[:, None] - koff[None, :]
    bias = rel_bias[:, jnp.clip(rel, -REL_CLIP, REL_CLIP) + REL_CLIP].astype(jnp.float32)

    def one_chunk(args):
        q_c, c = args
        start = c * CHUNK
        k_band = lax.dynamic_slice_in_dim(kp, start, BAND, axis=2)
        v_band = lax.dynamic_slice_in_dim(vp, start, BAND, axis=2)
        valid = (start + koff) >= pad
        logits = jnp.einsum('bhqd,bhkd->bhqk', q_c, k_band, preferred_element_type=jnp.float32) * scale
        logits = jnp.where(valid[None, None, None, :], logits + bias[None], -jnp.inf)
        p = jax.nn.softmax(logits, axis=-1)
        return jnp.einsum('bhqk,bhkd->bhqd', p.astype(v_band.dtype), v_band)

    out = lax.map(one_chunk, (qc, jnp.arange(nc)))
    return out.transpose(1, 2, 0, 3, 4).reshape(b, h, s, d)


def setup_inputs(seed: int = 0) -> dict:
    key = jax.random.key(seed)
    ks = jax.random.split(key, 17)
    n = jax.random.normal
    f32 = jnp.float32
    return {
        "x": n(ks[0], (BATCH, SEQ, D_MODEL), f32),
        "norm_mix_g": 1.0 + 0.02 * n(ks[1], (DEPTH, D_MODEL), f32),
        "w_in": n(ks[2], (DEPTH, D_MODEL, N_IN), f32) * D_MODEL ** -0.5,
        "b_gate": 0.1 * n(ks[3], (DEPTH, N_BRANCH, D_MODEL), f32),
        "qk_g_diff": 1.0 + 0.02 * n(ks[4], (DEPTH, 2, HEAD_DIM), f32),
        "lambda_qk": 0.1 * n(ks[5], (DEPTH, 4, HEAD_DIM), f32),
        "subln_g": 1.0 + 0.02 * n(ks[6], (DEPTH, 2 * HEAD_DIM), f32),
        "qk_g_ch": 1.0 + 0.02 * n(ks[7], (DEPTH, 2, HEAD_DIM), f32),
        "rel_bias": 0.1 * n(ks[8], (DEPTH, H_CH, 2 * REL_CLIP + 1), f32),
        "w_branch_sb": n(ks[9], (DEPTH, W_SB, D_MODEL), f32) * W_SB ** -0.5,
        "w_branch_diff": n(ks[10], (DEPTH, W_DIFF, D_MODEL), f32) * W_DIFF ** -0.5,
        "w_branch_ch": n(ks[11], (DEPTH, W_CH, D_MODEL), f32) * W_CH ** -0.5,
        "w_out": n(ks[12], (DEPTH, D_MODEL, D_MODEL), f32) * D_MODEL ** -0.5,
        "norm_ffn_g": 1.0 + 0.02 * n(ks[13], (DEPTH, D_MODEL), f32),
        "w_gu": n(ks[14], (DEPTH, D_MODEL, 2 * D_FF), f32) * D_MODEL ** -0.5,
        "w_down": n(ks[15], (DEPTH, D_FF, D_MODEL), f32) * D_FF ** -0.5,
    }


def reference(x, norm_mix_g, w_in, b_gate, qk_g_diff, lambda_qk, subln_g, qk_g_ch, rel_bias,
              w_branch_sb, w_branch_diff, w_branch_ch, w_out, norm_ffn_g, w_gu, w_down):
    b, s, _ = x.shape
    slopes = jnp.asarray(alibi_slopes(H_DIFF))
    for l in range(DEPTH):
        h = rms_norm(x, norm_mix_g[l])
        proj = h @ w_in[l]
        (q_a, k_a, v_a, q_b, k_b, v_b, q_c, k_c, v_c, g_lin) = jnp.split(proj, _SPLIT_POINTS, axis=-1)

        o_a = merge_heads(stick_breaking_attention(to_heads(q_a, H_SB), to_heads(k_a, H_SB), to_heads(v_a, H_SB)))

        qb = rms_norm(q_b.reshape(b, s, H_DIFF, 2, HEAD_DIM).transpose(0, 2, 3, 1, 4), qk_g_diff[l, 0])
        kb = rms_norm(k_b.reshape(b, s, H_DIFF, 2, HEAD_DIM).transpose(0, 2, 3, 1, 4), qk_g_diff[l, 1])
        vb = to_heads(v_b, H_DIFF)
        lam_init = 0.8 - 0.6 * math.exp(-0.3 * l)
        lq = lambda_qk[l].astype(jnp.float32)
        lam = jnp.exp(jnp.sum(lq[0] * lq[1])) - jnp.exp(jnp.sum(lq[2] * lq[3])) + lam_init
        ob = diff_attention(qb, kb, vb, lam, slopes)
        o_b = merge_heads(rms_norm(ob, subln_g[l]) * (1.0 - lam_init))

        qc = rms_norm(to_heads(q_c, H_CH), qk_g_ch[l, 0])
        kc = rms_norm(to_heads(k_c, H_CH), qk_g_ch[l, 1])
        o_c = merge_heads(chunked_rel_attention(qc, kc, to_heads(v_c, H_CH), rel_bias[l]))

        gates = jax.nn.sigmoid(g_lin.reshape(b, s, N_BRANCH, D_MODEL) + b_gate[l])
        merged = (gates[:, :, 0] * (o_a @ w_branch_sb[l])
                  + gates[:, :, 1] * (o_b @ w_branch_diff[l])
                  + gates[:, :, 2] * (o_c @ w_branch_ch[l]))
        x = x + merged @ w_out[l]

        h2 = rms_norm(x, norm_ffn_g[l])
        gate, up = jnp.split(h2 @ w_gu[l], 2, axis=-1)
        x = x + (jax.nn.silu(gate) * up) @ w_down[l]
    return x
```

```python
import math
import numpy as np
import ml_dtypes
from contextlib import ExitStack
import concourse.bass as bass
import concourse.mybir as mybir
from concourse.bass_utils import run_bass_kernel_spmd

F32 = mybir.dt.float32
BF16 = mybir.dt.bfloat16
AF = mybir.ActivationFunctionType
ALU = mybir.AluOpType

NCORES = 8
D = 1024
SEQ = 2048
T = 4096
DEPTH = 4
DFF = 2816
NIN = 7680
EPS = 1e-6
NEG = -30000.0
SLOPES = [2.0 ** (-8.0 * (i + 1) / 4) for i in range(4)]

C_ID, C_ONE, C_BLK, C_NLU, C_NMA, C_NEG1, C_ZERO, C_CORR = 0, 128, 256, 384, 512, 640, 768, 896
C_E0, C_HM0, C_HM1 = 1408, 1536, 1664
NCST = 1792
P_GMIX, P_GFFN, P_BG, P_GQD, P_GQC, P_GSUB, P_LQK = 0, 32, 64, 160, 168, 176, 180
NPRM = 180 + 1024


def ts(i, n):
    return slice(i * n, (i + 1) * n)


class Buf:
    __slots__ = ("name", "w", "r")

    def __init__(self, name=""):
        self.name = name
        self.w = None
        self.r = {}


class Prog:
    ENG = ("pe", "act", "dve", "pool", "sp")

    def __init__(self, nc, es, block):
        self.nc = nc
        self.es = es
        self.block = block
        self.streams = {e: [] for e in self.ENG}
        self.sems = {}
        self.cnt = {}
        self.known = {e: {} for e in self.ENG}
        for e in self.ENG:
            self.sems[e] = es.enter_context(nc.semaphore("s_" + e))
            self.cnt[e] = 0
        self.nsem = 0
        self.pools = {}
        self.uid = 0
        self.ninst = {e: 0 for e in self.ENG}

    def new_sem(self, pes=None, kind="sp"):
        if pes is not None:
            pool = self.pools.setdefault(kind, [])
            if pool:
                name = pool.pop()
            else:
                name = self.new_sem()
            pes.callback(pool.append, name)
            return name
        self.nsem += 1
        name = "d%d" % self.nsem
        self.sems[name] = self.es.enter_context(self.nc.semaphore(name))
        self.cnt[name] = 0
        return name

    def _deps(self, eng, reads, writes):
        waits = {}
        for b in reads:
            if b.w is not None and waits.get(b.w[0], 0) < b.w[1]:
                waits[b.w[0]] = b.w[1]
        for b in writes:
            if b.w is not None and waits.get(b.w[0], 0) < b.w[1]:
                waits[b.w[0]] = b.w[1]
            for k, v in b.r.items():
                if waits.get(k, 0) < v:
                    waits[k] = v
        self._emit_waits(eng, waits)

    def _emit_waits(self, eng, waits, skip_pe_self=True):
        st = self.streams[eng]
        kn = self.known[eng]
        for k, v in waits.items():
            if skip_pe_self and k == "pe" and eng == "pe":
                continue
            if kn.get(k, 0) >= v:
                continue
            kn[k] = v
            st.append(("wait", self.sems[k], v))
            self.ninst[eng] += 1

    def op(self, eng, name, kw, reads=(), writes=()):
        self._deps(eng, reads, writes)
        self.cnt[eng] += 1
        c = self.cnt[eng]
        self.streams[eng].append(("op", name, kw, self.sems[eng], 1))
        self.ninst[eng] += 1
        for b in reads:
            if b.r.get(eng, 0) < c:
                b.r[eng] = c
        for b in writes:
            b.w = (eng, c)
            b.r = {}

    def dma(self, q, semname, kw, reads=(), writes=()):
        self._deps(q, reads, writes)
        self.cnt[semname] += 16
        c = self.cnt[semname]
        self.streams[q].append(("op", "dma_start", kw, self.sems[semname], 16))
        self.ninst[q] += 1
        for b in reads:
            if b.r.get(semname, 0) < c:
                b.r[semname] = c
        for b in writes:
            b.w = (semname, c)
            b.r = {}

    def barrier(self):
        waits = {k: v for k, v in self.cnt.items() if v > 0}
        for e in self.ENG:
            self._emit_waits(e, dict(waits), skip_pe_self=False)
        self.flush()

    def flush(self):
        for en, meth in (("pe", "tensor"), ("act", "scalar"), ("dve", "vector"), ("pool", "gpsimd"), ("sp", "sync")):
            items = self.streams[en]
            if not items:
                continue
            self.streams[en] = []

            def body(e, items=items):
                for it in items:
                    if it[0] == "wait":
                        e.wait_ge(it[1], it[2])
                    else:
                        getattr(e, it[1])(**it[2]).then_inc(it[3], it[4])

            getattr(self.block, meth)(body)


class Ring:
    def __init__(self, P, es, name, n, shape, dtype, dma=False):
        self.slots = []
        for i in range(n):
            P.uid += 1
            t = es.enter_context(P.nc.sbuf_tensor("%s%d_%d" % (name, i, P.uid), shape, dtype))
            self.slots.append((t, Buf(name + str(i)), P.new_sem(es, dma if isinstance(dma, str) else "sp") if dma else None))
        self.i = 0
        self.n = n

    def next(self):
        s = self.slots[self.i % self.n]
        self.i += 1
        return s


class BankRing:
    def __init__(self, banks):
        self.b = banks
        self.i = 0

    def next(self):
        s = self.b[self.i % len(self.b)]
        self.i += 1
        return s


def lam_init(l):
    return 0.8 - 0.6 * math.exp(-0.3 * l)


def build(n_layers=DEPTH, dbg=False, stop_after=None):
    nc = bass.Bass("TRN2", target_bir_lowering=False)
    dt = lambda name, shape, dtype, kind: nc.dram_tensor(name, shape, dtype, kind=kind).ap()
    skind = "ExternalOutput" if dbg else "Internal"
    xT_d = dt("xT", [D, T], F32, "ExternalInput")
    w_in_d = dt("w_in", [DEPTH, D, NIN], F32, "ExternalInput")
    wbr_d = dt("w_br", [DEPTH, 3, 512, D], F32, "ExternalInput")
    w_out_d = dt("w_out", [DEPTH, D, D], F32, "ExternalInput")
    w_gu_d = dt("w_gu", [DEPTH, D, 2 * DFF], F32, "ExternalInput")
    w_dn_d = dt("w_down", [DEPTH, DFF, D], F32, "ExternalInput")
    rbt_d = dt("rbt", [DEPTH, 8, 128, 640], F32, "ExternalInput")
    prm_d = dt("prm", [128, NPRM], F32, "ExternalInput")
    cst_d = dt("cst", [128, NCST], BF16, "ExternalInput")
    augq_d = dt("augq", [4, 4, SEQ], BF16, "ExternalInput")
    augk_d = dt("augk", [4, 4, SEQ], BF16, "ExternalInput")
    maskc_d = dt("maskc", [128, 640], F32, "ExternalInput")
    yT_d = dt("yT", [D, T], F32, "ExternalOutput")
    xs_d = dt("xs", [D, T], F32, skind)
    qk_d = dt("qk", [4608, T], BF16, skind)
    v_d = dt("vs", [3, T, 512], BF16, skind)
    gt_d = dt("gts", [3072, T], BF16, skind)
    o_d = dt("os", [1536, T], BF16, skind)
    ac_d = dt("acs", [DFF, T], BF16, skind)
    kpa_d = dt("kpa", [1024, T], BF16, "Internal")
    kpc_d = dt("kpc", [1024, T], BF16, "Internal")
    vpa_d = dt("vpa", [T, 1024], BF16, "Internal")
    vpc_d = dt("vpc", [T, 1024], BF16, "Internal")
    h1_d = dt("h1s", [D, T], BF16, "Internal")
    h2_d = dt("h2s", [D, T], BF16, "Internal")

    with ExitStack() as es:
        block = es.enter_context(nc.Block())
        P = Prog(nc, es, block)
        def sb(n, s, d, e=es):
            P.uid += 1
            return e.enter_context(nc.sbuf_tensor("%s_%d" % (n, P.uid), s, d))
        banks = []
        for i in range(8):
            banks.append((es.enter_context(nc.psum_tensor("bank%d" % i, [128, 512], F32)), Buf("bank%d" % i)))
        cst = sb("cst_s", [128, NCST], BF16)
        prm = sb("prm_s", [128, NPRM], F32)
        lam_s = sb("lam_s", [128, 16], F32)
        arena = sb("arena", [128, 22, D], BF16)
        ARENA = [Buf("ar%d" % i) for i in range(22)]
        arst = Ring(P, es, "arst", 2, [128, 1, D], F32, dma=True)
        arcnt = [0]

        def prefetch(src3, chunks, engs=("dve", "act")):
            for (c_src, c_dst) in chunks:
                st, ST, sem = arst.next()
                P.dma("sp", sem, dict(out=st[:, 0, :], in_=src3[:, c_src, :]), writes=[ST])
                arcnt[0] += 1
                en = engs[arcnt[0] % len(engs)]
                if en == "dve":
                    P.op("dve", "tensor_copy", dict(out=arena[:, c_dst, :], in_=st[:, 0, :]), reads=[ST], writes=[ARENA[c_dst]])
                else:
                    P.op("act", "activation", dict(out=arena[:, c_dst, :], in_=st[:, 0, :], func=AF.Copy), reads=[ST], writes=[ARENA[c_dst]])
        CST = Buf("cst")
        PRM = Buf("prm")
        LAM = Buf("lam")
        s0 = P.new_sem()
        P.dma("sp", s0, dict(out=cst[:], in_=cst_d), writes=[CST])
        P.dma("sp", s0, dict(out=prm[:], in_=prm_d), writes=[PRM])
        ident = cst[:, C_ID:C_ID + 128]
        ones = cst[:, C_ONE:C_ONE + 128]
        blk64 = cst[:, C_BLK:C_BLK + 128]
        nlu = cst[:, C_NLU:C_NLU + 128]
        nma = cst[:, C_NMA:C_NMA + 128]
        neg1 = cst[:, C_NEG1:C_NEG1 + 128]
        zer = cst[:, C_ZERO:C_ZERO + 128]
        e0m = cst[:, C_E0:C_E0 + 128]
        hm = [cst[:, C_HM0:C_HM0 + 128], cst[:, C_HM1:C_HM1 + 128]]

        with ExitStack() as pes:
            tmp = sb("lq_tmp", [128, 64], F32, pes)
            sacc = sb("lq_acc", [128, 8], F32, pes)
            TMP = Buf()
            SACC = Buf()
            for l in range(DEPTH):
                for k in range(2):
                    a = prm[:, P_LQK + l * 256 + (2 * k) * 64: P_LQK + l * 256 + (2 * k + 1) * 64]
                    b = prm[:, P_LQK + l * 256 + (2 * k + 1) * 64: P_LQK + l * 256 + (2 * k + 2) * 64]
                    P.op("dve", "scalar_tensor_tensor", dict(out=tmp[:], in0=a, scalar=1.0, in1=b, op0=ALU.mult, op1=ALU.mult,
                                                             accum_out=sacc[:, 2 * l + k:2 * l + k + 1]), reads=[PRM], writes=[TMP, SACC])
            P.op("act", "activation", dict(out=sacc[:], in_=sacc[:], func=AF.Exp), writes=[SACC])
            for l in range(DEPTH):
                P.op("dve", "scalar_tensor_tensor", dict(out=lam_s[:, l:l + 1], in0=sacc[:, 2 * l + 1:2 * l + 2], scalar=-lam_init(l),
                                                         in1=sacc[:, 2 * l:2 * l + 1], op0=ALU.add, op1=ALU.subtract), reads=[SACC], writes=[LAM])
            P.barrier()

        def load_cast(pes, name, dst, DSTL, src3, nchunk, width, ring=None):
            per = max(1, 2048 // width)
            if ring is None:
                ring = Ring(P, pes, name, 2, [128, per, width], F32, dma=True)
            c = 0
            k = 0
            while c < nchunk:
                n = min(per, nchunk - c)
                st, ST, sem = ring.next()
                P.dma("sp", sem, dict(out=st[:, 0:n, :], in_=src3[:, c:c + n, :]), writes=[ST])
                for i in range(n):
                    k += 1
                    if k % 2:
                        P.op("dve", "tensor_copy", dict(out=dst[:, c + i, :], in_=st[:, i, :]), reads=[ST], writes=[DSTL[c + i]])
                    else:
                        P.op("act", "activation", dict(out=dst[:, c + i, :], in_=st[:, i, :], func=AF.Copy), reads=[ST], writes=[DSTL[c + i]])
                c += n

        def norm_epi_b(pes_rings, t, x, X, sq, SQ, gcol, hout_d, br):
            sqr, rsr, hr = pes_rings
            bk, BK = br.next()
            for c in range(8):
                P.op("pe", "matmul", dict(out=bk[:], lhsT=ones, rhs=sq[:, c, :], start=(c == 0), stop=(c == 7)), reads=[SQ, CST], writes=[BK])
            r, R, _ = rsr.next()
            P.op("act", "activation", dict(out=r[:], in_=bk[:], func=AF.Ln, bias=EPS, scale=1.0 / D), reads=[BK], writes=[R])
            P.op("act", "activation", dict(out=r[:], in_=r[:], func=AF.Exp, scale=-0.5), writes=[R])
            hh, HH, hsem = hr.next()
            for c in range(8):
                P.op("dve", "scalar_tensor_tensor", dict(out=hh[:, c, :], in0=x[:, c, :], scalar=prm[:, gcol + c:gcol + c + 1],
                                                         in1=r[:], op0=ALU.mult, op1=ALU.mult), reads=[X, R, PRM], writes=[HH])
            P.dma("pool", hsem, dict(out=hout_d.rearrange("(c p) t -> p c t", p=128)[:, :, ts(t, 512)], in_=hh[:]), reads=[HH])

        def epilogue_rings(pes, nm):
            return (Ring(P, pes, nm + "sq", 2, [128, 8, 512], BF16), Ring(P, pes, nm + "rs", 2, [128, 512], F32),
                    Ring(P, pes, nm + "hh", 1, [128, 8, 512], BF16, dma="pool"))

        def load_h(h_d, hT, HT, les):
            hv = h_d.rearrange("(c p) t -> p c t", p=128)
            for t in range(8):
                sem = P.new_sem(les)
                P.dma("sp", sem, dict(out=hT[:, :, ts(t, 512)], in_=hv[:, :, ts(t, 512)]), writes=[HT[t]])

        def phase_norm(src_d, gcol, hT, HT):
            with ExitStack() as pes:
                xr = Ring(P, pes, "nx", 2, [128, 8, 512], F32, dma=True)
                sq = Ring(P, pes, "nsq", 2, [128, 8, 512], BF16)
                rs = Ring(P, pes, "nrs", 2, [128, 512], F32)
                br = BankRing(banks[0:2])
                xv = src_d.rearrange("(c p) t -> p c t", p=128)
                for t in range(8):
                    x, X, xsem = xr.next()
                    P.dma("sp", xsem, dict(out=x[:], in_=xv[:, :, ts(t, 512)]), writes=[X])
                    s, S, _ = sq.next()
                    P.op("act", "activation", dict(out=s[:], in_=x[:], func=AF.Square), reads=[X], writes=[S])
                    bk, BK = br.next()
                    for c in range(8):
                        P.op("pe", "matmul", dict(out=bk[:], lhsT=ones, rhs=s[:, c, :], start=(c == 0), stop=(c == 7)), reads=[S, CST], writes=[BK])
                    r, R, _ = rs.next()
                    P.op("act", "activation", dict(out=r[:], in_=bk[:], func=AF.Ln, bias=EPS, scale=1.0 / D), reads=[BK], writes=[R])
                    P.op("act", "activation", dict(out=r[:], in_=r[:], func=AF.Exp, scale=-0.5), writes=[R])
                    for c in range(8):
                        P.op("dve", "scalar_tensor_tensor", dict(out=hT[:, c, ts(t, 512)], in0=x[:, c, :], scalar=prm[:, gcol + c:gcol + c + 1],
                                                                 in1=r[:], op0=ALU.mult, op1=ALU.mult), reads=[X, R, PRM], writes=[HT[t]])
                P.barrier()

        def phase_proj(l, hT, HT, pre=None):
            with ExitStack() as pes:
                wfr = Ring(P, pes, "pwf", 2, [128, 8, 512], F32, dma=True)
                wbr = Ring(P, pes, "pwb", 2, [128, 8, 512], BF16)
                stg = Ring(P, pes, "pst", 3, [128, 4, 512], BF16, dma="pool")
                sqr = Ring(P, pes, "psq", 3, [128, 512], BF16)
                rsr = Ring(P, pes, "prs", 2, [128, 512], F32)
                mb = BankRing(banks[0:5])
                sb2 = BankRing(banks[5:8])
                wv = w_in_d[l].rearrange("(kc p) n -> p kc n", p=128)
                evi = 0
                pending = []
                for sl in range(15):
                    wf, WF, wsem = wfr.next()
                    P.dma("sp", wsem, dict(out=wf[:], in_=wv[:, :, ts(sl, 512)]), writes=[WF])
                    if sl == 0 and pre is not None:
                        pre()
                    wb, WB, _ = wbr.next()
                    P.op("dve", "tensor_copy", dict(out=wb[:], in_=wf[:]), reads=[WF], writes=[WB])
                    if sl in (2, 5, 8):
                        brn = (sl - 2) // 3
                        for tb4 in range(8):
                            st, ST, ssem = stg.next()
                            for j in range(4):
                                tb = tb4 * 4 + j
                                bk, BK = mb.next()
                                for kc in range(8):
                                    P.op("pe", "matmul", dict(out=bk[:], lhsT=hT[:, kc, ts(tb, 128)], rhs=wb[:, kc, :], start=(kc == 0), stop=(kc == 7)),
                                         reads=[HT[tb // 4], WB], writes=[BK])
                                evi += 1
                                if evi % 2:
                                    P.op("act", "activation", dict(out=st[:, j, :], in_=bk[:], func=AF.Copy), reads=[BK], writes=[ST])
                                else:
                                    P.op("dve", "tensor_copy", dict(out=st[:, j, :], in_=bk[:]), reads=[BK], writes=[ST])
                            if brn == 1:
                                P.dma("pool", ssem, dict(out=v_d[brn][ts(tb4, 512), :].rearrange("(j p) n -> p j n", p=128), in_=st[:]), reads=[ST])
                            elif brn == 0:
                                for j in range(4):
                                    rows = vpa_d[ts(tb4 * 4 + j, 128), :].rearrange("p (hp c) -> p hp c", c=256)
                                    srcv = st[:, j, :].rearrange("p (hp e d) -> p hp e d", e=2, d=64)
                                    for e_ in range(2):
                                        P.dma("pool", ssem, dict(out=rows[:, :, e_ * 192:e_ * 192 + 64], in_=srcv[:, :, e_, :]), reads=[ST])
                            else:
                                for j in range(4):
                                    rows = vpc_d[ts(tb4 * 4 + j, 128), :].rearrange("p (h c) -> p h c", c=128)
                                    P.dma("pool", ssem, dict(out=rows[:, :, 0:64], in_=st[:, j, :].rearrange("p (h d) -> p h d", d=64)), reads=[ST])
                        continue
                    for j in range(4):
                        oc = sl * 4 + j
                        for half in range(2):
                            st, ST, ssem = stg.next()
                            if oc < 36:
                                dd = qk_d[ts(oc, 128), ts(half, 2048)]
                            else:
                                dd = gt_d[ts(oc - 36, 128), ts(half, 2048)]
                            for tq in range(4):
                                t = half * 4 + tq
                                bk, BK = mb.next()
                                for kc in range(8):
                                    P.op("pe", "matmul", dict(out=bk[:], lhsT=wb[:, kc, ts(j, 128)], rhs=hT[:, kc, ts(t, 512)], start=(kc == 0), stop=(kc == 7)),
                                         reads=[HT[t], WB], writes=[BK])
                                if pending:
                                    pending.pop(0)()
                                dst = st[:, tq, :]
                                evi += 1

                                def store(st=st, ST=ST, ssem=ssem, dd=dd, sl=sl, j=j, half=half):
                                    if sl in (1, 7):
                                        kp = kpa_d if sl == 1 else kpc_d
                                        for e_ in range(2):
                                            r0_ = (2 * j + e_) * 128 + e_ * 64
                                            P.dma("pool", ssem, dict(out=kp[r0_:r0_ + 64, ts(half, 2048)].rearrange("p (j n) -> p j n", j=4),
                                                                     in_=st[e_ * 64:(e_ + 1) * 64, :, :]), reads=[ST])
                                    else:
                                        P.dma("pool", ssem, dict(out=dd.rearrange("p (j n) -> p j n", j=4), in_=st[:]), reads=[ST])

                                if sl == 0:
                                    if evi % 2:
                                        P.op("act", "activation", dict(out=dst, in_=bk[:], func=AF.Copy, scale=0.125), reads=[BK], writes=[ST])
                                    else:
                                        P.op("dve", "tensor_scalar", dict(out=dst, in0=bk[:], scalar1=0.125, scalar2=None, op0=ALU.mult), reads=[BK], writes=[ST])
                                elif sl == 1:
                                    if evi % 2:
                                        P.op("act", "activation", dict(out=dst, in_=bk[:], func=AF.Copy), reads=[BK], writes=[ST])
                                    else:
                                        P.op("dve", "tensor_copy", dict(out=dst, in_=bk[:]), reads=[BK], writes=[ST])
                                elif sl in (3, 4, 6, 7):
                                    isq = sl in (3, 6)
                                    gbase = (P_GQD if sl in (3, 4) else P_GQC) + 2 * l + (0 if isq else 1)
                                    s_, S_, _ = sqr.next()
                                    P.op("act", "activation", dict(out=s_[:], in_=bk[:], func=AF.Square), reads=[BK], writes=[S_])

                                    def tail(s_=s_, S_=S_, bk=bk, BK=BK, dst=dst, ST=ST, isq=isq, gbase=gbase, last=(tq == 3), store=store):
                                        b2, B2 = sb2.next()
                                        P.op("pe", "matmul", dict(out=b2[:], lhsT=blk64, rhs=s_[:], start=True, stop=True), reads=[S_, CST], writes=[B2])
                                        r, R, _ = rsr.next()
                                        if isq:
                                            P.op("act", "activation", dict(out=r[:], in_=b2[:], func=AF.Ln, bias=64.0 * EPS, scale=1.0), reads=[B2], writes=[R])
                                        else:
                                            P.op("act", "activation", dict(out=r[:], in_=b2[:], func=AF.Ln, bias=EPS, scale=1.0 / 64), reads=[B2], writes=[R])
                                        P.op("act", "activation", dict(out=r[:], in_=r[:], func=AF.Exp, scale=-0.5), writes=[R])
                                        P.op("dve", "scalar_tensor_tensor", dict(out=dst, in0=bk[:], scalar=prm[:, gbase:gbase + 1], in1=r[:], op0=ALU.mult, op1=ALU.mult),
                                             reads=[BK, R, PRM], writes=[ST])
                                        if last:
                                            store()

                                    pending.append(tail)
                                    continue
                                else:
                                    gi = oc - 36
                                    bcol = P_BG + l * 24 + gi
                                    P.op("act", "activation", dict(out=dst, in_=bk[:], func=AF.Sigmoid, bias=prm[:, bcol:bcol + 1], scale=1.0), reads=[BK, PRM], writes=[ST])
                                if tq == 3:
                                    store()
                    while pending:
                        pending.pop(0)()
                P.barrier()

        def phase_attn_a(l):
            with ExitStack() as pes:
                vr = Ring(P, pes, "av", 2, [128, 16, 1024], BF16, dma=True)
                qr = Ring(P, pes, "aq", 2, [128, SEQ], BF16, dma=True)
                kr = Ring(P, pes, "ak", 4, [128, SEQ], BF16, dma=True)
                er = Ring(P, pes, "ae", 3, [128, 512], F32)
                spr = Ring(P, pes, "asp", 3, [128, 512], BF16)
                pr = Ring(P, pes, "ap", 3, [128, 512], BF16)
                cr = Ring(P, pes, "ac", 3, [128, 512], BF16)
                ost = Ring(P, pes, "ao", 2, [128, 4, 512], BF16, dma="pool")
                zb = BankRing(banks[0:4])
                cbk, CBK = banks[4]
                obr = BankRing(banks[5:7])
                items = []
                for s in range(2):
                    for j in range(4):
                        for qt in range(4):
                            for e in range(2):
                                kbs = list(range(4 * qt + 3, -1, -1))
                                for ii, kb in enumerate(kbs):
                                    items.append(dict(s=s, j=j, qt=qt, e=e, kb=kb, first=(ii == 0), last=(ii == len(kbs) - 1)))
                state = dict(v=None, q=None, ob=None, st=None)

                def stage1(it):
                    s, j, qt, e, kb = it["s"], it["j"], it["qt"], it["e"], it["kb"]
                    if it["first"] and e == 0 and qt == 0:
                        q, Q, qsem = qr.next()
                        P.dma("sp", qsem, dict(out=q[:], in_=qk_d[ts(j, 128), ts(s, SEQ)]), writes=[Q])
                        state["q"] = (q, Q)
                        for e_ in range(2):
                            k, K, ksem = kr.next()
                            P.dma("sp", ksem, dict(out=k[:], in_=kpa_d[ts(2 * j + e_, 128), ts(s, SEQ)]), writes=[K])
                            state[("k", e_)] = (k, K)
                        if j == 0:
                            v, V, vsem = vr.next()
                            P.dma("sp", vsem, dict(out=v[:], in_=vpa_d[ts(s, SEQ), :].rearrange("(b p) n -> p b n", p=128)), writes=[V])
                            state["v"] = (v, V)
                        state["st"] = ost.next()
                    it["v"] = state["v"]
                    it["stg"] = state["st"]
                    q, Q = state["q"]
                    k, K = state[("k", e)]
                    it["qk"] = (q, Q, k, K)
                    p = kb - 4 * qt
                    c0 = 128 * p if p > 0 else 0
                    it["c0"] = c0
                    n = 512 - c0
                    zbk, ZB = zb.next()
                    it["zb"] = (zbk, ZB)
                    t0 = qt * 512
                    P.op("pe", "matmul", dict(out=zbk[:, c0:512], lhsT=k[:, ts(kb, 128)], rhs=q[:, t0 + c0:t0 + 512], start=True, stop=(p < 0)),
                         reads=[Q, K], writes=[ZB])
                    if p >= 0:
                        P.op("pe", "matmul", dict(out=zbk[:, c0:c0 + 128], lhsT=ident, rhs=nma, start=False, stop=True), reads=[CST], writes=[ZB])
                    ee, EE, _ = er.next()
                    P.op("act", "activation", dict(out=ee[:, 0:n], in_=zbk[:, c0:512], func=AF.Exp), reads=[ZB], writes=[EE])
                    sp, SP, _ = spr.next()
                    P.op("act", "activation", dict(out=sp[:, 0:n], in_=ee[:, 0:n], func=AF.Ln, bias=1.0, scale=1.0), reads=[EE], writes=[SP])
                    it["sp"] = (sp, SP)

                def stage2(it):
                    e, qt = it["e"], it["qt"]
                    c0 = it["c0"]
                    n = 512 - c0
                    zbk, ZB = it["zb"]
                    sp, SP = it["sp"]
                    q, Q, k, K = it["qk"]
                    if it["first"]:
                        if e == 0:
                            state["ob"] = obr.next()
                            obk, OB = state["ob"]
                            P.op("pe", "matmul", dict(out=obk[:], lhsT=zer, rhs=q[:, 0:512], start=True, stop=False), reads=[CST, Q], writes=[OB])
                        P.op("pe", "matmul", dict(out=cbk[:], lhsT=zer, rhs=q[:, 0:512], start=True, stop=False), reads=[CST, Q], writes=[CBK])
                    it["ob"] = state["ob"]
                    P.op("pe", "matmul", dict(out=zbk[:, c0:512], lhsT=nlu, rhs=sp[:, 0:n], start=False, stop=it["first"], skip_group_check=True),
                         reads=[SP, CST], writes=[ZB])
                    if not it["first"]:
                        cy, CY = state["carry"]
                        P.op("pe", "matmul", dict(out=zbk[:, c0:512], lhsT=e0m, rhs=cy[:, c0:512], start=False, stop=True, skip_group_check=True),
                             reads=[CY, CST], writes=[ZB])
                    if not it["last"]:
                        P.op("pe", "matmul", dict(out=cbk[:, c0:512], lhsT=neg1, rhs=sp[:, 0:n], start=False, stop=False, skip_group_check=True),
                             reads=[SP, CST], writes=[CBK])
                        cy, CY, _ = cr.next()
                        P.op("dve", "tensor_copy", dict(out=cy[:], in_=cbk[:]), reads=[CBK], writes=[CY])
                        state["carry"] = (cy, CY)
                    pp, PP, _ = pr.next()
                    P.op("act", "activation", dict(out=pp[:, 0:n], in_=zbk[:, c0:512], func=AF.Exp), reads=[ZB], writes=[PP])
                    it["pp"] = (pp, PP)

                def stage3(it):
                    s, j, qt, e, kb = it["s"], it["j"], it["qt"], it["e"], it["kb"]
                    c0 = it["c0"]
                    n = 512 - c0
                    pp, PP = it["pp"]
                    v, V = it["v"]
                    obk, OB = it["ob"]
                    P.op("pe", "matmul", dict(out=obk[:, c0:512], lhsT=v[:, kb, (2 * j + e) * 128:(2 * j + e + 1) * 128], rhs=pp[:, 0:n], start=False, stop=(it["last"] and e == 1), skip_group_check=True),
                         reads=[V, PP], writes=[OB])
                    if it["last"] and e == 1:
                        st, ST, ssem = it["stg"]
                        P.op("dve", "tensor_copy", dict(out=st[:, qt, :], in_=obk[:]), reads=[OB], writes=[ST])
                        if qt == 3:
                            P.dma("pool", ssem, dict(out=o_d[ts(j, 128), ts(s, SEQ)].rearrange("p (a n) -> p a n", a=4), in_=st[:]), reads=[ST])

                n_it = len(items)
                for i in range(n_it + 2):
                    if i < n_it:
                        stage1(items[i])
                    if 0 <= i - 1 < n_it:
                        stage2(items[i - 1])
                    if 0 <= i - 2 < n_it:
                        stage3(items[i - 2])
                P.barrier()

        def phase_attn_b(l):
            with ExitStack() as pes:
                vr = Ring(P, pes, "bv", 2, [128, 16, 512], BF16, dma=True)
                qr = Ring(P, pes, "bq", 4, [68, SEQ], BF16, dma=True)
                kr = Ring(P, pes, "bk", 4, [68, SEQ], BF16, dma=True)
                pr = Ring(P, pes, "bp", 4, [128, 512], BF16)
                rdr = Ring(P, pes, "brd", 2, [128, 512], F32)
                rr = Ring(P, pes, "br", 4, [128, 512], F32)
                o32 = Ring(P, pes, "bo", 2, [128, 512], F32)
                sqr = Ring(P, pes, "bsq", 2, [128, 512], BF16)
                rsr = Ring(P, pes, "brs", 2, [128, 512], F32)
                ost = Ring(P, pes, "bos", 2, [128, 4, 512], BF16, dma="pool")
                zb = BankRing(banks[0:4])
                ndr = BankRing([(banks[4], banks[5]), (banks[6], banks[7])])
                li = lam_init(l)
                items = []
                for s in range(2):
                    for h in range(4):
                        for qt in range(4):
                            for m in range(2):
                                nb = 4 * qt + 4
                                for kb in range(nb):
                                    items.append(dict(s=s, h=h, qt=qt, m=m, kb=kb, first=(kb == 0), last=(kb == nb - 1)))
                state = {}
                deferred = []

                def stage1(it):
                    s, h, qt, m, kb = it["s"], it["h"], it["qt"], it["m"], it["kb"]
                    if it["first"] and qt == 0 and m == 0:
                        for mm in range(2):
                            u = h * 2 + mm
                            q, Q, qsem = qr.next()
                            P.dma("sp", qsem, dict(out=q[0:64, :], in_=qk_d[1536 + u * 64:1536 + (u + 1) * 64, ts(s, SEQ)]), writes=[Q])
                            P.dma("sp", qsem, dict(out=q[64:68, :], in_=augq_d[h]), writes=[Q])
                            k, K, ksem = kr.next()
                            P.dma("sp", ksem, dict(out=k[0:64, :], in_=qk_d[2048 + u * 64:2048 + (u + 1) * 64, ts(s, SEQ)]), writes=[K])
                            P.dma("sp", ksem, dict(out=k[64:68, :], in_=augk_d[h]), writes=[K])
                            state[("q", mm)] = (q, Q)
                            state[("k", mm)] = (k, K)
                        if h == 0:
                            v, V, vsem = vr.next()
                            P.dma("sp", vsem, dict(out=v[:], in_=v_d[1][ts(s, SEQ), :].rearrange("(b p) n -> p b n", p=128)), writes=[V])
                            state["v"] = (v, V)
                        state["st"] = ost.next()
                    it["v"] = state["v"]
                    it["stg"] = state["st"]
                    q, Q = state[("q", m)]
                    k, K = state[("k", m)]
                    p = kb - 4 * qt
                    c0 = 128 * p if p > 0 else 0
                    it["c0"] = c0
                    zbk, ZB = zb.next()
                    t0 = qt * 512
                    P.op("pe", "matmul", dict(out=zbk[:, c0:512], lhsT=k[0:68, ts(kb, 128)], rhs=q[0:68, t0 + c0:t0 + 512], start=True, stop=(p < 0)),
                         reads=[Q, K], writes=[ZB])
                    if p >= 0:
                        P.op("pe", "matmul", dict(out=zbk[:, c0:c0 + 128], lhsT=ident, rhs=cst[:, C_CORR + h * 128:C_CORR + (h + 1) * 128], start=False, stop=True),
                             reads=[CST], writes=[ZB])
                    pp, PP, _ = pr.next()
                    P.op("act", "activation", dict(out=pp[:, 0:512 - c0], in_=zbk[:, c0:512], func=AF.Exp), reads=[ZB], writes=[PP])
                    it["pp"] = (pp, PP)

                def stage2(it):
                    s, h, qt, m, kb = it["s"], it["h"], it["qt"], it["m"], it["kb"]
                    c0 = it["c0"]
                    n = 512 - c0
                    pp, PP = it["pp"]
                    v, V = it["v"]
                    if it["first"]:
                        state["nd"] = ndr.next()
                    (nbk, NB), (dbk, DB) = state["nd"]
                    P.op("pe", "matmul", dict(out=nbk[:, c0:512], lhsT=v[:, kb, ts(h, 128)], rhs=pp[:, 0:n], start=it["first"], stop=it["last"], skip_group_check=True),
                         reads=[V, PP], writes=[NB])
                    P.op("pe", "matmul", dict(out=dbk[:, c0:512], lhsT=ones, rhs=pp[:, 0:n], start=it["first"], stop=it["last"], skip_group_check=True),
                         reads=[CST, PP], writes=[DB])
                    if it["last"]:
                        deferred.append([2, lambda it=it, nbk=nbk, NB=NB, dbk=dbk, DB=DB: combine(it, nbk, NB, dbk, DB)])

                def combine(it, nbk, NB, dbk, DB):
                    s, h, qt, m, kb = it["s"], it["h"], it["qt"], it["m"], it["kb"]
                    if True:
                        rd, RD, _ = rdr.next()
                        P.op("act", "activation", dict(out=rd[:], in_=dbk[:], func=AF.Ln), reads=[DB], writes=[RD])
                        P.op("act", "activation", dict(out=rd[:], in_=rd[:], func=AF.Exp, scale=-1.0), writes=[RD])
                        r, R, _ = rr.next()
                        P.op("dve", "tensor_tensor", dict(out=r[:], in0=nbk[:], in1=rd[:], op=ALU.mult), reads=[NB, RD], writes=[R])
                        state[("r", m)] = (r, R)
                        if m == 1:
                            r0, R0 = state[("r", 0)]
                            o, O, _ = o32.next()
                            P.op("dve", "scalar_tensor_tensor", dict(out=o[:], in0=r[:], scalar=lam_s[:, l:l + 1], in1=r0[:], op0=ALU.mult, op1=ALU.add),
                                 reads=[R, R0, LAM], writes=[O])
                            sq, SQ, _ = sqr.next()
                            P.op("act", "activation", dict(out=sq[:], in_=o[:], func=AF.Square), reads=[O], writes=[SQ])
                            deferred.append([4, lambda it=it, o=o, O=O, sq=sq, SQ=SQ: combine2(it, o, O, sq, SQ)])

                def combine2(it, o, O, sq, SQ):
                    s, h, qt, m, kb = it["s"], it["h"], it["qt"], it["m"], it["kb"]
                    if True:
                        if True:
                            ssb, SSB = zb.next()
                            P.op("pe", "matmul", dict(out=ssb[:], lhsT=ones, rhs=sq[:], start=True, stop=True), reads=[SQ, CST], writes=[SSB])
                            rs, RS, _ = rsr.next()
                            sc = 1.0 / (128.0 * (1 - li) ** 2)
                            P.op("act", "activation", dict(out=rs[:], in_=ssb[:], func=AF.Ln, bias=EPS / (1 - li) ** 2, scale=sc), reads=[SSB], writes=[RS])
                            P.op("act", "activation", dict(out=rs[:], in_=rs[:], func=AF.Exp, scale=-0.5), writes=[RS])
                            st, ST, ssem = it["stg"]
                            P.op("dve", "scalar_tensor_tensor", dict(out=st[:, qt, :], in0=o[:], scalar=prm[:, P_GSUB + l:P_GSUB + l + 1], in1=rs[:], op0=ALU.mult, op1=ALU.mult),
                                 reads=[O, RS, PRM], writes=[ST])
                            if qt == 3:
                                P.dma("pool", ssem, dict(out=o_d[512 + h * 128:512 + (h + 1) * 128, ts(s, SEQ)].rearrange("p (a n) -> p a n", a=4), in_=st[:]), reads=[ST])

                n_it = len(items)
                for i in range(n_it + 2):
                    if i < n_it:
                        stage1(items[i])
                    if 0 <= i - 2 < n_it:
                        stage2(items[i - 2])
                    for d_ in deferred:
                        d_[0] -= 1
                    while deferred and deferred[0][0] <= 0:
                        deferred.pop(0)[1]()
                while deferred:
                    deferred.pop(0)[1]()
                P.barrier()

        def phase_attn_c(l):
            with ExitStack() as pes:
                vr = Ring(P, pes, "cv", 2, [128, 16, 1024], BF16, dma=True)
                qr = Ring(P, pes, "cq", 2, [128, SEQ], BF16, dma=True)
                kr = Ring(P, pes, "ck", 4, [128, SEQ], BF16, dma=True)
                pr = Ring(P, pes, "cp", 4, [128, 512], BF16)
                rdr = Ring(P, pes, "crd", 2, [128, 512], F32)
                ost = Ring(P, pes, "cos", 2, [128, 4, 512], BF16, dma="pool")
                cb = sb("ccb", [128, 8, 640], BF16, pes)
                CB = Buf()
                mk = sb("cmk", [128, 640], F32, pes)
                MK = Buf()
                rbr = Ring(P, pes, "crb", 2, [128, 640], F32, dma=True)
                msem = P.new_sem(pes)
                P.dma("sp", msem, dict(out=mk[:], in_=maskc_d), writes=[MK])
                for h in range(8):
                    rb, RB, rsem = rbr.next()
                    P.dma("sp", rsem, dict(out=rb[:], in_=rbt_d[l][h]), writes=[RB])
                    P.op("dve", "tensor_tensor", dict(out=cb[:, h, :], in0=rb[:], in1=mk[:], op=ALU.add), reads=[RB, MK], writes=[CB])
                zb = BankRing(banks[0:4])
                ndr = BankRing(banks[4:8])
                items = []
                for s in range(2):
                    for j in range(4):
                        for qt in range(4):
                            for e in range(2):
                                ps_ = [p for p in range(-4, 4) if 4 * qt + p >= 0]
                                first_p = -1 if -1 in ps_ else 0
                                order = [first_p] + [p for p in ps_ if p != first_p]
                                for ii, p in enumerate(order):
                                    items.append(dict(s=s, j=j, qt=qt, e=e, p=p, first=(ii == 0), last=(ii == len(order) - 1)))
                state = {}
                deferred = []

                def stage1(it):
                    s, j, qt, e, p = it["s"], it["j"], it["qt"], it["e"], it["p"]
                    if it["first"] and qt == 0 and e == 0:
                        q, Q, qsem = qr.next()
                        P.dma("sp", qsem, dict(out=q[:], in_=qk_d[3072 + j * 128:3072 + (j + 1) * 128, ts(s, SEQ)]), writes=[Q])
                        state["q"] = (q, Q)
                        for e_ in range(2):
                            k, K, ksem = kr.next()
                            P.dma("sp", ksem, dict(out=k[:], in_=kpc_d[ts(2 * j + e_, 128), ts(s, SEQ)]), writes=[K])
                            state[("k", e_)] = (k, K)
                        if j == 0:
                            v, V, vsem = vr.next()
                            P.dma("sp", vsem, dict(out=v[:], in_=vpc_d[ts(s, SEQ), :].rearrange("(b p) n -> p b n", p=128)), writes=[V])
                            state["v"] = (v, V)
                        state["st"] = ost.next()
                    it["v"] = state["v"]
                    it["stg"] = state["st"]
                    q, Q = state["q"]
                    k, K = state[("k", e)]
                    h = 2 * j + e
                    kb = 4 * qt + p
                    qa = max(0, p)
                    qb_ = min(3, p + 4)
                    c0, c1 = 128 * qa, 128 * (qb_ + 1)
                    r0 = qa - p
                    it["c"] = (c0, c1)
                    it["kb"] = kb
                    zbk, ZB = zb.next()
                    t0 = qt * 512
                    P.op("pe", "matmul", dict(out=zbk[:, c0:c1], lhsT=k[:, ts(kb, 128)], rhs=q[:, t0 + c0:t0 + c1], start=True, stop=False),
                         reads=[Q, K], writes=[ZB])
                    P.op("pe", "matmul", dict(out=zbk[:, c0:c1], lhsT=ident, rhs=cb[:, h, r0 * 128:r0 * 128 + (c1 - c0)], start=False, stop=True),
                         reads=[CST, CB], writes=[ZB])
                    pp, PP, _ = pr.next()
                    P.op("act", "activation", dict(out=pp[:, 0:c1 - c0], in_=zbk[:, c0:c1], func=AF.Exp), reads=[ZB], writes=[PP])
                    it["pp"] = (pp, PP)

                def stage2(it):
                    s, j, qt, e, p = it["s"], it["j"], it["qt"], it["e"], it["p"]
                    c0, c1 = it["c"]
                    kb = it["kb"]
                    pp, PP = it["pp"]
                    v, V = it["v"]
                    if it["first"]:
                        state["nd"] = ndr.next()
                    nbk, NB = state["nd"]
                    P.op("pe", "matmul", dict(out=nbk[:, c0:c1], lhsT=v[:, kb, (2 * j + e) * 128:(2 * j + e + 1) * 128], rhs=pp[:, 0:c1 - c0],
                                              start=it["first"], stop=it["last"], skip_group_check=True), reads=[V, PP], writes=[NB])
                    if it["last"]:
                        deferred.append([2, lambda it=it, nbk=nbk, NB=NB: combine(it, nbk, NB)])

                def combine(it, nbk, NB):
                    s, j, qt, e = it["s"], it["j"], it["qt"], it["e"]
                    rd, RD, _ = rdr.next()
                    P.op("act", "activation", dict(out=rd[0:64, :], in_=nbk[64:128, :], func=AF.Ln), reads=[NB], writes=[RD])
                    P.op("act", "activation", dict(out=rd[0:64, :], in_=rd[0:64, :], func=AF.Exp, scale=-1.0), writes=[RD])
                    st, ST, ssem = it["stg"]
                    P.op("dve", "tensor_tensor", dict(out=st[e * 64:(e + 1) * 64, qt, :], in0=nbk[0:64, :], in1=rd[0:64, :], op=ALU.mult), reads=[NB, RD], writes=[ST])
                    if qt == 3 and e == 1:
                        P.dma("pool", ssem, dict(out=o_d[1024 + j * 128:1024 + (j + 1) * 128, ts(s, SEQ)].rearrange("p (a n) -> p a n", a=4), in_=st[:]), reads=[ST])

                n_it = len(items)
                wbv = wbr_d[l].rearrange("i (kc p) n -> p (i kc) n", p=128)
                wov = w_out_d[l].rearrange("(kc p) n -> p kc n", p=128)
                pf = [(wbv, c, c) for c in range(12)] + [(wov, c, 12 + c) for c in range(8)]
                for i in range(n_it + 2):
                    if i % 20 == 10 and pf:
                        src_, cs_, cd_ = pf.pop(0)
                        prefetch(src_, [(cs_, cd_)], engs=("dve",))
                    if i < n_it:
                        stage1(items[i])
                    if 0 <= i - 2 < n_it:
                        stage2(items[i - 2])
                    for d_ in deferred:
                        d_[0] -= 1
                    while deferred and deferred[0][0] <= 0:
                        deferred.pop(0)[1]()
                while deferred:
                    deferred.pop(0)[1]()
                while pf:
                    src_, cs_, cd_ = pf.pop(0)
                    prefetch(src_, [(cs_, cd_)], engs=("dve",))
                P.barrier()

        def phase_merge(l, src_d, dst_d, gcol, hout_d):
            with ExitStack() as pes:
                orr = Ring(P, pes, "mo", 1, [128, 12, 512], BF16, dma=True)
                epr = epilogue_rings(pes, "me")
                gr = Ring(P, pes, "mg", 1, [128, 24, 512], BF16, dma=True)
                xr = Ring(P, pes, "mx", 2, [128, 8, 512], F32, dma=True)
                xst = [P.new_sem(pes, "pool") for _ in range(2)]
                mr = Ring(P, pes, "mm", 2, [128, 512], F32)
                tr = Ring(P, pes, "mt", 2, [128, 512], F32)
                mbr = Ring(P, pes, "mb", 1, [128, 8, 512], BF16)
                br = BankRing(banks)
                xv = src_d.rearrange("(c p) t -> p c t", p=128)
                dv = dst_d.rearrange("(c p) t -> p c t", p=128)
                ov = o_d.rearrange("(c p) t -> p c t", p=128)
                gv = gt_d.rearrange("(c p) t -> p c t", p=128)
                pend = []
                for t in range(8):
                    o, O, osem = orr.next()
                    P.dma("sp", osem, dict(out=o[:], in_=ov[:, :, ts(t, 512)]), writes=[O])
                    g, G, gsem = gr.next()
                    P.dma("sp", gsem, dict(out=g[:], in_=gv[:, :, ts(t, 512)]), writes=[G])
                    x, X, xsem = xr.next()
                    P.dma("sp", xsem, dict(out=x[:], in_=xv[:, :, ts(t, 512)]), writes=[X])
                    mbf, MBF, _ = mbr.next()
                    sq_, SQ_, _ = epr[0].next()
                    for oc in range(8):
                        m, M, _ = mr.next()
                        for i in range(3):
                            bk, BK = br.next()
                            for kc in range(4):
                                P.op("pe", "matmul", dict(out=bk[:], lhsT=arena[:, i * 4 + kc, ts(oc, 128)], rhs=o[:, i * 4 + kc, :], start=(kc == 0), stop=(kc == 3)),
                                     reads=[ARENA[i * 4 + kc], O], writes=[BK])
                            if i == 0:
                                P.op("dve", "tensor_tensor", dict(out=m[:], in0=bk[:], in1=g[:, oc, :], op=ALU.mult), reads=[BK, G], writes=[M])
                            else:
                                tt, TT, _ = tr.next()
                                P.op("dve", "tensor_tensor", dict(out=tt[:], in0=bk[:], in1=g[:, i * 8 + oc, :], op=ALU.mult), reads=[BK, G], writes=[TT])
                                if i == 1:
                                    P.op("pool", "tensor_tensor", dict(out=m[:], in0=m[:], in1=tt[:], op=ALU.add), reads=[TT], writes=[M])
                                else:
                                    P.op("pool", "tensor_tensor", dict(out=mbf[:, oc, :], in0=m[:], in1=tt[:], op=ALU.add), reads=[TT, M], writes=[MBF])
                        if oc == 1 and pend:
                            pend.pop(0)()
                    for oc in range(8):
                        bk, BK = br.next()
                        for kc in range(8):
                            P.op("pe", "matmul", dict(out=bk[:], lhsT=arena[:, 12 + kc, ts(oc, 128)], rhs=mbf[:, kc, :], start=(kc == 0), stop=(kc == 7)),
                                 reads=[ARENA[12 + kc], MBF], writes=[BK])
                        P.op("dve", "tensor_tensor", dict(out=x[:, oc, :], in0=bk[:], in1=x[:, oc, :], op=ALU.add), reads=[BK], writes=[X])
                        P.op("act", "activation", dict(out=sq_[:, oc, :], in_=x[:, oc, :], func=AF.Square), reads=[X], writes=[SQ_])
                    P.dma("pool", xst[t % 2], dict(out=dv[:, :, ts(t, 512)], in_=x[:]), reads=[X])
                    pend.append(lambda t=t, x=x, X=X, sq_=sq_, SQ_=SQ_: norm_epi_b(epr, t, x, X, sq_, SQ_, gcol, hout_d, br))
                while pend:
                    pend.pop(0)()
                P.barrier()

        def phase_ffn_up(l, hT, HT, pre=None):
            with ExitStack() as pes:
                wgf = Ring(P, pes, "fgf", 2, [128, 8, 256], F32, dma=True)
                wuf = Ring(P, pes, "fuf", 2, [128, 8, 256], F32, dma=True)
                wgb = Ring(P, pes, "fgb", 2, [128, 8, 256], BF16)
                wub = Ring(P, pes, "fub", 2, [128, 8, 256], BF16)
                sgr = Ring(P, pes, "fsg", 3, [128, 512], F32)
                stg = Ring(P, pes, "fst", 3, [128, 4, 512], BF16, dma="pool")
                br = BankRing(banks)
                wv = w_gu_d[l].rearrange("(kc p) n -> p kc n", p=128)
                for sl in range(11):
                    wg, WG, gsem = wgf.next()
                    P.dma("sp", gsem, dict(out=wg[:], in_=wv[:, :, ts(sl, 256)]), writes=[WG])
                    wu, WU, usem = wuf.next()
                    P.dma("sp", usem, dict(out=wu[:], in_=wv[:, :, DFF + sl * 256:DFF + (sl + 1) * 256]), writes=[WU])
                    if sl == 0 and pre is not None:
                        pre()
                    gb, GB, _ = wgb.next()
                    P.op("dve", "tensor_copy", dict(out=gb[:], in_=wg[:]), reads=[WG], writes=[GB])
                    ub, UB, _ = wub.next()
                    P.op("dve", "tensor_copy", dict(out=ub[:], in_=wu[:]), reads=[WU], writes=[UB])
                    for j in range(2):
                        fc = sl * 2 + j
                        for half in range(2):
                            st, ST, ssem = stg.next()
                            for tq in range(4):
                                t = half * 4 + tq
                                gk, GK = br.next()
                                for kc in range(8):
                                    P.op("pe", "matmul", dict(out=gk[:], lhsT=gb[:, kc, ts(j, 128)], rhs=hT[:, kc, ts(t, 512)], start=(kc == 0), stop=(kc == 7)),
                                         reads=[HT[t], GB], writes=[GK])
                                uk, UK = br.next()
                                for kc in range(8):
                                    P.op("pe", "matmul", dict(out=uk[:], lhsT=ub[:, kc, ts(j, 128)], rhs=hT[:, kc, ts(t, 512)], start=(kc == 0), stop=(kc == 7)),
                                         reads=[HT[t], UB], writes=[UK])
                                sg, SG, _ = sgr.next()
                                P.op("act", "activation", dict(out=sg[:], in_=gk[:], func=AF.Silu), reads=[GK], writes=[SG])
                                P.op("dve", "tensor_tensor", dict(out=st[:, tq, :], in0=uk[:], in1=sg[:], op=ALU.mult), reads=[UK, SG], writes=[ST])
                            P.dma("pool", ssem, dict(out=ac_d[ts(fc, 128), ts(half, 2048)].rearrange("p (a n) -> p a n", a=4), in_=st[:]), reads=[ST])
                    prefetch(w_dn_d[l].rearrange("(kc p) n -> p kc n", p=128), [(2 * sl, 2 * sl), (2 * sl + 1, 2 * sl + 1)])
                P.barrier()

        def phase_ffn_down(l, src_d, dst_d, gcol, hout_d):
            with ExitStack() as pes:
                ar = Ring(P, pes, "da", 2, [128, 22, 512], BF16, dma=True)
                epr = epilogue_rings(pes, "de") if hout_d is not None else None
                xr = Ring(P, pes, "dx", 2, [128, 8, 512], F32, dma=True)
                xst = [P.new_sem(pes, "pool") for _ in range(2)]
                br = BankRing(banks)
                xv = src_d.rearrange("(c p) t -> p c t", p=128)
                dv = dst_d.rearrange("(c p) t -> p c t", p=128)
                av = ac_d.rearrange("(c p) t -> p c t", p=128)
                pend = []
                for t in range(8):
                    a, A, asem = ar.next()
                    P.dma("sp", asem, dict(out=a[:], in_=av[:, :, ts(t, 512)]), writes=[A])
                    x, X, xsem = xr.next()
                    P.dma("sp", xsem, dict(out=x[:], in_=xv[:, :, ts(t, 512)]), writes=[X])
                    if hout_d is not None:
                        sq_, SQ_, _ = epr[0].next()
                    for oc in range(8):
                        bk, BK = br.next()
                        for kc in range(22):
                            P.op("pe", "matmul", dict(out=bk[:], lhsT=arena[:, kc, ts(oc, 128)], rhs=a[:, kc, :], start=(kc == 0), stop=(kc == 21)),
                                 reads=[ARENA[kc], A], writes=[BK])
                        P.op("dve", "tensor_tensor", dict(out=x[:, oc, :], in0=bk[:], in1=x[:, oc, :], op=ALU.add), reads=[BK], writes=[X])
                        if hout_d is not None:
                            P.op("act", "activation", dict(out=sq_[:, oc, :], in_=x[:, oc, :], func=AF.Square), reads=[X], writes=[SQ_])
                        if oc == 1 and pend:
                            pend.pop(0)()
                    P.dma("pool", xst[t % 2], dict(out=dv[:, :, ts(t, 512)], in_=x[:]), reads=[X])
                    if hout_d is not None:
                        pend.append(lambda t=t, x=x, X=X, sq_=sq_, SQ_=SQ_: norm_epi_b(epr, t, x, X, sq_, SQ_, gcol, hout_d, br))
                while pend:
                    pend.pop(0)()
                P.barrier()

        phases = 0

        def done():
            nonlocal phases
            phases += 1
            return stop_after is not None and phases >= stop_after

        stop = False
        for l in range(n_layers):
            src = xT_d if l == 0 else xs_d
            with ExitStack() as les:
                hT = sb("hT", [128, 8, T], BF16, les)
                HT = [Buf("hT%d" % i) for i in range(8)]
                if l == 0:
                    zt = sb("zt", [128, 4, 1024], BF16, les)
                    ot = sb("ot", [128, 4, 1024], BF16, les)
                    ZT = Buf()
                    OT = Buf()
                    P.op("dve", "memset", dict(ap=zt[:], constant=0.0), writes=[ZT])
                    P.op("dve", "memset", dict(ap=ot[:], constant=1.0), writes=[OT])
                    fsem = P.new_sem()
                    for a_ in range(8):
                        P.dma("pool", fsem, dict(out=kpa_d[ts(a_, 128), :].rearrange("p (a n) -> p a n", a=4), in_=zt[:]), reads=[ZT])
                        P.dma("pool", fsem, dict(out=kpc_d[ts(a_, 128), :].rearrange("p (a n) -> p a n", a=4), in_=zt[:]), reads=[ZT])
                        P.dma("pool", fsem, dict(out=vpa_d[ts(a_, 512), :].rearrange("(a p) n -> p a n", p=128), in_=zt[:]), reads=[ZT])
                        P.dma("pool", fsem, dict(out=vpc_d[ts(a_, 512), :].rearrange("(a p) n -> p a n", p=128), in_=ot[:]), reads=[OT])
                    phase_norm(src, P_GMIX + l * 8, hT, HT)
                    phase_proj(l, hT, HT)
                else:
                    phase_proj(l, hT, HT, pre=lambda: load_h(h1_d, hT, HT, les))
            if done():
                break
            phase_attn_a(l)
            if done():
                break
            phase_attn_b(l)
            if done():
                break
            phase_attn_c(l)
            if done():
                break
            phase_merge(l, src, xs_d, P_GFFN + l * 8, h2_d)
            if done():
                break
            with ExitStack() as les:
                hT = sb("hT2", [128, 8, T], BF16, les)
                HT = [Buf("hT2%d" % i) for i in range(8)]
                phase_ffn_up(l, hT, HT, pre=lambda: load_h(h2_d, hT, HT, les))
            if done():
                break
            last = (l == n_layers - 1)
            phase_ffn_down(l, xs_d, yT_d if last else xs_d, P_GMIX + (l + 1) * 8 if not last else 0, None if last else h1_d)
            if done():
                break
        P.barrier()
        build.last_ninst = dict(P.ninst)
    return nc


def _bf(a):
    return np.asarray(a, dtype=np.float32).astype(ml_dtypes.bfloat16)


def make_consts():
    i = np.arange(128)[:, None]
    j = np.arange(128)[None, :]
    cst = np.zeros((128, NCST), np.float32)
    cst[:, C_ID:C_ID + 128] = (i == j)
    cst[:, C_ONE:C_ONE + 128] = 1.0
    cst[:, C_BLK:C_BLK + 128] = ((i // 64) == (j // 64))
    cst[:, C_NLU:C_NLU + 128] = -1.0 * (i >= j)
    cst[:, C_NMA:C_NMA + 128] = np.where(i < j, 0.0, NEG)
    cst[:, C_NEG1:C_NEG1 + 128] = -1.0
    for h in range(4):
        sl = SLOPES[h]
        ok = (i // 64) <= (j // 64)
        corr = np.where(ok, np.where(i > j, -2.0 * sl * (i - j), 0.0), NEG)
        cst[:, C_CORR + h * 128:C_CORR + (h + 1) * 128] = corr
    cst[:, C_E0:C_E0 + 128] = (i == 0)
    cst[:, C_HM0:C_HM0 + 64] = 1.0
    cst[:, C_HM1 + 64:C_HM1 + 128] = 1.0
    pos = np.arange(SEQ)
    augq = np.zeros((4, 4, SEQ), np.float32)
    augk = np.zeros((4, 4, SEQ), np.float32)
    for h in range(4):
        sl = SLOPES[h]
        augq[h, 0] = -sl * 64.0 * (pos // 64)
        augq[h, 1] = -sl * (pos % 64)
        augq[h, 2] = 1.0
        augq[h, 3] = 1.0
        augk[h, 0] = 1.0
        augk[h, 1] = 1.0
        augk[h, 2] = sl * 64.0 * (pos // 64)
        augk[h, 3] = sl * (pos % 64)
    maskc = np.zeros((128, 640), np.float32)
    a_ = (np.arange(128)[None, :] // 64)
    b_ = (np.arange(128)[:, None] // 64)
    maskc[:, 0:128] = np.where(a_ >= b_, 0.0, NEG)
    maskc[:, 512:640] = np.where(a_ <= b_, 0.0, NEG)
    return _bf(cst), _bf(augq), _bf(augk), maskc


def make_rbt(rel_bias):
    i = np.arange(128)[:, None]
    c = np.arange(640)[None, :]
    idx = np.clip(c - i, -128, 128) + 128
    return np.ascontiguousarray(rel_bias[:, :, idx]).astype(np.float32)


def make_prm(norm_mix_g, norm_ffn_g, b_gate, qk_g_diff, qk_g_ch, subln_g, lambda_qk):
    prm = np.zeros((128, NPRM), np.float32)
    p = np.arange(128)
    for l in range(DEPTH):
        prm[:, P_GMIX + l * 8:P_GMIX + (l + 1) * 8] = norm_mix_g[l].reshape(8, 128).T
        prm[:, P_GFFN + l * 8:P_GFFN + (l + 1) * 8] = norm_ffn_g[l].reshape(8, 128).T
        prm[:, P_BG + l * 24:P_BG + (l + 1) * 24] = b_gate[l].reshape(24, 128).T
        for k in range(2):
            prm[:, P_GQD + 2 * l + k] = qk_g_diff[l, k][p % 64]
            prm[:, P_GQC + 2 * l + k] = qk_g_ch[l, k][p % 64]
        prm[:, P_GSUB + l] = subln_g[l]
        prm[:, P_LQK + l * 256:P_LQK + (l + 1) * 256] = lambda_qk[l].reshape(1, 256)
    return prm


_NC_CACHE = {}


def kernel(x, norm_mix_g, w_in, b_gate, qk_g_diff, lambda_qk, subln_g, qk_g_ch, rel_bias,
           w_branch_sb, w_branch_diff, w_branch_ch, w_out, norm_ffn_g, w_gu, w_down):
    f = lambda a: np.ascontiguousarray(np.asarray(a, dtype=np.float32))
    x = f(x)
    cst, augq, augk, maskc = make_consts()
    shared = {
        "w_in": f(w_in),
        "w_br": np.ascontiguousarray(np.stack([f(w_branch_sb), f(w_branch_diff), f(w_branch_ch)], axis=1)),
        "w_out": f(w_out), "w_gu": f(w_gu), "w_down": f(w_down),
        "rbt": make_rbt(f(rel_bias)),
        "prm": make_prm(f(norm_mix_g), f(norm_ffn_g), f(b_gate), f(qk_g_diff), f(qk_g_ch), f(subln_g), f(lambda_qk)),
        "cst": cst, "augq": augq, "augk": augk, "maskc": maskc,
    }
    in_maps = []
    for c in range(NCORES):
        m = dict(shared)
        m["xT"] = np.ascontiguousarray(x[2 * c:2 * c + 2].reshape(T, D).T)
        in_maps.append(m)
    if "nc" not in _NC_CACHE:
        _NC_CACHE["nc"] = build()
    res = run_bass_kernel_spmd(_NC_CACHE["nc"], in_maps, core_ids=list(range(NCORES)))
    out = np.empty((16, SEQ, D), np.float32)
    for c in range(NCORES):
        out[2 * c:2 * c + 2] = np.asarray(res.results[c]["yT"]).T.reshape(2, SEQ, D)
    return out
```

```python
import math
import numpy as np
import ml_dtypes
from contextlib import ExitStack
import concourse.bass as bass
import concourse.mybir as mybir
from concourse.bass_utils import run_bass_kernel_spmd

F32 = mybir.dt.float32
BF16 = mybir.dt.bfloat16
AF = mybir.ActivationFunctionType
ALU = mybir.AluOpType

NCORES = 8
D = 1024
SEQ = 2048
T = 4096
DEPTH = 4
DFF = 2816
NIN = 7680
EPS = 1e-6
NEG = -30000.0
SLOPES = [2.0 ** (-8.0 * (i + 1) / 4) for i in range(4)]

C_ID, C_ONE, C_BLK, C_NLU, C_NMA, C_NEG1, C_ZERO, C_CORR = 0, 128, 256, 384, 512, 640, 768, 896
C_E0, C_HM0, C_HM1 = 1408, 1536, 1664
NCST = 1792
P_GMIX, P_GFFN, P_BG, P_GQD, P_GQC, P_GSUB, P_LQK = 0, 32, 64, 160, 168, 176, 180
NPRM = 180 + 1024


def ts(i, n):
    return slice(i * n, (i + 1) * n)


class Buf:
    __slots__ = ("name", "w", "r")

    def __init__(self, name=""):
        self.name = name
        self.w = None
        self.r = {}


class Prog:
    ENG = ("pe", "act", "dve", "pool", "sp")

    def __init__(self, nc, es, block):
        self.nc = nc
        self.es = es
        self.block = block
        self.streams = {e: [] for e in self.ENG}
        self.sems = {}
        self.cnt = {}
        self.known = {e: {} for e in self.ENG}
        for e in self.ENG:
            self.sems[e] = es.enter_context(nc.semaphore("s_" + e))
            self.cnt[e] = 0
        self.nsem = 0
        self.pools = {}
        self.uid = 0
        self.ninst = {e: 0 for e in self.ENG}

    def new_sem(self, pes=None, kind="sp"):
        if pes is not None:
            pool = self.pools.setdefault(kind, [])
            if pool:
                name = pool.pop()
            else:
                name = self.new_sem()
            pes.callback(pool.append, name)
            return name
        self.nsem += 1
        name = "d%d" % self.nsem
        self.sems[name] = self.es.enter_context(self.nc.semaphore(name))
        self.cnt[name] = 0
        return name

    def _deps(self, eng, reads, writes):
        waits = {}
        for b in reads:
            if b.w is not None and waits.get(b.w[0], 0) < b.w[1]:
                waits[b.w[0]] = b.w[1]
        for b in writes:
            if b.w is not None and waits.get(b.w[0], 0) < b.w[1]:
                waits[b.w[0]] = b.w[1]
            for k, v in b.r.items():
                if waits.get(k, 0) < v:
                    waits[k] = v
        self._emit_waits(eng, waits)

    def _emit_waits(self, eng, waits, skip_pe_self=True):
        st = self.streams[eng]
        kn = self.known[eng]
        for k, v in waits.items():
            if skip_pe_self and k == "pe" and eng == "pe":
                continue
            if kn.get(k, 0) >= v:
                continue
            kn[k] = v
            st.append(("wait", self.sems[k], v))
            self.ninst[eng] += 1

    def op(self, eng, name, kw, reads=(), writes=()):
        self._deps(eng, reads, writes)
        self.cnt[eng] += 1
        c = self.cnt[eng]
        self.streams[eng].append(("op", name, kw, self.sems[eng], 1))
        self.ninst[eng] += 1
        for b in reads:
            if b.r.get(eng, 0) < c:
                b.r[eng] = c
        for b in writes:
            b.w = (eng, c)
            b.r = {}

    def dma(self, q, semname, kw, reads=(), writes=()):
        self._deps(q, reads, writes)
        self.cnt[semname] += 16
        c = self.cnt[semname]
        self.streams[q].append(("op", "dma_start", kw, self.sems[semname], 16))
        self.ninst[q] += 1
        for b in reads:
            if b.r.get(semname, 0) < c:
                b.r[semname] = c
        for b in writes:
            b.w = (semname, c)
            b.r = {}

    def barrier(self):
        waits = {k: v for k, v in self.cnt.items() if v > 0}
        for e in self.ENG:
            self._emit_waits(e, dict(waits), skip_pe_self=False)
        self.flush()

    def flush(self):
        for en, meth in (("pe", "tensor"), ("act", "scalar"), ("dve", "vector"), ("pool", "gpsimd"), ("sp", "sync")):
            items = self.streams[en]
            if not items:
                continue
            self.streams[en] = []

            def body(e, items=items):
                for it in items:
                    if it[0] == "wait":
                        e.wait_ge(it[1], it[2])
                    else:
                        getattr(e, it[1])(**it[2]).then_inc(it[3], it[4])

            getattr(self.block, meth)(body)


class Ring:
    def __init__(self, P, es, name, n, shape, dtype, dma=False):
        self.slots = []
        for i in range(n):
            P.uid += 1
            t = es.enter_context(P.nc.sbuf_tensor("%s%d_%d" % (name, i, P.uid), shape, dtype))
            self.slots.append((t, Buf(name + str(i)), P.new_sem(es, dma if isinstance(dma, str) else "sp") if dma else None))
        self.i = 0
        self.n = n

    def next(self):
        s = self.slots[self.i % self.n]
        self.i += 1
        return s


class BankRing:
    def __init__(self, banks):
        self.b = banks
        self.i = 0

    def next(self):
        s = self.b[self.i % len(self.b)]
        self.i += 1
        return s


def lam_init(l):
    return 0.8 - 0.6 * math.exp(-0.3 * l)


def build(n_layers=DEPTH, dbg=False, stop_after=None):
    nc = bass.Bass("TRN2", target_bir_lowering=False)
    dt = lambda name, shape, dtype, kind: nc.dram_tensor(name, shape, dtype, kind=kind).ap()
    skind = "ExternalOutput" if dbg else "Internal"
    xT_d = dt("xT", [D, T], F32, "ExternalInput")
    w_in_d = dt("w_in", [DEPTH, D, NIN], F32, "ExternalInput")
    wbr_d = dt("w_br", [DEPTH, 3, 512, D], F32, "ExternalInput")
    w_out_d = dt("w_out", [DEPTH, D, D], F32, "ExternalInput")
    w_gu_d = dt("w_gu", [DEPTH, D, 2 * DFF], F32, "ExternalInput")
    w_dn_d = dt("w_down", [DEPTH, DFF, D], F32, "ExternalInput")
    rbt_d = dt("rbt", [DEPTH, 8, 128, 640], F32, "ExternalInput")
    prm_d = dt("prm", [128, NPRM], F32, "ExternalInput")
    cst_d = dt("cst", [128, NCST], BF16, "ExternalInput")
    augq_d = dt("augq", [4, 4, SEQ], BF16, "ExternalInput")
    augk_d = dt("augk", [4, 4, SEQ], BF16, "ExternalInput")
    maskc_d = dt("maskc", [128, 640], F32, "ExternalInput")
    yT_d = dt("yT", [D, T], F32, "ExternalOutput")
    xs_d = dt("xs", [D, T], F32, skind)
    qk_d = dt("qk", [4608, T], BF16, skind)
    v_d = dt("vs", [3, T, 512], BF16, skind)
    gt_d = dt("gts", [3072, T], BF16, skind)
    o_d = dt("os", [1536, T], BF16, skind)
    ac_d = dt("acs", [DFF, T], BF16, skind)
    kpa_d = dt("kpa", [1024, T], BF16, "Internal")
    kpc_d = dt("kpc", [1024, T], BF16, "Internal")
    vpa_d = dt("vpa", [T, 1024], BF16, "Internal")
    vpc_d = dt("vpc", [T, 1024], BF16, "Internal")
    h1_d = dt("h1s", [D, T], BF16, "Internal")
    h2_d = dt("h2s", [D, T], BF16, "Internal")

    with ExitStack() as es:
        block = es.enter_context(nc.Block())
        P = Prog(nc, es, block)
        def sb(n, s, d, e=es):
            P.uid += 1
            return e.enter_context(nc.sbuf_tensor("%s_%d" % (n, P.uid), s, d))
        banks = []
        for i in range(8):
            banks.append((es.enter_context(nc.psum_tensor("bank%d" % i, [128, 512], F32)), Buf("bank%d" % i)))
        cst = sb("cst_s", [128, NCST], BF16)
        prm = sb("prm_s", [128, NPRM], F32)
        lam_s = sb("lam_s", [128, 16], F32)
        arena = sb("arena", [128, 22, D], BF16)
        ARENA = [Buf("ar%d" % i) for i in range(22)]
        arst = Ring(P, es, "arst", 2, [128, 1, D], F32, dma=True)
        arcnt = [0]

        def prefetch(src3, chunks, engs=("dve", "act")):
            for (c_src, c_dst) in chunks:
                st, ST, sem = arst.next()
                P.dma("sp", sem, dict(out=st[:, 0, :], in_=src3[:, c_src, :]), writes=[ST])
                arcnt[0] += 1
                en = engs[arcnt[0] % len(engs)]
                if en == "dve":
                    P.op("dve", "tensor_copy", dict(out=arena[:, c_dst, :], in_=st[:, 0, :]), reads=[ST], writes=[ARENA[c_dst]])
                else:
                    P.op("act", "activation", dict(out=arena[:, c_dst, :], in_=st[:, 0, :], func=AF.Copy), reads=[ST], writes=[ARENA[c_dst]])
        CST = Buf("cst")
        PRM = Buf("prm")
        LAM = Buf("lam")
        s0 = P.new_sem()
        P.dma("sp", s0, dict(out=cst[:], in_=cst_d), writes=[CST])
        P.dma("sp", s0, dict(out=prm[:], in_=prm_d), writes=[PRM])
        ident = cst[:, C_ID:C_ID + 128]
        ones = cst[:, C_ONE:C_ONE + 128]
        blk64 = cst[:, C_BLK:C_BLK + 128]
        nlu = cst[:, C_NLU:C_NLU + 128]
        nma = cst[:, C_NMA:C_NMA + 128]
        neg1 = cst[:, C_NEG1:C_NEG1 + 128]
        zer = cst[:, C_ZERO:C_ZERO + 128]
        e0m = cst[:, C_E0:C_E0 + 128]
        hm = [cst[:, C_HM0:C_HM0 + 128], cst[:, C_HM1:C_HM1 + 128]]

        with ExitStack() as pes:
            tmp = sb("lq_tmp", [128, 64], F32, pes)
            sacc = sb("lq_acc", [128, 8], F32, pes)
            TMP = Buf()
            SACC = Buf()
            for l in range(DEPTH):
                for k in range(2):
                    a = prm[:, P_LQK + l * 256 + (2 * k) * 64: P_LQK + l * 256 + (2 * k + 1) * 64]
                    b = prm[:, P_LQK + l * 256 + (2 * k + 1) * 64: P_LQK + l * 256 + (2 * k + 2) * 64]
                    P.op("dve", "scalar_tensor_tensor", dict(out=tmp[:], in0=a, scalar=1.0, in1=b, op0=ALU.mult, op1=ALU.mult,
                                                             accum_out=sacc[:, 2 * l + k:2 * l + k + 1]), reads=[PRM], writes=[TMP, SACC])
            P.op("act", "activation", dict(out=sacc[:], in_=sacc[:], func=AF.Exp), writes=[SACC])
            for l in range(DEPTH):
                P.op("dve", "scalar_tensor_tensor", dict(out=lam_s[:, l:l + 1], in0=sacc[:, 2 * l + 1:2 * l + 2], scalar=-lam_init(l),
                                                         in1=sacc[:, 2 * l:2 * l + 1], op0=ALU.add, op1=ALU.subtract), reads=[SACC], writes=[LAM])
            P.barrier()

        def load_cast(pes, name, dst, DSTL, src3, nchunk, width, ring=None):
            per = max(1, 2048 // width)
            if ring is None:
                ring = Ring(P, pes, name, 2, [128, per, width], F32, dma=True)
            c = 0
            k = 0
            while c < nchunk:
                n = min(per, nchunk - c)
                st, ST, sem = ring.next()
                P.dma("sp", sem, dict(out=st[:, 0:n, :], in_=src3[:, c:c + n, :]), writes=[ST])
                for i in range(n):
                    k += 1
                    if k % 2:
                        P.op("dve", "tensor_copy", dict(out=dst[:, c + i, :], in_=st[:, i, :]), reads=[ST], writes=[DSTL[c + i]])
                    else:
                        P.op("act", "activation", dict(out=dst[:, c + i, :], in_=st[:, i, :], func=AF.Copy), reads=[ST], writes=[DSTL[c + i]])
                c += n

        def norm_epi_b(pes_rings, t, x, X, sq, SQ, gcol, hout_d, br):
            sqr, rsr, hr = pes_rings
            bk, BK = br.next()
            for c in range(8):
                P.op("pe", "matmul", dict(out=bk[:], lhsT=ones, rhs=sq[:, c, :], start=(c == 0), stop=(c == 7)), reads=[SQ, CST], writes=[BK])
            r, R, _ = rsr.next()
            P.op("act", "activation", dict(out=r[:], in_=bk[:], func=AF.Ln, bias=EPS, scale=1.0 / D), reads=[BK], writes=[R])
            P.op("act", "activation", dict(out=r[:], in_=r[:], func=AF.Exp, scale=-0.5), writes=[R])
            hh, HH, hsem = hr.next()
            for c in range(8):
                P.op("dve", "scalar_tensor_tensor", dict(out=hh[:, c, :], in0=x[:, c, :], scalar=prm[:, gcol + c:gcol + c + 1],
                                                         in1=r[:], op0=ALU.mult, op1=ALU.mult), reads=[X, R, PRM], writes=[HH])
            P.dma("pool", hsem, dict(out=hout_d.rearrange("(c p) t -> p c t", p=128)[:, :, ts(t, 512)], in_=hh[:]), reads=[HH])

        def epilogue_rings(pes, nm):
            return (Ring(P, pes, nm + "sq", 2, [128, 8, 512], BF16), Ring(P, pes, nm + "rs", 2, [128, 512], F32),
                    Ring(P, pes, nm + "hh", 1, [128, 8, 512], BF16, dma="pool"))

        def load_h(h_d, hT, HT, les):
            hv = h_d.rearrange("(c p) t -> p c t", p=128)
            for t in range(8):
                sem = P.new_sem(les)
                P.dma("sp", sem, dict(out=hT[:, :, ts(t, 512)], in_=hv[:, :, ts(t, 512)]), writes=[HT[t]])

        def phase_norm(src_d, gcol, hT, HT):
            with ExitStack() as pes:
                xr = Ring(P, pes, "nx", 2, [128, 8, 512], F32, dma=True)
                sq = Ring(P, pes, "nsq", 2, [128, 8, 512], BF16)
                rs = Ring(P, pes, "nrs", 2, [128, 512], F32)
                br = BankRing(banks[0:2])
                xv = src_d.rearrange("(c p) t -> p c t", p=128)
                for t in range(8):
                    x, X, xsem = xr.next()
                    P.dma("sp", xsem, dict(out=x[:], in_=xv[:, :, ts(t, 512)]), writes=[X])
                    s, S, _ = sq.next()
                    P.op("act", "activation", dict(out=s[:], in_=x[:], func=AF.Square), reads=[X], writes=[S])
                    bk, BK = br.next()
                    for c in range(8):
                        P.op("pe", "matmul", dict(out=bk[:], lhsT=ones, rhs=s[:, c, :], start=(c == 0), stop=(c == 7)), reads=[S, CST], writes=[BK])
                    r, R, _ = rs.next()
                    P.op("act", "activation", dict(out=r[:], in_=bk[:], func=AF.Ln, bias=EPS, scale=1.0 / D), reads=[BK], writes=[R])
                    P.op("act", "activation", dict(out=r[:], in_=r[:], func=AF.Exp, scale=-0.5), writes=[R])
                    for c in range(8):
                        P.op("dve", "scalar_tensor_tensor", dict(out=hT[:, c, ts(t, 512)], in0=x[:, c, :], scalar=prm[:, gcol + c:gcol + c + 1],
                                                                 in1=r[:], op0=ALU.mult, op1=ALU.mult), reads=[X, R, PRM], writes=[HT[t]])
                P.barrier()

        def phase_proj(l, hT, HT, pre=None):
            with ExitStack() as pes:
                wfr = Ring(P, pes, "pwf", 2, [128, 8, 512], F32, dma=True)
                wbr = Ring(P, pes, "pwb", 2, [128, 8, 512], BF16)
                stg = Ring(P, pes, "pst", 3, [128, 4, 512], BF16, dma="pool")
                sqr = Ring(P, pes, "psq", 3, [128, 512], BF16)
                rsr = Ring(P, pes, "prs", 2, [128, 512], F32)
                mb = BankRing(banks[0:5])
                sb2 = BankRing(banks[5:8])
                wv = w_in_d[l].rearrange("(kc p) n -> p kc n", p=128)
                evi = 0
                pending = []
                for sl in range(15):
                    wf, WF, wsem = wfr.next()
                    P.dma("sp", wsem, dict(out=wf[:], in_=wv[:, :, ts(sl, 512)]), writes=[WF])
                    if sl == 0 and pre is not None:
                        pre()
                    wb, WB, _ = wbr.next()
                    P.op("dve", "tensor_copy", dict(out=wb[:], in_=wf[:]), reads=[WF], writes=[WB])
                    if sl in (2, 5, 8):
                        brn = (sl - 2) // 3
                        for tb4 in range(8):
                            st, ST, ssem = stg.next()
                            for j in range(4):
                                tb = tb4 * 4 + j
                                bk, BK = mb.next()
                                for kc in range(8):
                                    P.op("pe", "matmul", dict(out=bk[:], lhsT=hT[:, kc, ts(tb, 128)], rhs=wb[:, kc, :], start=(kc == 0), stop=(kc == 7)),
                                         reads=[HT[tb // 4], WB], writes=[BK])
                                evi += 1
                                if evi % 2:
                                    P.op("act", "activation", dict(out=st[:, j, :], in_=bk[:], func=AF.Copy), reads=[BK], writes=[ST])
                                else:
                                    P.op("dve", "tensor_copy", dict(out=st[:, j, :], in_=bk[:]), reads=[BK], writes=[ST])
                            if brn == 1:
                                P.dma("pool", ssem, dict(out=v_d[brn][ts(tb4, 512), :].rearrange("(j p) n -> p j n", p=128), in_=st[:]), reads=[ST])
                            elif brn == 0:
                                for j in range(4):
                                    rows = vpa_d[ts(tb4 * 4 + j, 128), :].rearrange("p (hp c) -> p hp c", c=256)
                                    srcv = st[:, j, :].rearrange("p (hp e d) -> p hp e d", e=2, d=64)
                                    for e_ in range(2):
                                        P.dma("pool", ssem, dict(out=rows[:, :, e_ * 192:e_ * 192 + 64], in_=srcv[:, :, e_, :]), reads=[ST])
                            else:
                                for j in range(4):
                                    rows = vpc_d[ts(tb4 * 4 + j, 128), :].rearrange("p (h c) -> p h c", c=128)
                                    P.dma("pool", ssem, dict(out=rows[:, :, 0:64], in_=st[:, j, :].rearrange("p (h d) -> p h d", d=64)), reads=[ST])
                        continue
                    for j in range(4):
                        oc = sl * 4 + j
                        for half in range(2):
                            st, ST, ssem = stg.next()
                            if oc < 36:
                                dd = qk_d[ts(oc, 128), ts(half, 2048)]
                            else:
                                dd = gt_d[ts(oc - 36, 128), ts(half, 2048)]
                            for tq in range(4):
                                t = half * 4 + tq
                                bk, BK = mb.next()
                                for kc in range(8):
                                    P.op("pe", "matmul", dict(out=bk[:], lhsT=wb[:, kc, ts(j, 128)], rhs=hT[:, kc, ts(t, 512)], start=(kc == 0), stop=(kc == 7)),
                                         reads=[HT[t], WB], writes=[BK])
                                if pending:
                                    pending.pop(0)()
                                dst = st[:, tq, :]
                                evi += 1

                                def store(st=st, ST=ST, ssem=ssem, dd=dd, sl=sl, j=j, half=half):
                                    if sl in (1, 7):
                                        kp = kpa_d if sl == 1 else kpc_d
                                        for e_ in range(2):
                                            r0_ = (2 * j + e_) * 128 + e_ * 64
                                            P.dma("pool", ssem, dict(out=kp[r0_:r0_ + 64, ts(half, 2048)].rearrange("p (j n) -> p j n", j=4),
                                                                     in_=st[e_ * 64:(e_ + 1) * 64, :, :]), reads=[ST])
                                    else:
                                        P.dma("pool", ssem, dict(out=dd.rearrange("p (j n) -> p j n", j=4), in_=st[:]), reads=[ST])

                                if sl == 0:
                                    if evi % 2:
                                        P.op("act", "activation", dict(out=dst, in_=bk[:], func=AF.Copy, scale=0.125), reads=[BK], writes=[ST])
                                    else:
                                        P.op("dve", "tensor_scalar", dict(out=dst, in0=bk[:], scalar1=0.125, scalar2=None, op0=ALU.mult), reads=[BK], writes=[ST])
                                elif sl == 1:
                                    if evi % 2:
                                        P.op("act", "activation", dict(out=dst, in_=bk[:], func=AF.Copy), reads=[BK], writes=[ST])
                                    else:
                                        P.op("dve", "tensor_copy", dict(out=dst, in_=bk[:]), reads=[BK], writes=[ST])
                                elif sl in (3, 4, 6, 7):
                                    isq = sl in (3, 6)
                                    gbase = (P_GQD if sl in (3, 4) else P_GQC) + 2 * l + (0 if isq else 1)
                                    s_, S_, _ = sqr.next()
                                    P.op("act", "activation", dict(out=s_[:], in_=bk[:], func=AF.Square), reads=[BK], writes=[S_])

                                    def tail(s_=s_, S_=S_, bk=bk, BK=BK, dst=dst, ST=ST, isq=isq, gbase=gbase, last=(tq == 3), store=store):
                                        b2, B2 = sb2.next()
                                        P.op("pe", "matmul", dict(out=b2[:], lhsT=blk64, rhs=s_[:], start=True, stop=True), reads=[S_, CST], writes=[B2])
                                        r, R, _ = rsr.next()
                                        if isq:
                                            P.op("act", "activation", dict(out=r[:], in_=b2[:], func=AF.Ln, bias=64.0 * EPS, scale=1.0), reads=[B2], writes=[R])
                                        else:
                                            P.op("act", "activation", dict(out=r[:], in_=b2[:], func=AF.Ln, bias=EPS, scale=1.0 / 64), reads=[B2], writes=[R])
                                        P.op("act", "activation", dict(out=r[:], in_=r[:], func=AF.Exp, scale=-0.5), writes=[R])
                                        P.op("dve", "scalar_tensor_tensor", dict(out=dst, in0=bk[:], scalar=prm[:, gbase:gbase + 1], in1=r[:], op0=ALU.mult, op1=ALU.mult),
                                             reads=[BK, R, PRM], writes=[ST])
                                        if last:
                                            store()

                                    pending.append(tail)
                                    continue
                                else:
                                    gi = oc - 36
                                    bcol = P_BG + l * 24 + gi
                                    P.op("act", "activation", dict(out=dst, in_=bk[:], func=AF.Sigmoid, bias=prm[:, bcol:bcol + 1], scale=1.0), reads=[BK, PRM], writes=[ST])
                                if tq == 3:
                                    store()
                    while pending:
                        pending.pop(0)()
                P.barrier()

        def phase_attn_a(l):
            with ExitStack() as pes:
                vr = Ring(P, pes, "av", 2, [128, 16, 1024], BF16, dma=True)
                qr = Ring(P, pes, "aq", 2, [128, SEQ], BF16, dma=True)
                kr = Ring(P, pes, "ak", 4, [128, SEQ], BF16, dma=True)
                er = Ring(P, pes, "ae", 3, [128, 512], BF16)
                spr = Ring(P, pes, "asp", 3, [128, 512], BF16)
                pr = Ring(P, pes, "ap", 3, [128, 512], BF16)
                cr = Ring(P, pes, "ac", 3, [128, 512], BF16)
                ost = Ring(P, pes, "ao", 2, [128, 4, 512], BF16, dma="pool")
                zb = BankRing(banks[0:4])
                cbk, CBK = banks[4]
                obr = BankRing(banks[5:7])
                items = []
                for s in range(2):
                    for j in range(4):
                        for qt in range(4):
                            for e in range(2):
                                kbs = list(range(4 * qt + 3, -1, -1))
                                for ii, kb in enumerate(kbs):
                                    items.append(dict(s=s, j=j, qt=qt, e=e, kb=kb, first=(ii == 0), last=(ii == len(kbs) - 1)))
                state = dict(v=None, q=None, ob=None, st=None)

                def stage1(it):
                    s, j, qt, e, kb = it["s"], it["j"], it["qt"], it["e"], it["kb"]
                    if it["first"] and e == 0 and qt == 0:
                        q, Q, qsem = qr.next()
                        P.dma("sp", qsem, dict(out=q[:], in_=qk_d[ts(j, 128), ts(s, SEQ)]), writes=[Q])
                        state["q"] = (q, Q)
                        for e_ in range(2):
                            k, K, ksem = kr.next()
                            P.dma("sp", ksem, dict(out=k[:], in_=kpa_d[ts(2 * j + e_, 128), ts(s, SEQ)]), writes=[K])
                            state[("k", e_)] = (k, K)
                        if j == 0:
                            v, V, vsem = vr.next()
                            P.dma("sp", vsem, dict(out=v[:], in_=vpa_d[ts(s, SEQ), :].rearrange("(b p) n -> p b n", p=128)), writes=[V])
                            state["v"] = (v, V)
                        state["st"] = ost.next()
                    it["v"] = state["v"]
                    it["stg"] = state["st"]
                    q, Q = state["q"]
                    k, K = state[("k", e)]
                    it["qk"] = (q, Q, k, K)
                    p = kb - 4 * qt
                    c0 = 128 * p if p > 0 else 0
                    it["c0"] = c0
                    n = 512 - c0
                    zbk, ZB = zb.next()
                    it["zb"] = (zbk, ZB)
                    t0 = qt * 512
                    P.op("pe", "matmul", dict(out=zbk[:, c0:512], lhsT=k[:, ts(kb, 128)], rhs=q[:, t0 + c0:t0 + 512], start=True, stop=(p < 0)),
                         reads=[Q, K], writes=[ZB])
                    if p >= 0:
                        P.op("pe", "matmul", dict(out=zbk[:, c0:c0 + 128], lhsT=ident, rhs=nma, start=False, stop=True), reads=[CST], writes=[ZB])
                    ee, EE, _ = er.next()
                    P.op("act", "activation", dict(out=ee[:, 0:n], in_=zbk[:, c0:512], func=AF.Exp), reads=[ZB], writes=[EE])
                    sp, SP, _ = spr.next()
                    P.op("act", "activation", dict(out=sp[:, 0:n], in_=ee[:, 0:n], func=AF.Ln, bias=1.0, scale=1.0), reads=[EE], writes=[SP])
                    it["sp"] = (sp, SP)

                def stage2(it):
                    e, qt = it["e"], it["qt"]
                    c0 = it["c0"]
                    n = 512 - c0
                    zbk, ZB = it["zb"]
                    sp, SP = it["sp"]
                    q, Q, k, K = it["qk"]
                    if it["first"]:
                        if e == 0:
                            state["ob"] = obr.next()
                            obk, OB = state["ob"]
                            P.op("pe", "matmul", dict(out=obk[:], lhsT=zer, rhs=q[:, 0:512], start=True, stop=False), reads=[CST, Q], writes=[OB])
                        P.op("pe", "matmul", dict(out=cbk[:], lhsT=zer, rhs=q[:, 0:512], start=True, stop=False), reads=[CST, Q], writes=[CBK])
                    it["ob"] = state["ob"]
                    P.op("pe", "matmul", dict(out=zbk[:, c0:512], lhsT=nlu, rhs=sp[:, 0:n], start=False, stop=it["first"], skip_group_check=True),
                         reads=[SP, CST], writes=[ZB])
                    if not it["first"]:
                        cy, CY = state["carry"]
                        P.op("pe", "matmul", dict(out=zbk[:, c0:512], lhsT=e0m, rhs=cy[:, c0:512], start=False, stop=True, skip_group_check=True),
                             reads=[CY, CST], writes=[ZB])
                    if not it["last"]:
                        P.op("pe", "matmul", dict(out=cbk[:, c0:512], lhsT=neg1, rhs=sp[:, 0:n], start=False, stop=False, skip_group_check=True),
                             reads=[SP, CST], writes=[CBK])
                        cy, CY, _ = cr.next()
                        P.op("dve", "tensor_copy", dict(out=cy[:], in_=cbk[:]), reads=[CBK], writes=[CY])
                        state["carry"] = (cy, CY)
                    pp, PP, _ = pr.next()
                    P.op("act", "activation", dict(out=pp[:, 0:n], in_=zbk[:, c0:512], func=AF.Exp), reads=[ZB], writes=[PP])
                    it["pp"] = (pp, PP)

                def stage3(it):
                    s, j, qt, e, kb = it["s"], it["j"], it["qt"], it["e"], it["kb"]
                    c0 = it["c0"]
                    n = 512 - c0
                    pp, PP = it["pp"]
                    v, V = it["v"]
                    obk, OB = it["ob"]
                    P.op("pe", "matmul", dict(out=obk[:, c0:512], lhsT=v[:, kb, (2 * j + e) * 128:(2 * j + e + 1) * 128], rhs=pp[:, 0:n], start=False, stop=(it["last"] and e == 1), skip_group_check=True),
                         reads=[V, PP], writes=[OB])
                    if it["last"] and e == 1:
                        st, ST, ssem = it["stg"]
                        P.op("dve", "tensor_copy", dict(out=st[:, qt, :], in_=obk[:]), reads=[OB], writes=[ST])
                        if qt == 3:
                            P.dma("pool", ssem, dict(out=o_d[ts(j, 128), ts(s, SEQ)].rearrange("p (a n) -> p a n", a=4), in_=st[:]), reads=[ST])

                n_it = len(items)
                for i in range(n_it + 2):
                    if i < n_it:
                        stage1(items[i])
                    if 0 <= i - 1 < n_it:
                        stage2(items[i - 1])
                    if 0 <= i - 2 < n_it:
                        stage3(items[i - 2])
                P.barrier()

        def phase_attn_b(l):
            with ExitStack() as pes:
                vr = Ring(P, pes, "bv", 2, [128, 16, 512], BF16, dma=True)
                qr = Ring(P, pes, "bq", 4, [68, SEQ], BF16, dma=True)
                kr = Ring(P, pes, "bk", 4, [68, SEQ], BF16, dma=True)
                pr = Ring(P, pes, "bp", 4, [128, 512], BF16)
                rdr = Ring(P, pes, "brd", 2, [128, 512], F32)
                rr = Ring(P, pes, "br", 4, [128, 512], F32)
                o32 = Ring(P, pes, "bo", 2, [128, 512], F32)
                sqr = Ring(P, pes, "bsq", 2, [128, 512], BF16)
                rsr = Ring(P, pes, "brs", 2, [128, 512], F32)
                ost = Ring(P, pes, "bos", 2, [128, 4, 512], BF16, dma="pool")
                zb = BankRing(banks[0:4])
                ndr = BankRing([(banks[4], banks[5]), (banks[6], banks[7])])
                li = lam_init(l)
                items = []
                for s in range(2):
                    for h in range(4):
                        for qt in range(4):
                            for m in range(2):
                                nb = 4 * qt + 4
                                for kb in range(nb):
                                    items.append(dict(s=s, h=h, qt=qt, m=m, kb=kb, first=(kb == 0), last=(kb == nb - 1)))
                state = {}
                deferred = []

                def stage1(it):
                    s, h, qt, m, kb = it["s"], it["h"], it["qt"], it["m"], it["kb"]
                    if it["first"] and qt == 0 and m == 0:
                        for mm in range(2):
                            u = h * 2 + mm
                            q, Q, qsem = qr.next()
                            P.dma("sp", qsem, dict(out=q[0:64, :], in_=qk_d[1536 + u * 64:1536 + (u + 1) * 64, ts(s, SEQ)]), writes=[Q])
                            P.dma("sp", qsem, dict(out=q[64:68, :], in_=augq_d[h]), writes=[Q])
                            k, K, ksem = kr.next()
                            P.dma("sp", ksem, dict(out=k[0:64, :], in_=qk_d[2048 + u * 64:2048 + (u + 1) * 64, ts(s, SEQ)]), writes=[K])
                            P.dma("sp", ksem, dict(out=k[64:68, :], in_=augk_d[h]), writes=[K])
                            state[("q", mm)] = (q, Q)
                            state[("k", mm)] = (k, K)
                        if h == 0:
                            v, V, vsem = vr.next()
                            P.dma("sp", vsem, dict(out=v[:], in_=v_d[1][ts(s, SEQ), :].rearrange("(b p) n -> p b n", p=128)), writes=[V])
                            state["v"] = (v, V)
                        state["st"] = ost.next()
                    it["v"] = state["v"]
                    it["stg"] = state["st"]
                    q, Q = state[("q", m)]
                    k, K = state[("k", m)]
                    p = kb - 4 * qt
                    c0 = 128 * p if p > 0 else 0
                    it["c0"] = c0
                    zbk, ZB = zb.next()
                    t0 = qt * 512
                    P.op("pe", "matmul", dict(out=zbk[:, c0:512], lhsT=k[0:68, ts(kb, 128)], rhs=q[0:68, t0 + c0:t0 + 512], start=True, stop=(p < 0)),
                         reads=[Q, K], writes=[ZB])
                    if p >= 0:
                        P.op("pe", "matmul", dict(out=zbk[:, c0:c0 + 128], lhsT=ident, rhs=cst[:, C_CORR + h * 128:C_CORR + (h + 1) * 128], start=False, stop=True),
                             reads=[CST], writes=[ZB])
                    pp, PP, _ = pr.next()
                    P.op("act", "activation", dict(out=pp[:, 0:512 - c0], in_=zbk[:, c0:512], func=AF.Exp), reads=[ZB], writes=[PP])
                    it["pp"] = (pp, PP)

                def stage2(it):
                    s, h, qt, m, kb = it["s"], it["h"], it["qt"], it["m"], it["kb"]
                    c0 = it["c0"]
                    n = 512 - c0
                    pp, PP = it["pp"]
                    v, V = it["v"]
                    if it["first"]:
                        state["nd"] = ndr.next()
                    (nbk, NB), (dbk, DB) = state["nd"]
                    P.op("pe", "matmul", dict(out=nbk[:, c0:512], lhsT=v[:, kb, ts(h, 128)], rhs=pp[:, 0:n], start=it["first"], stop=it["last"], skip_group_check=True),
                         reads=[V, PP], writes=[NB])
                    P.op("pe", "matmul", dict(out=dbk[:, c0:512], lhsT=ones, rhs=pp[:, 0:n], start=it["first"], stop=it["last"], skip_group_check=True),
                         reads=[CST, PP], writes=[DB])
                    if it["last"]:
                        deferred.append([2, lambda it=it, nbk=nbk, NB=NB, dbk=dbk, DB=DB: combine(it, nbk, NB, dbk, DB)])

                def combine(it, nbk, NB, dbk, DB):
                    s, h, qt, m, kb = it["s"], it["h"], it["qt"], it["m"], it["kb"]
                    if True:
                        rd, RD, _ = rdr.next()
                        P.op("act", "activation", dict(out=rd[:], in_=dbk[:], func=AF.Ln), reads=[DB], writes=[RD])
                        P.op("act", "activation", dict(out=rd[:], in_=rd[:], func=AF.Exp, scale=-1.0), writes=[RD])
                        r, R, _ = rr.next()
                        P.op("dve", "tensor_tensor", dict(out=r[:], in0=nbk[:], in1=rd[:], op=ALU.mult), reads=[NB, RD], writes=[R])
                        state[("r", m)] = (r, R)
                        if m == 1:
                            r0, R0 = state[("r", 0)]
                            o, O, _ = o32.next()
                            P.op("dve", "scalar_tensor_tensor", dict(out=o[:], in0=r[:], scalar=lam_s[:, l:l + 1], in1=r0[:], op0=ALU.mult, op1=ALU.add),
                                 reads=[R, R0, LAM], writes=[O])
                            sq, SQ, _ = sqr.next()
                            P.op("act", "activation", dict(out=sq[:], in_=o[:], func=AF.Square), reads=[O], writes=[SQ])
                            deferred.append([4, lambda it=it, o=o, O=O, sq=sq, SQ=SQ: combine2(it, o, O, sq, SQ)])

                def combine2(it, o, O, sq, SQ):
                    s, h, qt, m, kb = it["s"], it["h"], it["qt"], it["m"], it["kb"]
                    if True:
                        if True:
                            ssb, SSB = zb.next()
                            P.op("pe", "matmul", dict(out=ssb[:], lhsT=ones, rhs=sq[:], start=True, stop=True), reads=[SQ, CST], writes=[SSB])
                            rs, RS, _ = rsr.next()
                            sc = 1.0 / (128.0 * (1 - li) ** 2)
                            P.op("act", "activation", dict(out=rs[:], in_=ssb[:], func=AF.Ln, bias=EPS / (1 - li) ** 2, scale=sc), reads=[SSB], writes=[RS])
                            P.op("act", "activation", dict(out=rs[:], in_=rs[:], func=AF.Exp, scale=-0.5), writes=[RS])
                            st, ST, ssem = it["stg"]
                            P.op("dve", "scalar_tensor_tensor", dict(out=st[:, qt, :], in0=o[:], scalar=prm[:, P_GSUB + l:P_GSUB + l + 1], in1=rs[:], op0=ALU.mult, op1=ALU.mult),
                                 reads=[O, RS, PRM], writes=[ST])
                            if qt == 3:
                                P.dma("pool", ssem, dict(out=o_d[512 + h * 128:512 + (h + 1) * 128, ts(s, SEQ)].rearrange("p (a n) -> p a n", a=4), in_=st[:]), reads=[ST])

                n_it = len(items)
                for i in range(n_it + 2):
                    if i < n_it:
                        stage1(items[i])
                    if 0 <= i - 2 < n_it:
                        stage2(items[i - 2])
                    for d_ in deferred:
                        d_[0] -= 1
                    while deferred and deferred[0][0] <= 0:
                        deferred.pop(0)[1]()
                while deferred:
                    deferred.pop(0)[1]()
                P.barrier()

        def phase_attn_c(l):
            with ExitStack() as pes:
                vr = Ring(P, pes, "cv", 2, [128, 16, 1024], BF16, dma=True)
                qr = Ring(P, pes, "cq", 2, [128, SEQ], BF16, dma=True)
                kr = Ring(P, pes, "ck", 4, [128, SEQ], BF16, dma=True)
                pr = Ring(P, pes, "cp", 4, [128, 512], BF16)
                rdr = Ring(P, pes, "crd", 2, [128, 512], F32)
                ost = Ring(P, pes, "cos", 2, [128, 4, 512], BF16, dma="pool")
                cb = sb("ccb", [128, 8, 640], BF16, pes)
                CB = Buf()
                mk = sb("cmk", [128, 640], F32, pes)
                MK = Buf()
                rbr = Ring(P, pes, "crb", 2, [128, 640], F32, dma=True)
                msem = P.new_sem(pes)
                P.dma("sp", msem, dict(out=mk[:], in_=maskc_d), writes=[MK])
                for h in range(8):
                    rb, RB, rsem = rbr.next()
                    P.dma("sp", rsem, dict(out=rb[:], in_=rbt_d[l][h]), writes=[RB])
                    P.op("dve", "tensor_tensor", dict(out=cb[:, h, :], in0=rb[:], in1=mk[:], op=ALU.add), reads=[RB, MK], writes=[CB])
                zb = BankRing(banks[0:4])
                ndr = BankRing(banks[4:8])
                items = []
                for s in range(2):
                    for j in range(4):
                        for qt in range(4):
                            for e in range(2):
                                ps_ = [p for p in range(-4, 4) if 4 * qt + p >= 0]
                                first_p = -1 if -1 in ps_ else 0
                                order = [first_p] + [p for p in ps_ if p != first_p]
                                for ii, p in enumerate(order):
                                    items.append(dict(s=s, j=j, qt=qt, e=e, p=p, first=(ii == 0), last=(ii == len(order) - 1)))
                state = {}
                deferred = []

                def stage1(it):
                    s, j, qt, e, p = it["s"], it["j"], it["qt"], it["e"], it["p"]
                    if it["first"] and qt == 0 and e == 0:
                        q, Q, qsem = qr.next()
                        P.dma("sp", qsem, dict(out=q[:], in_=qk_d[3072 + j * 128:3072 + (j + 1) * 128, ts(s, SEQ)]), writes=[Q])
                        state["q"] = (q, Q)
                        for e_ in range(2):
                            k, K, ksem = kr.next()
                            P.dma("sp", ksem, dict(out=k[:], in_=kpc_d[ts(2 * j + e_, 128), ts(s, SEQ)]), writes=[K])
                            state[("k", e_)] = (k, K)
                        if j == 0:
                            v, V, vsem = vr.next()
                            P.dma("sp", vsem, dict(out=v[:], in_=vpc_d[ts(s, SEQ), :].rearrange("(b p) n -> p b n", p=128)), writes=[V])
                            state["v"] = (v, V)
                        state["st"] = ost.next()
                    it["v"] = state["v"]
                    it["stg"] = state["st"]
                    q, Q = state["q"]
                    k, K = state[("k", e)]
                    h = 2 * j + e
                    kb = 4 * qt + p
                    qa = max(0, p)
                    qb_ = min(3, p + 4)
                    c0, c1 = 128 * qa, 128 * (qb_ + 1)
                    r0 = qa - p
                    it["c"] = (c0, c1)
                    it["kb"] = kb
                    zbk, ZB = zb.next()
                    t0 = qt * 512
                    P.op("pe", "matmul", dict(out=zbk[:, c0:c1], lhsT=k[:, ts(kb, 128)], rhs=q[:, t0 + c0:t0 + c1], start=True, stop=False),
                         reads=[Q, K], writes=[ZB])
                    P.op("pe", "matmul", dict(out=zbk[:, c0:c1], lhsT=ident, rhs=cb[:, h, r0 * 128:r0 * 128 + (c1 - c0)], start=False, stop=True),
                         reads=[CST, CB], writes=[ZB])
                    pp, PP, _ = pr.next()
                    P.op("act", "activation", dict(out=pp[:, 0:c1 - c0], in_=zbk[:, c0:c1], func=AF.Exp), reads=[ZB], writes=[PP])
                    it["pp"] = (pp, PP)

                def stage2(it):
                    s, j, qt, e, p = it["s"], it["j"], it["qt"], it["e"], it["p"]
                    c0, c1 = it["c"]
                    kb = it["kb"]
                    pp, PP = it["pp"]
                    v, V = it["v"]
                    if it["first"]:
                        state["nd"] = ndr.next()
                    nbk, NB = state["nd"]
                    P.op("pe", "matmul", dict(out=nbk[:, c0:c1], lhsT=v[:, kb, (2 * j + e) * 128:(2 * j + e + 1) * 128], rhs=pp[:, 0:c1 - c0],
                                              start=it["first"], stop=it["last"], skip_group_check=True), reads=[V, PP], writes=[NB])
                    if it["last"]:
                        deferred.append([2, lambda it=it, nbk=nbk, NB=NB: combine(it, nbk, NB)])

                def combine(it, nbk, NB):
                    s, j, qt, e = it["s"], it["j"], it["qt"], it["e"]
                    rd, RD, _ = rdr.next()
                    P.op("act", "activation", dict(out=rd[0:64, :], in_=nbk[64:128, :], func=AF.Ln), reads=[NB], writes=[RD])
                    P.op("act", "activation", dict(out=rd[0:64, :], in_=rd[0:64, :], func=AF.Exp, scale=-1.0), writes=[RD])
                    st, ST, ssem = it["stg"]
                    P.op("dve", "tensor_tensor", dict(out=st[e * 64:(e + 1) * 64, qt, :], in0=nbk[0:64, :], in1=rd[0:64, :], op=ALU.mult), reads=[NB, RD], writes=[ST])
                    if qt == 3 and e == 1:
                        P.dma("pool", ssem, dict(out=o_d[1024 + j * 128:1024 + (j + 1) * 128, ts(s, SEQ)].rearrange("p (a n) -> p a n", a=4), in_=st[:]), reads=[ST])

                n_it = len(items)
                wbv = wbr_d[l].rearrange("i (kc p) n -> p (i kc) n", p=128)
                wov = w_out_d[l].rearrange("(kc p) n -> p kc n", p=128)
                pf = [(wbv, c, c) for c in range(12)] + [(wov, c, 12 + c) for c in range(8)]
                for i in range(n_it + 2):
                    if i % 20 == 10 and pf:
                        src_, cs_, cd_ = pf.pop(0)
                        prefetch(src_, [(cs_, cd_)], engs=("dve",))
                    if i < n_it:
                        stage1(items[i])
                    if 0 <= i - 2 < n_it:
                        stage2(items[i - 2])
                    for d_ in deferred:
                        d_[0] -= 1
                    while deferred and deferred[0][0] <= 0:
                        deferred.pop(0)[1]()
                while deferred:
                    deferred.pop(0)[1]()
                while pf:
                    src_, cs_, cd_ = pf.pop(0)
                    prefetch(src_, [(cs_, cd_)], engs=("dve",))
                P.barrier()

        def phase_merge(l, src_d, dst_d, gcol, hout_d):
            with ExitStack() as pes:
                orr = Ring(P, pes, "mo", 1, [128, 12, 512], BF16, dma=True)
                epr = epilogue_rings(pes, "me")
                gr = Ring(P, pes, "mg", 1, [128, 24, 512], BF16, dma=True)
                xr = Ring(P, pes, "mx", 2, [128, 8, 512], F32, dma=True)
                xst = [P.new_sem(pes, "pool") for _ in range(2)]
                mr = Ring(P, pes, "mm", 2, [128, 512], F32)
                tr = Ring(P, pes, "mt", 2, [128, 512], F32)
                mbr = Ring(P, pes, "mb", 1, [128, 8, 512], BF16)
                br = BankRing(banks)
                xv = src_d.rearrange("(c p) t -> p c t", p=128)
                dv = dst_d.rearrange("(c p) t -> p c t", p=128)
                ov = o_d.rearrange("(c p) t -> p c t", p=128)
                gv = gt_d.rearrange("(c p) t -> p c t", p=128)
                pend = []
                for t in range(8):
                    o, O, osem = orr.next()
                    P.dma("sp", osem, dict(out=o[:], in_=ov[:, :, ts(t, 512)]), writes=[O])
                    g, G, gsem = gr.next()
                    P.dma("sp", gsem, dict(out=g[:], in_=gv[:, :, ts(t, 512)]), writes=[G])
                    x, X, xsem = xr.next()
                    P.dma("sp", xsem, dict(out=x[:], in_=xv[:, :, ts(t, 512)]), writes=[X])
                    mbf, MBF, _ = mbr.next()
                    sq_, SQ_, _ = epr[0].next()
                    for oc in range(8):
                        m, M, _ = mr.next()
                        for i in range(3):
                            bk, BK = br.next()
                            for kc in range(4):
                                P.op("pe", "matmul", dict(out=bk[:], lhsT=arena[:, i * 4 + kc, ts(oc, 128)], rhs=o[:, i * 4 + kc, :], start=(kc == 0), stop=(kc == 3)),
                                     reads=[ARENA[i * 4 + kc], O], writes=[BK])
                            if i == 0:
                                P.op("dve", "tensor_tensor", dict(out=m[:], in0=bk[:], in1=g[:, oc, :], op=ALU.mult), reads=[BK, G], writes=[M])
                            else:
                                tt, TT, _ = tr.next()
                                P.op("dve", "tensor_tensor", dict(out=tt[:], in0=bk[:], in1=g[:, i * 8 + oc, :], op=ALU.mult), reads=[BK, G], writes=[TT])
                                if i == 1:
                                    P.op("pool", "tensor_tensor", dict(out=m[:], in0=m[:], in1=tt[:], op=ALU.add), reads=[TT], writes=[M])
                                else:
                                    P.op("pool", "tensor_tensor", dict(out=mbf[:, oc, :], in0=m[:], in1=tt[:], op=ALU.add), reads=[TT, M], writes=[MBF])
                        if oc == 1 and pend:
                            pend.pop(0)()
                    for oc in range(8):
                        bk, BK = br.next()
                        for kc in range(8):
                            P.op("pe", "matmul", dict(out=bk[:], lhsT=arena[:, 12 + kc, ts(oc, 128)], rhs=mbf[:, kc, :], start=(kc == 0), stop=(kc == 7)),
                                 reads=[ARENA[12 + kc], MBF], writes=[BK])
                        P.op("dve", "tensor_tensor", dict(out=x[:, oc, :], in0=bk[:], in1=x[:, oc, :], op=ALU.add), reads=[BK], writes=[X])
                        P.op("act", "activation", dict(out=sq_[:, oc, :], in_=x[:, oc, :], func=AF.Square), reads=[X], writes=[SQ_])
                    P.dma("pool", xst[t % 2], dict(out=dv[:, :, ts(t, 512)], in_=x[:]), reads=[X])
                    pend.append(lambda t=t, x=x, X=X, sq_=sq_, SQ_=SQ_: norm_epi_b(epr, t, x, X, sq_, SQ_, gcol, hout_d, br))
                while pend:
                    pend.pop(0)()
                P.barrier()

        def phase_ffn_up(l, hT, HT, pre=None):
            with ExitStack() as pes:
                wgf = Ring(P, pes, "fgf", 2, [128, 8, 256], F32, dma=True)
                wuf = Ring(P, pes, "fuf", 2, [128, 8, 256], F32, dma=True)
                wgb = Ring(P, pes, "fgb", 2, [128, 8, 256], BF16)
                wub = Ring(P, pes, "fub", 2, [128, 8, 256], BF16)
                sgr = Ring(P, pes, "fsg", 3, [128, 512], F32)
                stg = Ring(P, pes, "fst", 3, [128, 4, 512], BF16, dma="pool")
                br = BankRing(banks)
                wv = w_gu_d[l].rearrange("(kc p) n -> p kc n", p=128)
                for sl in range(11):
                    wg, WG, gsem = wgf.next()
                    P.dma("sp", gsem, dict(out=wg[:], in_=wv[:, :, ts(sl, 256)]), writes=[WG])
                    wu, WU, usem = wuf.next()
                    P.dma("sp", usem, dict(out=wu[:], in_=wv[:, :, DFF + sl * 256:DFF + (sl + 1) * 256]), writes=[WU])
                    if sl == 0 and pre is not None:
                        pre()
                    gb, GB, _ = wgb.next()
                    P.op("dve", "tensor_copy", dict(out=gb[:], in_=wg[:]), reads=[WG], writes=[GB])
                    ub, UB, _ = wub.next()
                    P.op("dve", "tensor_copy", dict(out=ub[:], in_=wu[:]), reads=[WU], writes=[UB])
                    for j in range(2):
                        fc = sl * 2 + j
                        for half in range(2):
                            st, ST, ssem = stg.next()
                            for tq in range(4):
                                t = half * 4 + tq
                                gk, GK = br.next()
                                for kc in range(8):
                                    P.op("pe", "matmul", dict(out=gk[:], lhsT=gb[:, kc, ts(j, 128)], rhs=hT[:, kc, ts(t, 512)], start=(kc == 0), stop=(kc == 7)),
                                         reads=[HT[t], GB], writes=[GK])
                                uk, UK = br.next()
                                for kc in range(8):
                                    P.op("pe", "matmul", dict(out=uk[:], lhsT=ub[:, kc, ts(j, 128)], rhs=hT[:, kc, ts(t, 512)], start=(kc == 0), stop=(kc == 7)),
                                         reads=[HT[t], UB], writes=[UK])
                                sg, SG, _ = sgr.next()
                                P.op("act", "activation", dict(out=sg[:], in_=gk[:], func=AF.Silu), reads=[GK], writes=[SG])
                                P.op("dve", "tensor_tensor", dict(out=st[:, tq, :], in0=uk[:], in1=sg[:], op=ALU.mult), reads=[UK, SG], writes=[ST])
                            P.dma("pool", ssem, dict(out=ac_d[ts(fc, 128), ts(half, 2048)].rearrange("p (a n) -> p a n", a=4), in_=st[:]), reads=[ST])
                    prefetch(w_dn_d[l].rearrange("(kc p) n -> p kc n", p=128), [(2 * sl, 2 * sl), (2 * sl + 1, 2 * sl + 1)])
                P.barrier()

        def phase_ffn_down(l, src_d, dst_d, gcol, hout_d):
            with ExitStack() as pes:
                ar = Ring(P, pes, "da", 2, [128, 22, 512], BF16, dma=True)
                epr = epilogue_rings(pes, "de") if hout_d is not None else None
                xr = Ring(P, pes, "dx", 2, [128, 8, 512], F32, dma=True)
                xst = [P.new_sem(pes, "pool") for _ in range(2)]
                br = BankRing(banks)
                xv = src_d.rearrange("(c p) t -> p c t", p=128)
                dv = dst_d.rearrange("(c p) t -> p c t", p=128)
                av = ac_d.rearrange("(c p) t -> p c t", p=128)
                pend = []
                for t in range(8):
                    a, A, asem = ar.next()
                    P.dma("sp", asem, dict(out=a[:], in_=av[:, :, ts(t, 512)]), writes=[A])
                    x, X, xsem = xr.next()
                    P.dma("sp", xsem, dict(out=x[:], in_=xv[:, :, ts(t, 512)]), writes=[X])
                    if hout_d is not None:
                        sq_, SQ_, _ = epr[0].next()
                    for oc in range(8):
                        bk, BK = br.next()
                        for kc in range(22):
                            P.op("pe", "matmul", dict(out=bk[:], lhsT=arena[:, kc, ts(oc, 128)], rhs=a[:, kc, :], start=(kc == 0), stop=(kc == 21)),
                                 reads=[ARENA[kc], A], writes=[BK])
                        P.op("dve", "tensor_tensor", dict(out=x[:, oc, :], in0=bk[:], in1=x[:, oc, :], op=ALU.add), reads=[BK], writes=[X])
                        if hout_d is not None:
                            P.op("act", "activation", dict(out=sq_[:, oc, :], in_=x[:, oc, :], func=AF.Square), reads=[X], writes=[SQ_])
                        if oc == 1 and pend:
                            pend.pop(0)()
                    P.dma("pool", xst[t % 2], dict(out=dv[:, :, ts(t, 512)], in_=x[:]), reads=[X])
                    if hout_d is not None:
                        pend.append(lambda t=t, x=x, X=X, sq_=sq_, SQ_=SQ_: norm_epi_b(epr, t, x, X, sq_, SQ_, gcol, hout_d, br))
                while pend:
                    pend.pop(0)()
                P.barrier()

        phases = 0

        def done():
            nonlocal phases
            phases += 1
            return stop_after is not None and phases >= stop_after

        stop = False
        for l in range(n_layers):
            src = xT_d if l == 0 else xs_d
            with ExitStack() as les:
                hT = sb("hT", [128, 8, T], BF16, les)
                HT = [Buf("hT%d" % i) for i in range(8)]
                if l == 0:
                    zt = sb("zt", [128, 4, 1024], BF16, les)
                    ot = sb("ot", [128, 4, 1024], BF16, les)
                    ZT = Buf()
                    OT = Buf()
                    P.op("dve", "memset", dict(ap=zt[:], constant=0.0), writes=[ZT])
                    P.op("dve", "memset", dict(ap=ot[:], constant=1.0), writes=[OT])
                    fsem = P.new_sem()
                    for a_ in range(8):
                        P.dma("pool", fsem, dict(out=kpa_d[ts(a_, 128), :].rearrange("p (a n) -> p a n", a=4), in_=zt[:]), reads=[ZT])
                        P.dma("pool", fsem, dict(out=kpc_d[ts(a_, 128), :].rearrange("p (a n) -> p a n", a=4), in_=zt[:]), reads=[ZT])
                        P.dma("pool", fsem, dict(out=vpa_d[ts(a_, 512), :].rearrange("(a p) n -> p a n", p=128), in_=zt[:]), reads=[ZT])
                        P.dma("pool", fsem, dict(out=vpc_d[ts(a_, 512), :].rearrange("(a p) n -> p a n", p=128), in_=ot[:]), reads=[OT])
                    phase_norm(src, P_GMIX + l * 8, hT, HT)
                    phase_proj(l, hT, HT)
                else:
                    phase_proj(l, hT, HT, pre=lambda: load_h(h1_d, hT, HT, les))
            if done():
                break
            phase_attn_a(l)
            if done():
                break
            phase_attn_b(l)
            if done():
                break
            phase_attn_c(l)
            if done():
                break
            phase_merge(l, src, xs_d, P_GFFN + l * 8, h2_d)
            if done():
                break
            with ExitStack() as les:
                hT = sb("hT2", [128, 8, T], BF16, les)
                HT = [Buf("hT2%d" % i) for i in range(8)]
                phase_ffn_up(l, hT, HT, pre=lambda: load_h(h2_d, hT, HT, les))
            if done():
                break
            last = (l == n_layers - 1)
            phase_ffn_down(l, xs_d, yT_d if last else xs_d, P_GMIX + (l + 1) * 8 if not last else 0, None if last else h1_d)
            if done():
                break
        P.barrier()
        build.last_ninst = dict(P.ninst)
    return nc


def _bf(a):
    return np.asarray(a, dtype=np.float32).astype(ml_dtypes.bfloat16)


def make_consts():
    i = np.arange(128)[:, None]
    j = np.arange(128)[None, :]
    cst = np.zeros((128, NCST), np.float32)
    cst[:, C_ID:C_ID + 128] = (i == j)
    cst[:, C_ONE:C_ONE + 128] = 1.0
    cst[:, C_BLK:C_BLK + 128] = ((i // 64) == (j // 64))
    cst[:, C_NLU:C_NLU + 128] = -1.0 * (i >= j)
    cst[:, C_NMA:C_NMA + 128] = np.where(i < j, 0.0, NEG)
    cst[:, C_NEG1:C_NEG1 + 128] = -1.0
    for h in range(4):
        sl = SLOPES[h]
        ok = (i // 64) <= (j // 64)
        corr = np.where(ok, np.where(i > j, -2.0 * sl * (i - j), 0.0), NEG)
        cst[:, C_CORR + h * 128:C_CORR + (h + 1) * 128] = corr
    cst[:, C_E0:C_E0 + 128] = (i == 0)
    cst[:, C_HM0:C_HM0 + 64] = 1.0
    cst[:, C_HM1 + 64:C_HM1 + 128] = 1.0
    pos = np.arange(SEQ)
    augq = np.zeros((4, 4, SEQ), np.float32)
    augk = np.zeros((4, 4, SEQ), np.float32)
    for h in range(4):
        sl = SLOPES[h]
        augq[h, 0] = -sl * 64.0 * (pos // 64)
        augq[h, 1] = -sl * (pos % 64)
        augq[h, 2] = 1.0
        augq[h, 3] = 1.0
        augk[h, 0] = 1.0
        augk[h, 1] = 1.0
        augk[h, 2] = sl * 64.0 * (pos // 64)
        augk[h, 3] = sl * (pos % 64)
    maskc = np.zeros((128, 640), np.float32)
    a_ = (np.arange(128)[None, :] // 64)
    b_ = (np.arange(128)[:, None] // 64)
    maskc[:, 0:128] = np.where(a_ >= b_, 0.0, NEG)
    maskc[:, 512:640] = np.where(a_ <= b_, 0.0, NEG)
    return _bf(cst), _bf(augq), _bf(augk), maskc


def make_rbt(rel_bias):
    i = np.arange(128)[:, None]
    c = np.arange(640)[None, :]
    idx = np.clip(c - i, -128, 128) + 128
    return np.ascontiguousarray(rel_bias[:, :, idx]).astype(np.float32)


def make_prm(norm_mix_g, norm_ffn_g, b_gate, qk_g_diff, qk_g_ch, subln_g, lambda_qk):
    prm = np.zeros((128, NPRM), np.float32)
    p = np.arange(128)
    for l in range(DEPTH):
        prm[:, P_GMIX + l * 8:P_GMIX + (l + 1) * 8] = norm_mix_g[l].reshape(8, 128).T
        prm[:, P_GFFN + l * 8:P_GFFN + (l + 1) * 8] = norm_ffn_g[l].reshape(8, 128).T
        prm[:, P_BG + l * 24:P_BG + (l + 1) * 24] = b_gate[l].reshape(24, 128).T
        for k in range(2):
            prm[:, P_GQD + 2 * l + k] = qk_g_diff[l, k][p % 64]
            prm[:, P_GQC + 2 * l + k] = qk_g_ch[l, k][p % 64]
        prm[:, P_GSUB + l] = subln_g[l]
        prm[:, P_LQK + l * 256:P_LQK + (l + 1) * 256] = lambda_qk[l].reshape(1, 256)
    return prm


_NC_CACHE = {}


def kernel(x, norm_mix_g, w_in, b_gate, qk_g_diff, lambda_qk, subln_g, qk_g_ch, rel_bias,
           w_branch_sb, w_branch_diff, w_branch_ch, w_out, norm_ffn_g, w_gu, w_down):
    f = lambda a: np.ascontiguousarray(np.asarray(a, dtype=np.float32))
    x = f(x)
    cst, augq, augk, maskc = make_consts()
    shared = {
        "w_in": f(w_in),
        "w_br": np.ascontiguousarray(np.stack([f(w_branch_sb), f(w_branch_diff), f(w_branch_ch)], axis=1)),
        "w_out": f(w_out), "w_gu": f(w_gu), "w_down": f(w_down),
        "rbt": make_rbt(f(rel_bias)),
        "prm": make_prm(f(norm_mix_g), f(norm_ffn_g), f(b_gate), f(qk_g_diff), f(qk_g_ch), f(subln_g), f(lambda_qk)),
        "cst": cst, "augq": augq, "augk": augk, "maskc": maskc,
    }
    in_maps = []
    for c in range(NCORES):
        m = dict(shared)
        m["xT"] = np.ascontiguousarray(x[2 * c:2 * c + 2].reshape(T, D).T)
        in_maps.append(m)
    if "nc" not in _NC_CACHE:
        _NC_CACHE["nc"] = build()
    res = run_bass_kernel_spmd(_NC_CACHE["nc"], in_maps, core_ids=list(range(NCORES)))
    out = np.empty((16, SEQ, D), np.float32)
    for c in range(NCORES):
        out[2 * c:2 * c + 2] = np.asarray(res.results[c]["yT"]).T.reshape(2, SEQ, D)
    return out
```

```python
import math
import numpy as np
import ml_dtypes
from contextlib import ExitStack
import concourse.bass as bass
import concourse.mybir as mybir
from concourse.bass_utils import run_bass_kernel_spmd

F32 = mybir.dt.float32
BF16 = mybir.dt.bfloat16
AF = mybir.ActivationFunctionType
ALU = mybir.AluOpType

NCORES = 8
D = 1024
SEQ = 2048
T = 4096
DEPTH = 4
DFF = 2816
NIN = 7680
EPS = 1e-6
NEG = -30000.0
SLOPES = [2.0 ** (-8.0 * (i + 1) / 4) for i in range(4)]

C_ID, C_ONE, C_BLK, C_NLU, C_NMA, C_NEG1, C_ZERO, C_CORR = 0, 128, 256, 384, 512, 640, 768, 896
C_E0, C_HM0, C_HM1 = 1408, 1536, 1664
NCST = 1792
P_GMIX, P_GFFN, P_BG, P_GQD, P_GQC, P_GSUB, P_LQK = 0, 32, 64, 160, 168, 176, 180
NPRM = 180 + 1024


def ts(i, n):
    return slice(i * n, (i + 1) * n)


class Buf:
    __slots__ = ("name", "w", "r")

    def __init__(self, name=""):
        self.name = name
        self.w = None
        self.r = {}


class Prog:
    ENG = ("pe", "act", "dve", "pool", "sp")

    def __init__(self, nc, es, block):
        self.nc = nc
        self.es = es
        self.block = block
        self.streams = {e: [] for e in self.ENG}
        self.sems = {}
        self.cnt = {}
        self.known = {e: {} for e in self.ENG}
        for e in self.ENG:
            self.sems[e] = es.enter_context(nc.semaphore("s_" + e))
            self.cnt[e] = 0
        self.nsem = 0
        self.pools = {}
        self.uid = 0
        self.ninst = {e: 0 for e in self.ENG}

    def new_sem(self, pes=None, kind="sp"):
        if pes is not None:
            pool = self.pools.setdefault(kind, [])
            if pool:
                name = pool.pop()
            else:
                name = self.new_sem()
            pes.callback(pool.append, name)
            return name
        self.nsem += 1
        name = "d%d" % self.nsem
        self.sems[name] = self.es.enter_context(self.nc.semaphore(name))
        self.cnt[name] = 0
        return name

    def _deps(self, eng, reads, writes):
        waits = {}
        for b in reads:
            if b.w is not None and waits.get(b.w[0], 0) < b.w[1]:
                waits[b.w[0]] = b.w[1]
        for b in writes:
            if b.w is not None and waits.get(b.w[0], 0) < b.w[1]:
                waits[b.w[0]] = b.w[1]
            for k, v in b.r.items():
                if waits.get(k, 0) < v:
                    waits[k] = v
        self._emit_waits(eng, waits)

    def _emit_waits(self, eng, waits, skip_pe_self=True):
        st = self.streams[eng]
        kn = self.known[eng]
        for k, v in waits.items():
            if skip_pe_self and k == "pe" and eng == "pe":
                continue
            if kn.get(k, 0) >= v:
                continue
            kn[k] = v
            st.append(("wait", self.sems[k], v))
            self.ninst[eng] += 1

    def op(self, eng, name, kw, reads=(), writes=()):
        self._deps(eng, reads, writes)
        self.cnt[eng] += 1
        c = self.cnt[eng]
        self.streams[eng].append(("op", name, kw, self.sems[eng], 1))
        self.ninst[eng] += 1
        for b in reads:
            if b.r.get(eng, 0) < c:
                b.r[eng] = c
        for b in writes:
            b.w = (eng, c)
            b.r = {}

    def dma(self, q, semname, kw, reads=(), writes=()):
        self._deps(q, reads, writes)
        self.cnt[semname] += 16
        c = self.cnt[semname]
        self.streams[q].append(("op", "dma_start", kw, self.sems[semname], 16))
        self.ninst[q] += 1
        for b in reads:
            if b.r.get(semname, 0) < c:
                b.r[semname] = c
        for b in writes:
            b.w = (semname, c)
            b.r = {}

    def barrier(self):
        waits = {k: v for k, v in self.cnt.items() if v > 0}
        for e in self.ENG:
            self._emit_waits(e, dict(waits), skip_pe_self=False)
        self.flush()

    def flush(self):
        for en, meth in (("pe", "tensor"), ("act", "scalar"), ("dve", "vector"), ("pool", "gpsimd"), ("sp", "sync")):
            items = self.streams[en]
            if not items:
                continue
            self.streams[en] = []

            def body(e, items=items):
                for it in items:
                    if it[0] == "wait":
                        e.wait_ge(it[1], it[2])
                    else:
                        getattr(e, it[1])(**it[2]).then_inc(it[3], it[4])

            getattr(self.block, meth)(body)


class Ring:
    def __init__(self, P, es, name, n, shape, dtype, dma=False):
        self.slots = []
        for i in range(n):
            P.uid += 1
            t = es.enter_context(P.nc.sbuf_tensor("%s%d_%d" % (name, i, P.uid), shape, dtype))
            self.slots.append((t, Buf(name + str(i)), P.new_sem(es, dma if isinstance(dma, str) else "sp") if dma else None))
        self.i = 0
        self.n = n

    def next(self):
        s = self.slots[self.i % self.n]
        self.i += 1
        return s


class BankRing:
    def __init__(self, banks):
        self.b = banks
        self.i = 0

    def next(self):
        s = self.b[self.i % len(self.b)]
        self.i += 1
        return s


def lam_init(l):
    return 0.8 - 0.6 * math.exp(-0.3 * l)


def build(n_layers=DEPTH, dbg=False, stop_after=None):
    nc = bass.Bass("TRN2", target_bir_lowering=False)
    dt = lambda name, shape, dtype, kind: nc.dram_tensor(name, shape, dtype, kind=kind).ap()
    skind = "ExternalOutput" if dbg else "Internal"
    xT_d = dt("xT", [D, T], F32, "ExternalInput")
    w_in_d = dt("w_in", [DEPTH, D, NIN], F32, "ExternalInput")
    wbr_d = dt("w_br", [DEPTH, 3, 512, D], F32, "ExternalInput")
    w_out_d = dt("w_out", [DEPTH, D, D], F32, "ExternalInput")
    w_gu_d = dt("w_gu", [DEPTH, D, 2 * DFF], F32, "ExternalInput")
    w_dn_d = dt("w_down", [DEPTH, DFF, D], F32, "ExternalInput")
    rbt_d = dt("rbt", [DEPTH, 8, 128, 640], F32, "ExternalInput")
    prm_d = dt("prm", [128, NPRM], F32, "ExternalInput")
    cst_d = dt("cst", [128, NCST], BF16, "ExternalInput")
    augq_d = dt("augq", [4, 4, SEQ], BF16, "ExternalInput")
    augk_d = dt("augk", [4, 4, SEQ], BF16, "ExternalInput")
    maskc_d = dt("maskc", [128, 640], F32, "ExternalInput")
    yT_d = dt("yT", [D, T], F32, "ExternalOutput")
    xs_d = dt("xs", [D, T], F32, skind)
    qk_d = dt("qk", [4608, T], BF16, skind)
    v_d = dt("vs", [3, T, 512], BF16, skind)
    gt_d = dt("gts", [3072, T], BF16, skind)
    o_d = dt("os", [1536, T], BF16, skind)
    ac_d = dt("acs", [DFF, T], BF16, skind)
    kpa_d = dt("kpa", [1024, T], BF16, "Internal")
    kpc_d = dt("kpc", [1024, T], BF16, "Internal")
    vpa_d = dt("vpa", [T, 1024], BF16, "Internal")
    vpc_d = dt("vpc", [T, 1024], BF16, "Internal")
    h1_d = dt("h1s", [D, T], BF16, "Internal")
    h2_d = dt("h2s", [D, T], BF16, "Internal")

    with ExitStack() as es:
        block = es.enter_context(nc.Block())
        P = Prog(nc, es, block)
        def sb(n, s, d, e=es):
            P.uid += 1
            return e.enter_context(nc.sbuf_tensor("%s_%d" % (n, P.uid), s, d))
        banks = []
        for i in range(8):
            banks.append((es.enter_context(nc.psum_tensor("bank%d" % i, [128, 512], F32)), Buf("bank%d" % i)))
        cst = sb("cst_s", [128, NCST], BF16)
        prm = sb("prm_s", [128, NPRM], F32)
        lam_s = sb("lam_s", [128, 16], F32)
        arena = sb("arena", [128, 22, D], BF16)
        ARENA = [Buf("ar%d" % i) for i in range(22)]
        arst = Ring(P, es, "arst", 2, [128, 1, D], F32, dma=True)
        arcnt = [0]

        def prefetch(src3, chunks, engs=("dve", "act")):
            for (c_src, c_dst) in chunks:
                st, ST, sem = arst.next()
                P.dma("sp", sem, dict(out=st[:, 0, :], in_=src3[:, c_src, :]), writes=[ST])
                arcnt[0] += 1
                en = engs[arcnt[0] % len(engs)]
                if en == "dve":
                    P.op("dve", "tensor_copy", dict(out=arena[:, c_dst, :], in_=st[:, 0, :]), reads=[ST], writes=[ARENA[c_dst]])
                else:
                    P.op("act", "activation", dict(out=arena[:, c_dst, :], in_=st[:, 0, :], func=AF.Copy), reads=[ST], writes=[ARENA[c_dst]])
        CST = Buf("cst")
        PRM = Buf("prm")
        LAM = Buf("lam")
        s0 = P.new_sem()
        P.dma("sp", s0, dict(out=cst[:], in_=cst_d), writes=[CST])
        P.dma("sp", s0, dict(out=prm[:], in_=prm_d), writes=[PRM])
        ident = cst[:, C_ID:C_ID + 128]
        ones = cst[:, C_ONE:C_ONE + 128]
        blk64 = cst[:, C_BLK:C_BLK + 128]
        nlu = cst[:, C_NLU:C_NLU + 128]
        nma = cst[:, C_NMA:C_NMA + 128]
        neg1 = cst[:, C_NEG1:C_NEG1 + 128]
        zer = cst[:, C_ZERO:C_ZERO + 128]
        e0m = cst[:, C_E0:C_E0 + 128]
        hm = [cst[:, C_HM0:C_HM0 + 128], cst[:, C_HM1:C_HM1 + 128]]

        with ExitStack() as pes:
            tmp = sb("lq_tmp", [128, 64], F32, pes)
            sacc = sb("lq_acc", [128, 8], F32, pes)
            TMP = Buf()
            SACC = Buf()
            for l in range(DEPTH):
                for k in range(2):
                    a = prm[:, P_LQK + l * 256 + (2 * k) * 64: P_LQK + l * 256 + (2 * k + 1) * 64]
                    b = prm[:, P_LQK + l * 256 + (2 * k + 1) * 64: P_LQK + l * 256 + (2 * k + 2) * 64]
                    P.op("dve", "scalar_tensor_tensor", dict(out=tmp[:], in0=a, scalar=1.0, in1=b, op0=ALU.mult, op1=ALU.mult,
                                                             accum_out=sacc[:, 2 * l + k:2 * l + k + 1]), reads=[PRM], writes=[TMP, SACC])
            P.op("act", "activation", dict(out=sacc[:], in_=sacc[:], func=AF.Exp), writes=[SACC])
            for l in range(DEPTH):
                P.op("dve", "scalar_tensor_tensor", dict(out=lam_s[:, l:l + 1], in0=sacc[:, 2 * l + 1:2 * l + 2], scalar=-lam_init(l),
                                                         in1=sacc[:, 2 * l:2 * l + 1], op0=ALU.add, op1=ALU.subtract), reads=[SACC], writes=[LAM])
            P.barrier()

        def load_cast(pes, name, dst, DSTL, src3, nchunk, width, ring=None):
            per = max(1, 2048 // width)
            if ring is None:
                ring = Ring(P, pes, name, 2, [128, per, width], F32, dma=True)
            c = 0
            k = 0
            while c < nchunk:
                n = min(per, nchunk - c)
                st, ST, sem = ring.next()
                P.dma("sp", sem, dict(out=st[:, 0:n, :], in_=src3[:, c:c + n, :]), writes=[ST])
                for i in range(n):
                    k += 1
                    if k % 2:
                        P.op("dve", "tensor_copy", dict(out=dst[:, c + i, :], in_=st[:, i, :]), reads=[ST], writes=[DSTL[c + i]])
                    else:
                        P.op("act", "activation", dict(out=dst[:, c + i, :], in_=st[:, i, :], func=AF.Copy), reads=[ST], writes=[DSTL[c + i]])
                c += n

        def norm_epi_b(pes_rings, t, x, X, sq, SQ, gcol, hout_d, br):
            sqr, rsr, hr = pes_rings
            bk, BK = br.next()
            for c in range(8):
                P.op("pe", "matmul", dict(out=bk[:], lhsT=ones, rhs=sq[:, c, :], start=(c == 0), stop=(c == 7)), reads=[SQ, CST], writes=[BK])
            r, R, _ = rsr.next()
            P.op("act", "activation", dict(out=r[:], in_=bk[:], func=AF.Ln, bias=EPS, scale=1.0 / D), reads=[BK], writes=[R])
            P.op("act", "activation", dict(out=r[:], in_=r[:], func=AF.Exp, scale=-0.5), writes=[R])
            hh, HH, hsem = hr.next()
            for c in range(8):
                P.op("dve", "scalar_tensor_tensor", dict(out=hh[:, c, :], in0=x[:, c, :], scalar=prm[:, gcol + c:gcol + c + 1],
                                                         in1=r[:], op0=ALU.mult, op1=ALU.mult), reads=[X, R, PRM], writes=[HH])
            P.dma("pool", hsem, dict(out=hout_d.rearrange("(c p) t -> p c t", p=128)[:, :, ts(t, 512)], in_=hh[:]), reads=[HH])

        def epilogue_rings(pes, nm):
            return (Ring(P, pes, nm + "sq", 2, [128, 8, 512], BF16), Ring(P, pes, nm + "rs", 2, [128, 512], F32),
                    Ring(P, pes, nm + "hh", 1, [128, 8, 512], BF16, dma="pool"))

        def load_h(h_d, hT, HT, les):
            hv = h_d.rearrange("(c p) t -> p c t", p=128)
            for t in range(8):
                sem = P.new_sem(les)
                P.dma("sp", sem, dict(out=hT[:, :, ts(t, 512)], in_=hv[:, :, ts(t, 512)]), writes=[HT[t]])

        def phase_norm(src_d, gcol, hT, HT):
            with ExitStack() as pes:
                xr = Ring(P, pes, "nx", 2, [128, 8, 512], F32, dma=True)
                sq = Ring(P, pes, "nsq", 2, [128, 8, 512], BF16)
                rs = Ring(P, pes, "nrs", 2, [128, 512], F32)
                br = BankRing(banks[0:2])
                xv = src_d.rearrange("(c p) t -> p c t", p=128)
                for t in range(8):
                    x, X, xsem = xr.next()
                    P.dma("sp", xsem, dict(out=x[:], in_=xv[:, :, ts(t, 512)]), writes=[X])
                    s, S, _ = sq.next()
                    P.op("act", "activation", dict(out=s[:], in_=x[:], func=AF.Square), reads=[X], writes=[S])
                    bk, BK = br.next()
                    for c in range(8):
                        P.op("pe", "matmul", dict(out=bk[:], lhsT=ones, rhs=s[:, c, :], start=(c == 0), stop=(c == 7)), reads=[S, CST], writes=[BK])
                    r, R, _ = rs.next()
                    P.op("act", "activation", dict(out=r[:], in_=bk[:], func=AF.Ln, bias=EPS, scale=1.0 / D), reads=[BK], writes=[R])
                    P.op("act", "activation", dict(out=r[:], in_=r[:], func=AF.Exp, scale=-0.5), writes=[R])
                    for c in range(8):
                        P.op("dve", "scalar_tensor_tensor", dict(out=hT[:, c, ts(t, 512)], in0=x[:, c, :], scalar=prm[:, gcol + c:gcol + c + 1],
                                                                 in1=r[:], op0=ALU.mult, op1=ALU.mult), reads=[X, R, PRM], writes=[HT[t]])
                P.barrier()

        def phase_proj(l, hT, HT, pre=None):
            with ExitStack() as pes:
                wfr = Ring(P, pes, "pwf", 2, [128, 8, 512], F32, dma=True)
                wbr = Ring(P, pes, "pwb", 2, [128, 8, 512], BF16)
                stg = Ring(P, pes, "pst", 3, [128, 4, 512], BF16, dma="pool")
                sqr = Ring(P, pes, "psq", 3, [128, 512], BF16)
                rsr = Ring(P, pes, "prs", 2, [128, 512], F32)
                mb = BankRing(banks[0:5])
                sb2 = BankRing(banks[5:8])
                wv = w_in_d[l].rearrange("(kc p) n -> p kc n", p=128)
                evi = 0
                pending = []
                for sl in range(15):
                    wf, WF, wsem = wfr.next()
                    P.dma("sp", wsem, dict(out=wf[:], in_=wv[:, :, ts(sl, 512)]), writes=[WF])
                    if sl == 0 and pre is not None:
                        pre()
                    wb, WB, _ = wbr.next()
                    P.op("dve", "tensor_copy", dict(out=wb[:], in_=wf[:]), reads=[WF], writes=[WB])
                    if sl in (2, 5, 8):
                        brn = (sl - 2) // 3
                        for tb4 in range(8):
                            st, ST, ssem = stg.next()
                            for j in range(4):
                                tb = tb4 * 4 + j
                                bk, BK = mb.next()
                                for kc in range(8):
                                    P.op("pe", "matmul", dict(out=bk[:], lhsT=hT[:, kc, ts(tb, 128)], rhs=wb[:, kc, :], start=(kc == 0), stop=(kc == 7)),
                                         reads=[HT[tb // 4], WB], writes=[BK])
                                evi += 1
                                if evi % 2:
                                    P.op("act", "activation", dict(out=st[:, j, :], in_=bk[:], func=AF.Copy), reads=[BK], writes=[ST])
                                else:
                                    P.op("dve", "tensor_copy", dict(out=st[:, j, :], in_=bk[:]), reads=[BK], writes=[ST])
                            if brn == 1:
                                P.dma("pool", ssem, dict(out=v_d[brn][ts(tb4, 512), :].rearrange("(j p) n -> p j n", p=128), in_=st[:]), reads=[ST])
                            elif brn == 0:
                                for j in range(4):
                                    rows = vpa_d[ts(tb4 * 4 + j, 128), :].rearrange("p (hp c) -> p hp c", c=256)
                                    srcv = st[:, j, :].rearrange("p (hp e d) -> p hp e d", e=2, d=64)
                                    for e_ in range(2):
                                        P.dma("pool", ssem, dict(out=rows[:, :, e_ * 192:e_ * 192 + 64], in_=srcv[:, :, e_, :]), reads=[ST])
                            else:
                                for j in range(4):
                                    rows = vpc_d[ts(tb4 * 4 + j, 128), :].rearrange("p (h c) -> p h c", c=128)
                                    P.dma("pool", ssem, dict(out=rows[:, :, 0:64], in_=st[:, j, :].rearrange("p (h d) -> p h d", d=64)), reads=[ST])
                        continue
                    for j in range(4):
                        oc = sl * 4 + j
                        for half in range(2):
                            st, ST, ssem = stg.next()
                            if oc < 36:
                                dd = qk_d[ts(oc, 128), ts(half, 2048)]
                            else:
                                dd = gt_d[ts(oc - 36, 128), ts(half, 2048)]
                            for tq in range(4):
                                t = half * 4 + tq
                                bk, BK = mb.next()
                                for kc in range(8):
                                    P.op("pe", "matmul", dict(out=bk[:], lhsT=wb[:, kc, ts(j, 128)], rhs=hT[:, kc, ts(t, 512)], start=(kc == 0), stop=(kc == 7)),
                                         reads=[HT[t], WB], writes=[BK])
                                if pending:
                                    pending.pop(0)()
                                dst = st[:, tq, :]
                                evi += 1

                                def store(st=st, ST=ST, ssem=ssem, dd=dd, sl=sl, j=j, half=half):
                                    if sl in (1, 7):
                                        kp = kpa_d if sl == 1 else kpc_d
                                        for e_ in range(2):
                                            r0_ = (2 * j + e_) * 128 + e_ * 64
                                            P.dma("pool", ssem, dict(out=kp[r0_:r0_ + 64, ts(half, 2048)].rearrange("p (j n) -> p j n", j=4),
                                                                     in_=st[e_ * 64:(e_ + 1) * 64, :, :]), reads=[ST])
                                    else:
                                        P.dma("pool", ssem, dict(out=dd.rearrange("p (j n) -> p j n", j=4), in_=st[:]), reads=[ST])

                                if sl == 0:
                                    if evi % 2:
                                        P.op("act", "activation", dict(out=dst, in_=bk[:], func=AF.Copy, scale=0.125), reads=[BK], writes=[ST])
                                    else:
                                        P.op("dve", "tensor_scalar", dict(out=dst, in0=bk[:], scalar1=0.125, scalar2=None, op0=ALU.mult), reads=[BK], writes=[ST])
                                elif sl == 1:
                                    if evi % 2:
                                        P.op("act", "activation", dict(out=dst, in_=bk[:], func=AF.Copy), reads=[BK], writes=[ST])
                                    else:
                                        P.op("dve", "tensor_copy", dict(out=dst, in_=bk[:]), reads=[BK], writes=[ST])
                                elif sl in (3, 4, 6, 7):
                                    isq = sl in (3, 6)
                                    gbase = (P_GQD if sl in (3, 4) else P_GQC) + 2 * l + (0 if isq else 1)
                                    s_, S_, _ = sqr.next()
                                    P.op("act", "activation", dict(out=s_[:], in_=bk[:], func=AF.Square), reads=[BK], writes=[S_])

                                    def tail(s_=s_, S_=S_, bk=bk, BK=BK, dst=dst, ST=ST, isq=isq, gbase=gbase, last=(tq == 3), store=store):
                                        b2, B2 = sb2.next()
                                        P.op("pe", "matmul", dict(out=b2[:], lhsT=blk64, rhs=s_[:], start=True, stop=True), reads=[S_, CST], writes=[B2])
                                        r, R, _ = rsr.next()
                                        if isq:
                                            P.op("act", "activation", dict(out=r[:], in_=b2[:], func=AF.Ln, bias=64.0 * EPS, scale=1.0), reads=[B2], writes=[R])
                                        else:
                                            P.op("act", "activation", dict(out=r[:], in_=b2[:], func=AF.Ln, bias=EPS, scale=1.0 / 64), reads=[B2], writes=[R])
                                        P.op("act", "activation", dict(out=r[:], in_=r[:], func=AF.Exp, scale=-0.5), writes=[R])
                                        P.op("dve", "scalar_tensor_tensor", dict(out=dst, in0=bk[:], scalar=prm[:, gbase:gbase + 1], in1=r[:], op0=ALU.mult, op1=ALU.mult),
                                             reads=[BK, R, PRM], writes=[ST])
                                        if last:
                                            store()

                                    pending.append(tail)
                                    continue
                                else:
                                    gi = oc - 36
                                    bcol = P_BG + l * 24 + gi
                                    P.op("act", "activation", dict(out=dst, in_=bk[:], func=AF.Sigmoid, bias=prm[:, bcol:bcol + 1], scale=1.0), reads=[BK, PRM], writes=[ST])
                                if tq == 3:
                                    store()
                    while pending:
                        pending.pop(0)()
                P.barrier()

        def phase_attn_a(l):
            with ExitStack() as pes:
                vr = Ring(P, pes, "av", 2, [128, 16, 1024], BF16, dma=True)
                qr = Ring(P, pes, "aq", 2, [128, SEQ], BF16, dma=True)
                kr = Ring(P, pes, "ak", 4, [128, SEQ], BF16, dma=True)
                er = Ring(P, pes, "ae", 3, [128, 512], F32)
                spr = Ring(P, pes, "asp", 3, [128, 512], BF16)
                pr = Ring(P, pes, "ap", 3, [128, 512], BF16)
                cr = Ring(P, pes, "ac", 3, [128, 512], BF16)
                ost = Ring(P, pes, "ao", 2, [128, 4, 512], BF16, dma="pool")
                zb = BankRing(banks[0:4])
                cbk, CBK = banks[4]
                obr = BankRing(banks[5:7])
                items = []
                for s in range(2):
                    for j in range(4):
                        for qt in range(4):
                            for e in range(2):
                                kbs = list(range(4 * qt + 3, -1, -1))
                                for ii, kb in enumerate(kbs):
                                    items.append(dict(s=s, j=j, qt=qt, e=e, kb=kb, first=(ii == 0), last=(ii == len(kbs) - 1)))
                state = dict(v=None, q=None, ob=None, st=None)

                def stage1(it):
                    s, j, qt, e, kb = it["s"], it["j"], it["qt"], it["e"], it["kb"]
                    if it["first"] and e == 0 and qt == 0:
                        q, Q, qsem = qr.next()
                        P.dma("sp", qsem, dict(out=q[:], in_=qk_d[ts(j, 128), ts(s, SEQ)]), writes=[Q])
                        state["q"] = (q, Q)
                        for e_ in range(2):
                            k, K, ksem = kr.next()
                            P.dma("sp", ksem, dict(out=k[:], in_=kpa_d[ts(2 * j + e_, 128), ts(s, SEQ)]), writes=[K])
                            state[("k", e_)] = (k, K)
                        if j == 0:
                            v, V, vsem = vr.next()
                            P.dma("sp", vsem, dict(out=v[:], in_=vpa_d[ts(s, SEQ), :].rearrange("(b p) n -> p b n", p=128)), writes=[V])
                            state["v"] = (v, V)
                        state["st"] = ost.next()
                    it["v"] = state["v"]
                    it["stg"] = state["st"]
                    q, Q = state["q"]
                    k, K = state[("k", e)]
                    it["qk"] = (q, Q, k, K)
                    p = kb - 4 * qt
                    c0 = 128 * p if p > 0 else 0
                    it["c0"] = c0
                    n = 512 - c0
                    zbk, ZB = zb.next()
                    it["zb"] = (zbk, ZB)
                    t0 = qt * 512
                    P.op("pe", "matmul", dict(out=zbk[:, c0:512], lhsT=k[:, ts(kb, 128)], rhs=q[:, t0 + c0:t0 + 512], start=True, stop=(p < 0)),
                         reads=[Q, K], writes=[ZB])
                    if p >= 0:
                        P.op("pe", "matmul", dict(out=zbk[:, c0:c0 + 128], lhsT=ident, rhs=nma, start=False, stop=True), reads=[CST], writes=[ZB])
                    ee, EE, _ = er.next()
                    P.op("act", "activation", dict(out=ee[:, 0:n], in_=zbk[:, c0:512], func=AF.Exp), reads=[ZB], writes=[EE])
                    sp, SP, _ = spr.next()
                    P.op("act", "activation", dict(out=sp[:, 0:n], in_=ee[:, 0:n], func=AF.Ln, bias=1.0, scale=1.0), reads=[EE], writes=[SP])
                    it["sp"] = (sp, SP)

                def stage2(it):
                    e, qt = it["e"], it["qt"]
                    c0 = it["c0"]
                    n = 512 - c0
                    zbk, ZB = it["zb"]
                    sp, SP = it["sp"]
                    q, Q, k, K = it["qk"]
                    if it["first"]:
                        if e == 0:
                            state["ob"] = obr.next()
                            obk, OB = state["ob"]
                            P.op("pe", "matmul", dict(out=obk[:], lhsT=zer, rhs=q[:, 0:512], start=True, stop=False), reads=[CST, Q], writes=[OB])
                        P.op("pe", "matmul", dict(out=cbk[:], lhsT=zer, rhs=q[:, 0:512], start=True, stop=False), reads=[CST, Q], writes=[CBK])
                    it["ob"] = state["ob"]
                    P.op("pe", "matmul", dict(out=zbk[:, c0:512], lhsT=nlu, rhs=sp[:, 0:n], start=False, stop=it["first"], skip_group_check=True),
                         reads=[SP, CST], writes=[ZB])
                    if not it["first"]:
                        cy, CY = state["carry"]
                        P.op("pe", "matmul", dict(out=zbk[:, c0:512], lhsT=e0m, rhs=cy[:, c0:512], start=False, stop=True, skip_group_check=True),
                             reads=[CY, CST], writes=[ZB])
                    if not it["last"]:
                        P.op("pe", "matmul", dict(out=cbk[:, c0:512], lhsT=neg1, rhs=sp[:, 0:n], start=False, stop=True, skip_group_check=True),
                             reads=[SP, CST], writes=[CBK])
                        cy, CY, _ = cr.next()
                        P.op("dve", "tensor_copy", dict(out=cy[:], in_=cbk[:]), reads=[CBK], writes=[CY])
                        state["carry"] = (cy, CY)
                    pp, PP, _ = pr.next()
                    P.op("act", "activation", dict(out=pp[:, 0:n], in_=zbk[:, c0:512], func=AF.Exp), reads=[ZB], writes=[PP])
                    it["pp"] = (pp, PP)

                def stage3(it):
                    s, j, qt, e, kb = it["s"], it["j"], it["qt"], it["e"], it["kb"]
                    c0 = it["c0"]
                    n = 512 - c0
                    pp, PP = it["pp"]
                    v, V = it["v"]
                    obk, OB = it["ob"]
                    P.op("pe", "matmul", dict(out=obk[:, c0:512], lhsT=v[:, kb, (2 * j + e) * 128:(2 * j + e + 1) * 128], rhs=pp[:, 0:n], start=False, stop=(it["last"] and e == 1), skip_group_check=True),
                         reads=[V, PP], writes=[OB])
                    if it["last"] and e == 1:
                        st, ST, ssem = it["stg"]
                        P.op("dve", "tensor_copy", dict(out=st[:, qt, :], in_=obk[:]), reads=[OB], writes=[ST])
                        if qt == 3:
                            P.dma("pool", ssem, dict(out=o_d[ts(j, 128), ts(s, SEQ)].rearrange("p (a n) -> p a n", a=4), in_=st[:]), reads=[ST])

                n_it = len(items)
                for i in range(n_it + 2):
                    if i < n_it:
                        stage1(items[i])
                    if 0 <= i - 1 < n_it:
                        stage2(items[i - 1])
                    if 0 <= i - 2 < n_it:
                        stage3(items[i - 2])
                P.barrier()

        def phase_attn_b(l):
            with ExitStack() as pes:
                vr = Ring(P, pes, "bv", 2, [128, 16, 512], BF16, dma=True)
                qr = Ring(P, pes, "bq", 4, [68, SEQ], BF16, dma=True)
                kr = Ring(P, pes, "bk", 4, [68, SEQ], BF16, dma=True)
                pr = Ring(P, pes, "bp", 4, [128, 512], BF16)
                rdr = Ring(P, pes, "brd", 2, [128, 512], F32)
                rr = Ring(P, pes, "br", 4, [128, 512], F32)
                o32 = Ring(P, pes, "bo", 2, [128, 512], F32)
                sqr = Ring(P, pes, "bsq", 2, [128, 512], BF16)
                rsr = Ring(P, pes, "brs", 2, [128, 512], F32)
                ost = Ring(P, pes, "bos", 2, [128, 4, 512], BF16, dma="pool")
                zb = BankRing(banks[0:4])
                ndr = BankRing([(banks[4], banks[5]), (banks[6], banks[7])])
                li = lam_init(l)
                items = []
                for s in range(2):
                    for h in range(4):
                        for qt in range(4):
                            for m in range(2):
                                nb = 4 * qt + 4
                                for kb in range(nb):
                                    items.append(dict(s=s, h=h, qt=qt, m=m, kb=kb, first=(kb == 0), last=(kb == nb - 1)))
                state = {}
                deferred = []

                def stage1(it):
                    s, h, qt, m, kb = it["s"], it["h"], it["qt"], it["m"], it["kb"]
                    if it["first"] and qt == 0 and m == 0:
                        for mm in range(2):
                            u = h * 2 + mm
                            q, Q, qsem = qr.next()
                            P.dma("sp", qsem, dict(out=q[0:64, :], in_=qk_d[1536 + u * 64:1536 + (u + 1) * 64, ts(s, SEQ)]), writes=[Q])
                            P.dma("sp", qsem, dict(out=q[64:68, :], in_=augq_d[h]), writes=[Q])
                            k, K, ksem = kr.next()
                            P.dma("sp", ksem, dict(out=k[0:64, :], in_=qk_d[2048 + u * 64:2048 + (u + 1) * 64, ts(s, SEQ)]), writes=[K])
                            P.dma("sp", ksem, dict(out=k[64:68, :], in_=augk_d[h]), writes=[K])
                            state[("q", mm)] = (q, Q)
                            state[("k", mm)] = (k, K)
                        if h == 0:
                            v, V, vsem = vr.next()
                            P.dma("sp", vsem, dict(out=v[:], in_=v_d[1][ts(s, SEQ), :].rearrange("(b p) n -> p b n", p=128)), writes=[V])
                            state["v"] = (v, V)
                        state["st"] = ost.next()
                    it["v"] = state["v"]
                    it["stg"] = state["st"]
                    q, Q = state[("q", m)]
                    k, K = state[("k", m)]
                    p = kb - 4 * qt
                    c0 = 128 * p if p > 0 else 0
                    it["c0"] = c0
                    zbk, ZB = zb.next()
                    t0 = qt * 512
                    P.op("pe", "matmul", dict(out=zbk[:, c0:512], lhsT=k[0:68, ts(kb, 128)], rhs=q[0:68, t0 + c0:t0 + 512], start=True, stop=(p < 0)),
                         reads=[Q, K], writes=[ZB])
                    if p >= 0:
                        P.op("pe", "matmul", dict(out=zbk[:, c0:c0 + 128], lhsT=ident, rhs=cst[:, C_CORR + h * 128:C_CORR + (h + 1) * 128], start=False, stop=True),
                             reads=[CST], writes=[ZB])
                    pp, PP, _ = pr.next()
                    P.op("act", "activation", dict(out=pp[:, 0:512 - c0], in_=zbk[:, c0:512], func=AF.Exp), reads=[ZB], writes=[PP])
                    it["pp"] = (pp, PP)

                def stage2(it):
                    s, h, qt, m, kb = it["s"], it["h"], it["qt"], it["m"], it["kb"]
                    c0 = it["c0"]
                    n = 512 - c0
                    pp, PP = it["pp"]
                    v, V = it["v"]
                    if it["first"]:
                        state["nd"] = ndr.next()
                    (nbk, NB), (dbk, DB) = state["nd"]
                    P.op("pe", "matmul", dict(out=nbk[:, c0:512], lhsT=v[:, kb, ts(h, 128)], rhs=pp[:, 0:n], start=it["first"], stop=it["last"], skip_group_check=True),
                         reads=[V, PP], writes=[NB])
                    P.op("pe", "matmul", dict(out=dbk[:, c0:512], lhsT=ones, rhs=pp[:, 0:n], start=it["first"], stop=it["last"], skip_group_check=True),
                         reads=[CST, PP], writes=[DB])
                    if it["last"]:
                        deferred.append([2, lambda it=it, nbk=nbk, NB=NB, dbk=dbk, DB=DB: combine(it, nbk, NB, dbk, DB)])

                def combine(it, nbk, NB, dbk, DB):
                    s, h, qt, m, kb = it["s"], it["h"], it["qt"], it["m"], it["kb"]
                    if True:
                        rd, RD, _ = rdr.next()
                        P.op("act", "activation", dict(out=rd[:], in_=dbk[:], func=AF.Ln), reads=[DB], writes=[RD])
                        P.op("act", "activation", dict(out=rd[:], in_=rd[:], func=AF.Exp, scale=-1.0), writes=[RD])
                        r, R, _ = rr.next()
                        P.op("dve", "tensor_tensor", dict(out=r[:], in0=nbk[:], in1=rd[:], op=ALU.mult), reads=[NB, RD], writes=[R])
                        state[("r", m)] = (r, R)
                        if m == 1:
                            r0, R0 = state[("r", 0)]
                            o, O, _ = o32.next()
                            P.op("dve", "scalar_tensor_tensor", dict(out=o[:], in0=r[:], scalar=lam_s[:, l:l + 1], in1=r0[:], op0=ALU.mult, op1=ALU.add),
                                 reads=[R, R0, LAM], writes=[O])
                            sq, SQ, _ = sqr.next()
                            P.op("act", "activation", dict(out=sq[:], in_=o[:], func=AF.Square), reads=[O], writes=[SQ])
                            deferred.append([4, lambda it=it, o=o, O=O, sq=sq, SQ=SQ: combine2(it, o, O, sq, SQ)])

                def combine2(it, o, O, sq, SQ):
                    s, h, qt, m, kb = it["s"], it["h"], it["qt"], it["m"], it["kb"]
                    if True:
                        if True:
                            ssb, SSB = zb.next()
                            P.op("pe", "matmul", dict(out=ssb[:], lhsT=ones, rhs=sq[:], start=True, stop=True), reads=[SQ, CST], writes=[SSB])
                            rs, RS, _ = rsr.next()
                            sc = 1.0 / (128.0 * (1 - li) ** 2)
                            P.op("act", "activation", dict(out=rs[:], in_=ssb[:], func=AF.Ln, bias=EPS / (1 - li) ** 2, scale=sc), reads=[SSB], writes=[RS])
                            P.op("act", "activation", dict(out=rs[:], in_=rs[:], func=AF.Exp, scale=-0.5), writes=[RS])
                            st, ST, ssem = it["stg"]
                            P.op("dve", "scalar_tensor_tensor", dict(out=st[:, qt, :], in0=o[:], scalar=prm[:, P_GSUB + l:P_GSUB + l + 1], in1=rs[:], op0=ALU.mult, op1=ALU.mult),
                                 reads=[O, RS, PRM], writes=[ST])
                            if qt == 3:
                                P.dma("pool", ssem, dict(out=o_d[512 + h * 128:512 + (h + 1) * 128, ts(s, SEQ)].rearrange("p (a n) -> p a n", a=4), in_=st[:]), reads=[ST])

                n_it = len(items)
                for i in range(n_it + 2):
                    if i < n_it:
                        stage1(items[i])
                    if 0 <= i - 2 < n_it:
                        stage2(items[i - 2])
                    for d_ in deferred:
                        d_[0] -= 1
                    while deferred and deferred[0][0] <= 0:
                        deferred.pop(0)[1]()
                while deferred:
                    deferred.pop(0)[1]()
                P.barrier()

        def phase_attn_c(l):
            with ExitStack() as pes:
                vr = Ring(P, pes, "cv", 2, [128, 16, 1024], BF16, dma=True)
                qr = Ring(P, pes, "cq", 2, [128, SEQ], BF16, dma=True)
                kr = Ring(P, pes, "ck", 4, [128, SEQ], BF16, dma=True)
                pr = Ring(P, pes, "cp", 4, [128, 512], BF16)
                rdr = Ring(P, pes, "crd", 2, [128, 512], F32)
                ost = Ring(P, pes, "cos", 2, [128, 4, 512], BF16, dma="pool")
                cb = sb("ccb", [128, 8, 640], BF16, pes)
                CB = Buf()
                mk = sb("cmk", [128, 640], F32, pes)
                MK = Buf()
                rbr = Ring(P, pes, "crb", 2, [128, 640], F32, dma=True)
                msem = P.new_sem(pes)
                P.dma("sp", msem, dict(out=mk[:], in_=maskc_d), writes=[MK])
                for h in range(8):
                    rb, RB, rsem = rbr.next()
                    P.dma("sp", rsem, dict(out=rb[:], in_=rbt_d[l][h]), writes=[RB])
                    P.op("dve", "tensor_tensor", dict(out=cb[:, h, :], in0=rb[:], in1=mk[:], op=ALU.add), reads=[RB, MK], writes=[CB])
                zb = BankRing(banks[0:4])
                ndr = BankRing(banks[4:8])
                items = []
                for s in range(2):
                    for j in range(4):
                        for qt in range(4):
                            for e in range(2):
                                ps_ = [p for p in range(-4, 4) if 4 * qt + p >= 0]
                                first_p = -1 if -1 in ps_ else 0
                                order = [first_p] + [p for p in ps_ if p != first_p]
                                for ii, p in enumerate(order):
                                    items.append(dict(s=s, j=j, qt=qt, e=e, p=p, first=(ii == 0), last=(ii == len(order) - 1)))
                state = {}
                deferred = []

                def stage1(it):
                    s, j, qt, e, p = it["s"], it["j"], it["qt"], it["e"], it["p"]
                    if it["first"] and qt == 0 and e == 0:
                        q, Q, qsem = qr.next()
                        P.dma("sp", qsem, dict(out=q[:], in_=qk_d[3072 + j * 128:3072 + (j + 1) * 128, ts(s, SEQ)]), writes=[Q])
                        state["q"] = (q, Q)
                        for e_ in range(2):
                            k, K, ksem = kr.next()
                            P.dma("sp", ksem, dict(out=k[:], in_=kpc_d[ts(2 * j + e_, 128), ts(s, SEQ)]), writes=[K])
                            state[("k", e_)] = (k, K)
                        if j == 0:
                            v, V, vsem = vr.next()
                            P.dma("sp", vsem, dict(out=v[:], in_=vpc_d[ts(s, SEQ), :].rearrange("(b p) n -> p b n", p=128)), writes=[V])
                            state["v"] = (v, V)
                        state["st"] = ost.next()
                    it["v"] = state["v"]
                    it["stg"] = state["st"]
                    q, Q = state["q"]
                    k, K = state[("k", e)]
                    h = 2 * j + e
                    kb = 4 * qt + p
                    qa = max(0, p)
                    qb_ = min(3, p + 4)
                    c0, c1 = 128 * qa, 128 * (qb_ + 1)
                    r0 = qa - p
                    it["c"] = (c0, c1)
                    it["kb"] = kb
                    zbk, ZB = zb.next()
                    t0 = qt * 512
                    P.op("pe", "matmul", dict(out=zbk[:, c0:c1], lhsT=k[:, ts(kb, 128)], rhs=q[:, t0 + c0:t0 + c1], start=True, stop=False),
                         reads=[Q, K], writes=[ZB])
                    P.op("pe", "matmul", dict(out=zbk[:, c0:c1], lhsT=ident, rhs=cb[:, h, r0 * 128:r0 * 128 + (c1 - c0)], start=False, stop=True),
                         reads=[CST, CB], writes=[ZB])
                    pp, PP, _ = pr.next()
                    P.op("act", "activation", dict(out=pp[:, 0:c1 - c0], in_=zbk[:, c0:c1], func=AF.Exp), reads=[ZB], writes=[PP])
                    it["pp"] = (pp, PP)

                def stage2(it):
                    s, j, qt, e, p = it["s"], it["j"], it["qt"], it["e"], it["p"]
                    c0, c1 = it["c"]
                    kb = it["kb"]
                    pp, PP = it["pp"]
                    v, V = it["v"]
                    if it["first"]:
                        state["nd"] = ndr.next()
                    nbk, NB = state["nd"]
                    P.op("pe", "matmul", dict(out=nbk[:, c0:c1], lhsT=v[:, kb, (2 * j + e) * 128:(2 * j + e + 1) * 128], rhs=pp[:, 0:c1 - c0],
                                              start=it["first"], stop=it["last"], skip_group_check=True), reads=[V, PP], writes=[NB])
                    if it["last"]:
                        deferred.append([2, lambda it=it, nbk=nbk, NB=NB: combine(it, nbk, NB)])

                def combine(it, nbk, NB):
                    s, j, qt, e = it["s"], it["j"], it["qt"], it["e"]
                    rd, RD, _ = rdr.next()
                    P.op("act", "activation", dict(out=rd[0:64, :], in_=nbk[64:128, :], func=AF.Ln), reads=[NB], writes=[RD])
                    P.op("act", "activation", dict(out=rd[0:64, :], in_=rd[0:64, :], func=AF.Exp, scale=-1.0), writes=[RD])
                    st, ST, ssem = it["stg"]
                    P.op("dve", "tensor_tensor", dict(out=st[e * 64:(e + 1) * 64, qt, :], in0=nbk[0:64, :], in1=rd[0:64, :], op=ALU.mult), reads=[NB, RD], writes=[ST])
                    if qt == 3 and e == 1:
                        P.dma("pool", ssem, dict(out=o_d[1024 + j * 128:1024 + (j + 1) * 128, ts(s, SEQ)].rearrange("p (a n) -> p a n", a=4), in_=st[:]), reads=[ST])

                n_it = len(items)
                wbv = wbr_d[l].rearrange("i (kc p) n -> p (i kc) n", p=128)
                wov = w_out_d[l].rearrange("(kc p) n -> p kc n", p=128)
                pf = [(wbv, c, c) for c in range(12)] + [(wov, c, 12 + c) for c in range(8)]
                for i in range(n_it + 2):
                    if i % 20 == 10 and pf:
                        src_, cs_, cd_ = pf.pop(0)
                        prefetch(src_, [(cs_, cd_)], engs=("dve",))
                    if i < n_it:
                        stage1(items[i])
                    if 0 <= i - 2 < n_it:
                        stage2(items[i - 2])
                    for d_ in deferred:
                        d_[0] -= 1
                    while deferred and deferred[0][0] <= 0:
                        deferred.pop(0)[1]()
                while deferred:
                    deferred.pop(0)[1]()
                while pf:
                    src_, cs_, cd_ = pf.pop(0)
                    prefetch(src_, [(cs_, cd_)], engs=("dve",))
                P.barrier()

        def phase_merge(l, src_d, dst_d, gcol, hout_d):
            with ExitStack() as pes:
                orr = Ring(P, pes, "mo", 1, [128, 12, 512], BF16, dma=True)
                epr = epilogue_rings(pes, "me")
                gr = Ring(P, pes, "mg", 1, [128, 24, 512], BF16, dma=True)
                xr = Ring(P, pes, "mx", 2, [128, 8, 512], F32, dma=True)
                xst = [P.new_sem(pes, "pool") for _ in range(2)]
                mr = Ring(P, pes, "mm", 2, [128, 512], F32)
                tr = Ring(P, pes, "mt", 2, [128, 512], F32)
                mbr = Ring(P, pes, "mb", 1, [128, 8, 512], BF16)
                br = BankRing(banks)
                xv = src_d.rearrange("(c p) t -> p c t", p=128)
                dv = dst_d.rearrange("(c p) t -> p c t", p=128)
                ov = o_d.rearrange("(c p) t -> p c t", p=128)
                gv = gt_d.rearrange("(c p) t -> p c t", p=128)
                pend = []
                for t in range(8):
                    o, O, osem = orr.next()
                    P.dma("sp", osem, dict(out=o[:], in_=ov[:, :, ts(t, 512)]), writes=[O])
                    g, G, gsem = gr.next()
                    P.dma("sp", gsem, dict(out=g[:], in_=gv[:, :, ts(t, 512)]), writes=[G])
                    x, X, xsem = xr.next()
                    P.dma("sp", xsem, dict(out=x[:], in_=xv[:, :, ts(t, 512)]), writes=[X])
                    mbf, MBF, _ = mbr.next()
                    sq_, SQ_, _ = epr[0].next()
                    for oc in range(8):
                        m, M, _ = mr.next()
                        for i in range(3):
                            bk, BK = br.next()
                            for kc in range(4):
                                P.op("pe", "matmul", dict(out=bk[:], lhsT=arena[:, i * 4 + kc, ts(oc, 128)], rhs=o[:, i * 4 + kc, :], start=(kc == 0), stop=(kc == 3)),
                                     reads=[ARENA[i * 4 + kc], O], writes=[BK])
                            if i == 0:
                                P.op("dve", "tensor_tensor", dict(out=m[:], in0=bk[:], in1=g[:, oc, :], op=ALU.mult), reads=[BK, G], writes=[M])
                            else:
                                tt, TT, _ = tr.next()
                                P.op("dve", "tensor_tensor", dict(out=tt[:], in0=bk[:], in1=g[:, i * 8 + oc, :], op=ALU.mult), reads=[BK, G], writes=[TT])
                                if i == 1:
                                    P.op("pool", "tensor_tensor", dict(out=m[:], in0=m[:], in1=tt[:], op=ALU.add), reads=[TT], writes=[M])
                                else:
                                    P.op("pool", "tensor_tensor", dict(out=mbf[:, oc, :], in0=m[:], in1=tt[:], op=ALU.add), reads=[TT, M], writes=[MBF])
                        if oc == 1 and pend:
                            pend.pop(0)()
                    for oc in range(8):
                        bk, BK = br.next()
                        for kc in range(8):
                            P.op("pe", "matmul", dict(out=bk[:], lhsT=arena[:, 12 + kc, ts(oc, 128)], rhs=mbf[:, kc, :], start=(kc == 0), stop=(kc == 7)),
                                 reads=[ARENA[12 + kc], MBF], writes=[BK])
                        P.op("dve", "tensor_tensor", dict(out=x[:, oc, :], in0=bk[:], in1=x[:, oc, :], op=ALU.add), reads=[BK], writes=[X])
                        P.op("act", "activation", dict(out=sq_[:, oc, :], in_=x[:, oc, :], func=AF.Square), reads=[X], writes=[SQ_])
                    P.dma("pool", xst[t % 2], dict(out=dv[:, :, ts(t, 512)], in_=x[:]), reads=[X])
                    pend.append(lambda t=t, x=x, X=X, sq_=sq_, SQ_=SQ_: norm_epi_b(epr, t, x, X, sq_, SQ_, gcol, hout_d, br))
                while pend:
                    pend.pop(0)()
                P.barrier()

        def phase_ffn_up(l, hT, HT, pre=None):
            with ExitStack() as pes:
                wgf = Ring(P, pes, "fgf", 2, [128, 8, 256], F32, dma=True)
                wuf = Ring(P, pes, "fuf", 2, [128, 8, 256], F32, dma=True)
                wgb = Ring(P, pes, "fgb", 2, [128, 8, 256], BF16)
                wub = Ring(P, pes, "fub", 2, [128, 8, 256], BF16)
                sgr = Ring(P, pes, "fsg", 3, [128, 512], F32)
                stg = Ring(P, pes, "fst", 3, [128, 4, 512], BF16, dma="pool")
                br = BankRing(banks)
                wv = w_gu_d[l].rearrange("(kc p) n -> p kc n", p=128)
                for sl in range(11):
                    wg, WG, gsem = wgf.next()
                    P.dma("sp", gsem, dict(out=wg[:], in_=wv[:, :, ts(sl, 256)]), writes=[WG])
                    wu, WU, usem = wuf.next()
                    P.dma("sp", usem, dict(out=wu[:], in_=wv[:, :, DFF + sl * 256:DFF + (sl + 1) * 256]), writes=[WU])
                    if sl == 0 and pre is not None:
                        pre()
                    gb, GB, _ = wgb.next()
                    P.op("dve", "tensor_copy", dict(out=gb[:], in_=wg[:]), reads=[WG], writes=[GB])
                    ub, UB, _ = wub.next()
                    P.op("dve", "tensor_copy", dict(out=ub[:], in_=wu[:]), reads=[WU], writes=[UB])
                    for j in range(2):
                        fc = sl * 2 + j
                        for half in range(2):
                            st, ST, ssem = stg.next()
                            for tq in range(4):
                                t = half * 4 + tq
                                gk, GK = br.next()
                                for kc in range(8):
                                    P.op("pe", "matmul", dict(out=gk[:], lhsT=gb[:, kc, ts(j, 128)], rhs=hT[:, kc, ts(t, 512)], start=(kc == 0), stop=(kc == 7)),
                                         reads=[HT[t], GB], writes=[GK])
                                uk, UK = br.next()
                                for kc in range(8):
                                    P.op("pe", "matmul", dict(out=uk[:], lhsT=ub[:, kc, ts(j, 128)], rhs=hT[:, kc, ts(t, 512)], start=(kc == 0), stop=(kc == 7)),
                                         reads=[HT[t], UB], writes=[UK])
                                sg, SG, _ = sgr.next()
                                P.op("act", "activation", dict(out=sg[:], in_=gk[:], func=AF.Silu), reads=[GK], writes=[SG])
                                P.op("dve", "tensor_tensor", dict(out=st[:, tq, :], in0=uk[:], in1=sg[:], op=ALU.mult), reads=[UK, SG], writes=[ST])
                            P.dma("pool", ssem, dict(out=ac_d[ts(fc, 128), ts(half, 2048)].rearrange("p (a n) -> p a n", a=4), in_=st[:]), reads=[ST])
                    prefetch(w_dn_d[l].rearrange("(kc p) n -> p kc n", p=128), [(2 * sl, 2 * sl), (2 * sl + 1, 2 * sl + 1)])
                P.barrier()

        def phase_ffn_down(l, src_d, dst_d, gcol, hout_d):
            with ExitStack() as pes:
                ar = Ring(P, pes, "da", 2, [128, 22, 512], BF16, dma=True)
                epr = epilogue_rings(pes, "de") if hout_d is not None else None
                xr = Ring(P, pes, "dx", 2, [128, 8, 512], F32, dma=True)
                xst = [P.new_sem(pes, "pool") for _ in range(2)]
                br = BankRing(banks)
                xv = src_d.rearrange("(c p) t -> p c t", p=128)
                dv = dst_d.rearrange("(c p) t -> p c t", p=128)
                av = ac_d.rearrange("(c p) t -> p c t", p=128)
                pend = []
                for t in range(8):
                    a, A, asem = ar.next()
                    P.dma("sp", asem, dict(out=a[:], in_=av[:, :, ts(t, 512)]), writes=[A])
                    x, X, xsem = xr.next()
                    P.dma("sp", xsem, dict(out=x[:], in_=xv[:, :, ts(t, 512)]), writes=[X])
                    if hout_d is not None:
                        sq_, SQ_, _ = epr[0].next()
                    for oc in range(8):
                        bk, BK = br.next()
                        for kc in range(22):
                            P.op("pe", "matmul", dict(out=bk[:], lhsT=arena[:, kc, ts(oc, 128)], rhs=a[:, kc, :], start=(kc == 0), stop=(kc == 21)),
                                 reads=[ARENA[kc], A], writes=[BK])
                        P.op("dve", "tensor_tensor", dict(out=x[:, oc, :], in0=bk[:], in1=x[:, oc, :], op=ALU.add), reads=[BK], writes=[X])
                        if hout_d is not None:
                            P.op("act", "activation", dict(out=sq_[:, oc, :], in_=x[:, oc, :], func=AF.Square), reads=[X], writes=[SQ_])
                        if oc == 1 and pend:
                            pend.pop(0)()
                    P.dma("pool", xst[t % 2], dict(out=dv[:, :, ts(t, 512)], in_=x[:]), reads=[X])
                    if hout_d is not None:
                        pend.append(lambda t=t, x=x, X=X, sq_=sq_, SQ_=SQ_: norm_epi_b(epr, t, x, X, sq_, SQ_, gcol, hout_d, br))
                while pend:
                    pend.pop(0)()
                P.barrier()

        phases = 0

        def done():
            nonlocal phases
            phases += 1
            return stop_after is not None and phases >= stop_after

        stop = False
        for l in range(n_layers):
            src = xT_d if l == 0 else xs_d
            with ExitStack() as les:
                hT = sb("hT", [128, 8, T], BF16, les)
                HT = [Buf("hT%d" % i) for i in range(8)]
                if l == 0:
                    zt = sb("zt", [128, 4, 1024], BF16, les)
                    ot = sb("ot", [128, 4, 1024], BF16, les)
                    ZT = Buf()
                    OT = Buf()
                    P.op("dve", "memset", dict(ap=zt[:], constant=0.0), writes=[ZT])
                    P.op("dve", "memset", dict(ap=ot[:], constant=1.0), writes=[OT])
                    fsem = P.new_sem()
                    for a_ in range(8):
                        P.dma("pool", fsem, dict(out=kpa_d[ts(a_, 128), :].rearrange("p (a n) -> p a n", a=4), in_=zt[:]), reads=[ZT])
                        P.dma("pool", fsem, dict(out=kpc_d[ts(a_, 128), :].rearrange("p (a n) -> p a n", a=4), in_=zt[:]), reads=[ZT])
                        P.dma("pool", fsem, dict(out=vpa_d[ts(a_, 512), :].rearrange("(a p) n -> p a n", p=128), in_=zt[:]), reads=[ZT])
                        P.dma("pool", fsem, dict(out=vpc_d[ts(a_, 512), :].rearrange("(a p) n -> p a n", p=128), in_=ot[:]), reads=[OT])
                    phase_norm(src, P_GMIX + l * 8, hT, HT)
                    phase_proj(l, hT, HT)
                else:
                    phase_proj(l, hT, HT, pre=lambda: load_h(h1_d, hT, HT, les))
            if done():
                break
            phase_attn_a(l)
            if done():
                break
            phase_attn_b(l)
            if done():
                break
            phase_attn_c(l)
            if done():
                break
            phase_merge(l, src, xs_d, P_GFFN + l * 8, h2_d)
            if done():
                break
            with ExitStack() as les:
                hT = sb("hT2", [128, 8, T], BF16, les)
                HT = [Buf("hT2%d" % i) for i in range(8)]
                phase_ffn_up(l, hT, HT, pre=lambda: load_h(h2_d, hT, HT, les))
            if done():
                break
            last = (l == n_layers - 1)
            phase_ffn_down(l, xs_d, yT_d if last else xs_d, P_GMIX + (l + 1) * 8 if not last else 0, None if last else h1_d)
            if done():
                break
        P.barrier()
        build.last_ninst = dict(P.ninst)
    return nc


def _bf(a):
    return np.asarray(a, dtype=np.float32).astype(ml_dtypes.bfloat16)


def make_consts():
    i = np.arange(128)[:, None]
    j = np.arange(128)[None, :]
    cst = np.zeros((128, NCST), np.float32)
    cst[:, C_ID:C_ID + 128] = (i == j)
    cst[:, C_ONE:C_ONE + 128] = 1.0
    cst[:, C_BLK:C_BLK + 128] = ((i // 64) == (j // 64))
    cst[:, C_NLU:C_NLU + 128] = -1.0 * (i >= j)
    cst[:, C_NMA:C_NMA + 128] = np.where(i < j, 0.0, NEG)
    cst[:, C_NEG1:C_NEG1 + 128] = -1.0
    for h in range(4):
        sl = SLOPES[h]
        ok = (i // 64) <= (j // 64)
        corr = np.where(ok, np.where(i > j, -2.0 * sl * (i - j), 0.0), NEG)
        cst[:, C_CORR + h * 128:C_CORR + (h + 1) * 128] = corr
    cst[:, C_E0:C_E0 + 128] = (i == 0)
    cst[:, C_HM0:C_HM0 + 64] = 1.0
    cst[:, C_HM1 + 64:C_HM1 + 128] = 1.0
    pos = np.arange(SEQ)
    augq = np.zeros((4, 4, SEQ), np.float32)
    augk = np.zeros((4, 4, SEQ), np.float32)
    for h in range(4):
        sl = SLOPES[h]
        augq[h, 0] = -sl * 64.0 * (pos // 64)
        augq[h, 1] = -sl * (pos % 64)
        augq[h, 2] = 1.0
        augq[h, 3] = 1.0
        augk[h, 0] = 1.0
        augk[h, 1] = 1.0
        augk[h, 2] = sl * 64.0 * (pos // 64)
        augk[h, 3] = sl * (pos % 64)
    maskc = np.zeros((128, 640), np.float32)
    a_ = (np.arange(128)[None, :] // 64)
    b_ = (np.arange(128)[:, None] // 64)
    maskc[:, 0:128] = np.where(a_ >= b_, 0.0, NEG)
    maskc[:, 512:640] = np.where(a_ <= b_, 0.0, NEG)
    return _bf(cst), _bf(augq), _bf(augk), maskc


def make_rbt(rel_bias):
    i = np.arange(128)[:, None]
    c = np.arange(640)[None, :]
    idx = np.clip(c - i, -128, 128) + 128
    return np.ascontiguousarray(rel_bias[:, :, idx]).astype(np.float32)


def make_prm(norm_mix_g, norm_ffn_g, b_gate, qk_g_diff, qk_g_ch, subln_g, lambda_qk):
    prm = np.zeros((128, NPRM), np.float32)
    p = np.arange(128)
    for l in range(DEPTH):
        prm[:, P_GMIX + l * 8:P_GMIX + (l + 1) * 8] = norm_mix_g[l].reshape(8, 128).T
        prm[:, P_GFFN + l * 8:P_GFFN + (l + 1) * 8] = norm_ffn_g[l].reshape(8, 128).T
        prm[:, P_BG + l * 24:P_BG + (l + 1) * 24] = b_gate[l].reshape(24, 128).T
        for k in range(2):
            prm[:, P_GQD + 2 * l + k] = qk_g_diff[l, k][p % 64]
            prm[:, P_GQC + 2 * l + k] = qk_g_ch[l, k][p % 64]
        prm[:, P_GSUB + l] = subln_g[l]
        prm[:, P_LQK + l * 256:P_LQK + (l + 1) * 256] = lambda_qk[l].reshape(1, 256)
    return prm


_NC_CACHE = {}


def kernel(x, norm_mix_g, w_in, b_gate, qk_g_diff, lambda_qk, subln_g, qk_g_ch, rel_bias,
           w_branch_sb, w_branch_diff, w_branch_ch, w_out, norm_ffn_g, w_gu, w_down):
    f = lambda a: np.ascontiguousarray(np.asarray(a, dtype=np.float32))
    x = f(x)
    cst, augq, augk, maskc = make_consts()
    shared = {
        "w_in": f(w_in),
        "w_br": np.ascontiguousarray(np.stack([f(w_branch_sb), f(w_branch_diff), f(w_branch_ch)], axis=1)),
        "w_out": f(w_out), "w_gu": f(w_gu), "w_down": f(w_down),
        "rbt": make_rbt(f(rel_bias)),
        "prm": make_prm(f(norm_mix_g), f(norm_ffn_g), f(b_gate), f(qk_g_diff), f(qk_g_ch), f(subln_g), f(lambda_qk)),
        "cst": cst, "augq": augq, "augk": augk, "maskc": maskc,
    }
    in_maps = []
    for c in range(NCORES):
        m = dict(shared)
        m["xT"] = np.ascontiguousarray(x[2 * c:2 * c + 2].reshape(T, D).T)
        in_maps.append(m)
    if "nc" not in _NC_CACHE:
        _NC_CACHE["nc"] = build()
    res = run_bass_kernel_spmd(_NC_CACHE["nc"], in_maps, core_ids=list(range(NCORES)))
    out = np.empty((16, SEQ, D), np.float32)
    for c in range(NCORES):
        out[2 * c:2 * c + 2] = np.asarray(res.results[c]["yT"]).T.reshape(2, SEQ, D)
    return out
```

```python
import math
import numpy as np
import ml_dtypes
from contextlib import ExitStack
import concourse.bass as bass
import concourse.mybir as mybir
from concourse.bass_utils import run_bass_kernel_spmd

F32 = mybir.dt.float32
BF16 = mybir.dt.bfloat16
AF = mybir.ActivationFunctionType
ALU = mybir.AluOpType

NCORES = 8
D = 1024
SEQ = 2048
T = 4096
DEPTH = 4
DFF = 2816
NIN = 7680
EPS = 1e-6
NEG = -30000.0
SLOPES = [2.0 ** (-8.0 * (i + 1) / 4) for i in range(4)]

C_ID, C_ONE, C_BLK, C_NLU, C_NMA, C_NEG1, C_ZERO, C_CORR = 0, 128, 256, 384, 512, 640, 768, 896
C_E0, C_HM0, C_HM1 = 1408, 1536, 1664
NCST = 1792
P_GMIX, P_GFFN, P_BG, P_GQD, P_GQC, P_GSUB, P_LQK = 0, 32, 64, 160, 168, 176, 180
NPRM = 180 + 1024


def ts(i, n):
    return slice(i * n, (i + 1) * n)


class Buf:
    __slots__ = ("name", "w", "r")

    def __init__(self, name=""):
        self.name = name
        self.w = None
        self.r = {}


class Prog:
    ENG = ("pe", "act", "dve", "pool", "sp")

    def __init__(self, nc, es, block):
        self.nc = nc
        self.es = es
        self.block = block
        self.streams = {e: [] for e in self.ENG}
        self.sems = {}
        self.cnt = {}
        self.known = {e: {} for e in self.ENG}
        for e in self.ENG:
            self.sems[e] = es.enter_context(nc.semaphore("s_" + e))
            self.cnt[e] = 0
        self.nsem = 0
        self.pools = {}
        self.uid = 0
        self.ninst = {e: 0 for e in self.ENG}

    def new_sem(self, pes=None, kind="sp"):
        if pes is not None:
            pool = self.pools.setdefault(kind, [])
            if pool:
                name = pool.pop()
            else:
                name = self.new_sem()
            pes.callback(pool.append, name)
            return name
        self.nsem += 1
        name = "d%d" % self.nsem
        self.sems[name] = self.es.enter_context(self.nc.semaphore(name))
        self.cnt[name] = 0
        return name

    def _deps(self, eng, reads, writes):
        waits = {}
        for b in reads:
            if b.w is not None and waits.get(b.w[0], 0) < b.w[1]:
                waits[b.w[0]] = b.w[1]
        for b in writes:
            if b.w is not None and waits.get(b.w[0], 0) < b.w[1]:
                waits[b.w[0]] = b.w[1]
            for k, v in b.r.items():
                if waits.get(k, 0) < v:
                    waits[k] = v
        self._emit_waits(eng, waits)

    def _emit_waits(self, eng, waits, skip_pe_self=True):
        st = self.streams[eng]
        kn = self.known[eng]
        for k, v in waits.items():
            if skip_pe_self and k == "pe" and eng == "pe":
                continue
            if kn.get(k, 0) >= v:
                continue
            kn[k] = v
            st.append(("wait", self.sems[k], v))
            self.ninst[eng] += 1

    def op(self, eng, name, kw, reads=(), writes=()):
        self._deps(eng, reads, writes)
        self.cnt[eng] += 1
        c = self.cnt[eng]
        self.streams[eng].append(("op", name, kw, self.sems[eng], 1))
        self.ninst[eng] += 1
        for b in reads:
            if b.r.get(eng, 0) < c:
                b.r[eng] = c
        for b in writes:
            b.w = (eng, c)
            b.r = {}

    def dma(self, q, semname, kw, reads=(), writes=()):
        self._deps(q, reads, writes)
        self.cnt[semname] += 16
        c = self.cnt[semname]
        self.streams[q].append(("op", "dma_start", kw, self.sems[semname], 16))
        self.ninst[q] += 1
        for b in reads:
            if b.r.get(semname, 0) < c:
                b.r[semname] = c
        for b in writes:
            b.w = (semname, c)
            b.r = {}

    def barrier(self):
        waits = {k: v for k, v in self.cnt.items() if v > 0}
        for e in self.ENG:
            self._emit_waits(e, dict(waits), skip_pe_self=False)
        self.flush()

    def flush(self):
        for en, meth in (("pe", "tensor"), ("act", "scalar"), ("dve", "vector"), ("pool", "gpsimd"), ("sp", "sync")):
            items = self.streams[en]
            if not items:
                continue
            self.streams[en] = []

            def body(e, items=items):
                for it in items:
                    if it[0] == "wait":
                        e.wait_ge(it[1], it[2])
                    else:
                        getattr(e, it[1])(**it[2]).then_inc(it[3], it[4])

            getattr(self.block, meth)(body)


class Ring:
    def __init__(self, P, es, name, n, shape, dtype, dma=False):
        self.slots = []
        for i in range(n):
            P.uid += 1
            t = es.enter_context(P.nc.sbuf_tensor("%s%d_%d" % (name, i, P.uid), shape, dtype))
            self.slots.append((t, Buf(name + str(i)), P.new_sem(es, dma if isinstance(dma, str) else "sp") if dma else None))
        self.i = 0
        self.n = n

    def next(self):
        s = self.slots[self.i % self.n]
        self.i += 1
        return s


class BankRing:
    def __init__(self, banks):
        self.b = banks
        self.i = 0

    def next(self):
        s = self.b[self.i % len(self.b)]
        self.i += 1
        return s


def lam_init(l):
    return 0.8 - 0.6 * math.exp(-0.3 * l)


def build(n_layers=DEPTH, dbg=False, stop_after=None):
    nc = bass.Bass("TRN2", target_bir_lowering=False)
    dt = lambda name, shape, dtype, kind: nc.dram_tensor(name, shape, dtype, kind=kind).ap()
    skind = "ExternalOutput" if dbg else "Internal"
    xT_d = dt("xT", [D, T], F32, "ExternalInput")
    w_in_d = dt("w_in", [DEPTH, D, NIN], F32, "ExternalInput")
    wbr_d = dt("w_br", [DEPTH, 3, 512, D], F32, "ExternalInput")
    w_out_d = dt("w_out", [DEPTH, D, D], F32, "ExternalInput")
    w_gu_d = dt("w_gu", [DEPTH, D, 2 * DFF], F32, "ExternalInput")
    w_dn_d = dt("w_down", [DEPTH, DFF, D], F32, "ExternalInput")
    rbt_d = dt("rbt", [DEPTH, 8, 128, 640], F32, "ExternalInput")
    prm_d = dt("prm", [128, NPRM], F32, "ExternalInput")
    cst_d = dt("cst", [128, NCST], BF16, "ExternalInput")
    augq_d = dt("augq", [4, 4, SEQ], BF16, "ExternalInput")
    augk_d = dt("augk", [4, 4, SEQ], BF16, "ExternalInput")
    maskc_d = dt("maskc", [128, 640], F32, "ExternalInput")
    yT_d = dt("yT", [D, T], F32, "ExternalOutput")
    xs_d = dt("xs", [D, T], F32, skind)
    qk_d = dt("qk", [4608, T], BF16, skind)
    v_d = dt("vs", [3, T, 512], BF16, skind)
    gt_d = dt("gts", [3072, T], BF16, skind)
    o_d = dt("os", [1536, T], BF16, skind)
    ac_d = dt("acs", [DFF, T], BF16, skind)
    kpa_d = dt("kpa", [1024, T], BF16, "Internal")
    kpc_d = dt("kpc", [1024, T], BF16, "Internal")
    vpa_d = dt("vpa", [T, 1024], BF16, "Internal")
    vpc_d = dt("vpc", [T, 1024], BF16, "Internal")
    h1_d = dt("h1s", [D, T], BF16, "Internal")
    h2_d = dt("h2s", [D, T], BF16, "Internal")

    with ExitStack() as es:
        block = es.enter_context(nc.Block())
        P = Prog(nc, es, block)
        def sb(n, s, d, e=es):
            P.uid += 1
            return e.enter_context(nc.sbuf_tensor("%s_%d" % (n, P.uid), s, d))
        banks = []
        for i in range(8):
            banks.append((es.enter_context(nc.psum_tensor("bank%d" % i, [128, 512], F32)), Buf("bank%d" % i)))
        cst = sb("cst_s", [128, NCST], BF16)
        prm = sb("prm_s", [128, NPRM], F32)
        lam_s = sb("lam_s", [128, 16], F32)
        arena = sb("arena", [128, 22, D], BF16)
        ARENA = [Buf("ar%d" % i) for i in range(22)]
        arst = Ring(P, es, "arst", 2, [128, 1, D], F32, dma=True)
        arcnt = [0]

        def prefetch(src3, chunks, engs=("dve", "act")):
            for (c_src, c_dst) in chunks:
                st, ST, sem = arst.next()
                P.dma("sp", sem, dict(out=st[:, 0, :], in_=src3[:, c_src, :]), writes=[ST])
                arcnt[0] += 1
                en = engs[arcnt[0] % len(engs)]
                if en == "dve":
                    P.op("dve", "tensor_copy", dict(out=arena[:, c_dst, :], in_=st[:, 0, :]), reads=[ST], writes=[ARENA[c_dst]])
                else:
                    P.op("act", "activation", dict(out=arena[:, c_dst, :], in_=st[:, 0, :], func=AF.Copy), reads=[ST], writes=[ARENA[c_dst]])
        CST = Buf("cst")
        PRM = Buf("prm")
        LAM = Buf("lam")
        s0 = P.new_sem()
        P.dma("sp", s0, dict(out=cst[:], in_=cst_d), writes=[CST])
        P.dma("sp", s0, dict(out=prm[:], in_=prm_d), writes=[PRM])
        ident = cst[:, C_ID:C_ID + 128]
        ones = cst[:, C_ONE:C_ONE + 128]
        blk64 = cst[:, C_BLK:C_BLK + 128]
        nlu = cst[:, C_NLU:C_NLU + 128]
        nma = cst[:, C_NMA:C_NMA + 128]
        neg1 = cst[:, C_NEG1:C_NEG1 + 128]
        zer = cst[:, C_ZERO:C_ZERO + 128]
        e0m = cst[:, C_E0:C_E0 + 128]
        hm = [cst[:, C_HM0:C_HM0 + 128], cst[:, C_HM1:C_HM1 + 128]]

        with ExitStack() as pes:
            tmp = sb("lq_tmp", [128, 64], F32, pes)
            sacc = sb("lq_acc", [128, 8], F32, pes)
            TMP = Buf()
            SACC = Buf()
            for l in range(DEPTH):
                for k in range(2):
                    a = prm[:, P_LQK + l * 256 + (2 * k) * 64: P_LQK + l * 256 + (2 * k + 1) * 64]
                    b = prm[:, P_LQK + l * 256 + (2 * k + 1) * 64: P_LQK + l * 256 + (2 * k + 2) * 64]
                    P.op("dve", "scalar_tensor_tensor", dict(out=tmp[:], in0=a, scalar=1.0, in1=b, op0=ALU.mult, op1=ALU.mult,
                                                             accum_out=sacc[:, 2 * l + k:2 * l + k + 1]), reads=[PRM], writes=[TMP, SACC])
            P.op("act", "activation", dict(out=sacc[:], in_=sacc[:], func=AF.Exp), writes=[SACC])
            for l in range(DEPTH):
                P.op("dve", "scalar_tensor_tensor", dict(out=lam_s[:, l:l + 1], in0=sacc[:, 2 * l + 1:2 * l + 2], scalar=-lam_init(l),
                                                         in1=sacc[:, 2 * l:2 * l + 1], op0=ALU.add, op1=ALU.subtract), reads=[SACC], writes=[LAM])
            P.barrier()

        def load_cast(pes, name, dst, DSTL, src3, nchunk, width, ring=None):
            per = max(1, 2048 // width)
            if ring is None:
                ring = Ring(P, pes, name, 2, [128, per, width], F32, dma=True)
            c = 0
            k = 0
            while c < nchunk:
                n = min(per, nchunk - c)
                st, ST, sem = ring.next()
                P.dma("sp", sem, dict(out=st[:, 0:n, :], in_=src3[:, c:c + n, :]), writes=[ST])
                for i in range(n):
                    k += 1
                    if k % 2:
                        P.op("dve", "tensor_copy", dict(out=dst[:, c + i, :], in_=st[:, i, :]), reads=[ST], writes=[DSTL[c + i]])
                    else:
                        P.op("act", "activation", dict(out=dst[:, c + i, :], in_=st[:, i, :], func=AF.Copy), reads=[ST], writes=[DSTL[c + i]])
                c += n

        def norm_epi_b(pes_rings, t, x, X, sq, SQ, gcol, hout_d, br):
            sqr, rsr, hr = pes_rings
            bk, BK = br.next()
            for c in range(8):
                P.op("pe", "matmul", dict(out=bk[:], lhsT=ones, rhs=sq[:, c, :], start=(c == 0), stop=(c == 7)), reads=[SQ, CST], writes=[BK])
            r, R, _ = rsr.next()
            P.op("act", "activation", dict(out=r[:], in_=bk[:], func=AF.Ln, bias=EPS, scale=1.0 / D), reads=[BK], writes=[R])
            P.op("act", "activation", dict(out=r[:], in_=r[:], func=AF.Exp, scale=-0.5), writes=[R])
            hh, HH, hsem = hr.next()
            for c in range(8):
                P.op("dve", "scalar_tensor_tensor", dict(out=hh[:, c, :], in0=x[:, c, :], scalar=prm[:, gcol + c:gcol + c + 1],
                                                         in1=r[:], op0=ALU.mult, op1=ALU.mult), reads=[X, R, PRM], writes=[HH])
            P.dma("pool", hsem, dict(out=hout_d.rearrange("(c p) t -> p c t", p=128)[:, :, ts(t, 512)], in_=hh[:]), reads=[HH])

        def epilogue_rings(pes, nm):
            return (Ring(P, pes, nm + "sq", 2, [128, 8, 512], BF16), Ring(P, pes, nm + "rs", 2, [128, 512], F32),
                    Ring(P, pes, nm + "hh", 1, [128, 8, 512], BF16, dma="pool"))

        def load_h(h_d, hT, HT, les):
            hv = h_d.rearrange("(c p) t -> p c t", p=128)
            for t in range(8):
                sem = P.new_sem(les)
                P.dma("sp", sem, dict(out=hT[:, :, ts(t, 512)], in_=hv[:, :, ts(t, 512)]), writes=[HT[t]])

        def phase_norm(src_d, gcol, hT, HT):
            with ExitStack() as pes:
                xr = Ring(P, pes, "nx", 2, [128, 8, 512], F32, dma=True)
                sq = Ring(P, pes, "nsq", 2, [128, 8, 512], BF16)
                rs = Ring(P, pes, "nrs", 2, [128, 512], F32)
                br = BankRing(banks[0:2])
                xv = src_d.rearrange("(c p) t -> p c t", p=128)
                for t in range(8):
                    x, X, xsem = xr.next()
                    P.dma("sp", xsem, dict(out=x[:], in_=xv[:, :, ts(t, 512)]), writes=[X])
                    s, S, _ = sq.next()
                    P.op("act", "activation", dict(out=s[:], in_=x[:], func=AF.Square), reads=[X], writes=[S])
                    bk, BK = br.next()
                    for c in range(8):
                        P.op("pe", "matmul", dict(out=bk[:], lhsT=ones, rhs=s[:, c, :], start=(c == 0), stop=(c == 7)), reads=[S, CST], writes=[BK])
                    r, R, _ = rs.next()
                    P.op("act", "activation", dict(out=r[:], in_=bk[:], func=AF.Ln, bias=EPS, scale=1.0 / D), reads=[BK], writes=[R])
                    P.op("act", "activation", dict(out=r[:], in_=r[:], func=AF.Exp, scale=-0.5), writes=[R])
                    for c in range(8):
                        P.op("dve", "scalar_tensor_tensor", dict(out=hT[:, c, ts(t, 512)], in0=x[:, c, :], scalar=prm[:, gcol + c:gcol + c + 1],
                                                                 in1=r[:], op0=ALU.mult, op1=ALU.mult), reads=[X, R, PRM], writes=[HT[t]])
                P.barrier()

        def phase_proj(l, hT, HT, pre=None):
            with ExitStack() as pes:
                wfr = Ring(P, pes, "pwf", 2, [128, 8, 512], F32, dma=True)
                wbr = Ring(P, pes, "pwb", 2, [128, 8, 512], BF16)
                stg = Ring(P, pes, "pst", 3, [128, 4, 512], BF16, dma="pool")
                sqr = Ring(P, pes, "psq", 3, [128, 512], BF16)
                rsr = Ring(P, pes, "prs", 2, [128, 512], F32)
                mb = BankRing(banks[0:5])
                sb2 = BankRing(banks[5:8])
                wv = w_in_d[l].rearrange("(kc p) n -> p kc n", p=128)
                evi = 0
                pending = []
                for sl in range(15):
                    wf, WF, wsem = wfr.next()
                    P.dma("sp", wsem, dict(out=wf[:], in_=wv[:, :, ts(sl, 512)]), writes=[WF])
                    if sl == 0 and pre is not None:
                        pre()
                    wb, WB, _ = wbr.next()
                    P.op("dve", "tensor_copy", dict(out=wb[:], in_=wf[:]), reads=[WF], writes=[WB])
                    if sl in (2, 5, 8):
                        brn = (sl - 2) // 3
                        for tb4 in range(8):
                            st, ST, ssem = stg.next()
                            for j in range(4):
                                tb = tb4 * 4 + j
                                bk, BK = mb.next()
                                for kc in range(8):
                                    P.op("pe", "matmul", dict(out=bk[:], lhsT=hT[:, kc, ts(tb, 128)], rhs=wb[:, kc, :], start=(kc == 0), stop=(kc == 7)),
                                         reads=[HT[tb // 4], WB], writes=[BK])
                                evi += 1
                                if evi % 2:
                                    P.op("act", "activation", dict(out=st[:, j, :], in_=bk[:], func=AF.Copy), reads=[BK], writes=[ST])
                                else:
                                    P.op("dve", "tensor_copy", dict(out=st[:, j, :], in_=bk[:]), reads=[BK], writes=[ST])
                            if brn == 1:
                                P.dma("pool", ssem, dict(out=v_d[brn][ts(tb4, 512), :].rearrange("(j p) n -> p j n", p=128), in_=st[:]), reads=[ST])
                            elif brn == 0:
                                for j in range(4):
                                    rows = vpa_d[ts(tb4 * 4 + j, 128), :].rearrange("p (hp c) -> p hp c", c=256)
                                    srcv = st[:, j, :].rearrange("p (hp e d) -> p hp e d", e=2, d=64)
                                    for e_ in range(2):
                                        P.dma("pool", ssem, dict(out=rows[:, :, e_ * 192:e_ * 192 + 64], in_=srcv[:, :, e_, :]), reads=[ST])
                            else:
                                for j in range(4):
                                    rows = vpc_d[ts(tb4 * 4 + j, 128), :].rearrange("p (h c) -> p h c", c=128)
                                    P.dma("pool", ssem, dict(out=rows[:, :, 0:64], in_=st[:, j, :].rearrange("p (h d) -> p h d", d=64)), reads=[ST])
                        continue
                    for j in range(4):
                        oc = sl * 4 + j
                        for half in range(2):
                            st, ST, ssem = stg.next()
                            if oc < 36:
                                dd = qk_d[ts(oc, 128), ts(half, 2048)]
                            else:
                                dd = gt_d[ts(oc - 36, 128), ts(half, 2048)]
                            for tq in range(4):
                                t = half * 4 + tq
                                bk, BK = mb.next()
                                for kc in range(8):
                                    P.op("pe", "matmul", dict(out=bk[:], lhsT=wb[:, kc, ts(j, 128)], rhs=hT[:, kc, ts(t, 512)], start=(kc == 0), stop=(kc == 7)),
                                         reads=[HT[t], WB], writes=[BK])
                                if pending:
                                    pending.pop(0)()
                                dst = st[:, tq, :]
                                evi += 1

                                def store(st=st, ST=ST, ssem=ssem, dd=dd, sl=sl, j=j, half=half):
                                    if sl in (1, 7):
                                        kp = kpa_d if sl == 1 else kpc_d
                                        for e_ in range(2):
                                            r0_ = (2 * j + e_) * 128 + e_ * 64
                                            P.dma("pool", ssem, dict(out=kp[r0_:r0_ + 64, ts(half, 2048)].rearrange("p (j n) -> p j n", j=4),
                                                                     in_=st[e_ * 64:(e_ + 1) * 64, :, :]), reads=[ST])
                                    else:
                                        P.dma("pool", ssem, dict(out=dd.rearrange("p (j n) -> p j n", j=4), in_=st[:]), reads=[ST])

                                if sl == 0:
                                    if evi % 2:
                                        P.op("act", "activation", dict(out=dst, in_=bk[:], func=AF.Copy, scale=0.125), reads=[BK], writes=[ST])
                                    else:
                                        P.op("dve", "tensor_scalar", dict(out=dst, in0=bk[:], scalar1=0.125, scalar2=None, op0=ALU.mult), reads=[BK], writes=[ST])
                                elif sl == 1:
                                    if evi % 2:
                                        P.op("act", "activation", dict(out=dst, in_=bk[:], func=AF.Copy), reads=[BK], writes=[ST])
                                    else:
                                        P.op("dve", "tensor_copy", dict(out=dst, in_=bk[:]), reads=[BK], writes=[ST])
                                elif sl in (3, 4, 6, 7):
                                    isq = sl in (3, 6)
                                    gbase = (P_GQD if sl in (3, 4) else P_GQC) + 2 * l + (0 if isq else 1)
                                    s_, S_, _ = sqr.next()
                                    P.op("act", "activation", dict(out=s_[:], in_=bk[:], func=AF.Square), reads=[BK], writes=[S_])

                                    def tail(s_=s_, S_=S_, bk=bk, BK=BK, dst=dst, ST=ST, isq=isq, gbase=gbase, last=(tq == 3), store=store):
                                        b2, B2 = sb2.next()
                                        P.op("pe", "matmul", dict(out=b2[:], lhsT=blk64, rhs=s_[:], start=True, stop=True), reads=[S_, CST], writes=[B2])
                                        r, R, _ = rsr.next()
                                        if isq:
                                            P.op("act", "activation", dict(out=r[:], in_=b2[:], func=AF.Ln, bias=64.0 * EPS, scale=1.0), reads=[B2], writes=[R])
                                        else:
                                            P.op("act", "activation", dict(out=r[:], in_=b2[:], func=AF.Ln, bias=EPS, scale=1.0 / 64), reads=[B2], writes=[R])
                                        P.op("act", "activation", dict(out=r[:], in_=r[:], func=AF.Exp, scale=-0.5), writes=[R])
                                        P.op("dve", "scalar_tensor_tensor", dict(out=dst, in0=bk[:], scalar=prm[:, gbase:gbase + 1], in1=r[:], op0=ALU.mult, op1=ALU.mult),
                                             reads=[BK, R, PRM], writes=[ST])
                                        if last:
                                            store()

                                    pending.append(tail)
                                    continue
                                else:
                                    gi = oc - 36
                                    bcol = P_BG + l * 24 + gi
                                    P.op("act", "activation", dict(out=dst, in_=bk[:], func=AF.Sigmoid, bias=prm[:, bcol:bcol + 1], scale=1.0), reads=[BK, PRM], writes=[ST])
                                if tq == 3:
                                    store()
                    while pending:
                        pending.pop(0)()
                P.barrier()

        def phase_attn_a(l):
            with ExitStack() as pes:
                vr = Ring(P, pes, "av", 2, [128, 16, 1024], BF16, dma=True)
                qr = Ring(P, pes, "aq", 2, [128, SEQ], BF16, dma=True)
                kr = Ring(P, pes, "ak", 4, [128, SEQ], BF16, dma=True)
                er = Ring(P, pes, "ae", 3, [128, 512], F32)
                spr = Ring(P, pes, "asp", 3, [128, 512], BF16)
                pr = Ring(P, pes, "ap", 3, [128, 512], BF16)
                cr = Ring(P, pes, "ac", 3, [128, 512], BF16)
                ost = Ring(P, pes, "ao", 2, [128, 4, 512], BF16, dma="pool")
                zb = BankRing(banks[0:4])
                cbk, CBK = banks[4]
                obr = BankRing(banks[5:7])
                items = []
                for s in range(2):
                    for j in range(4):
                        for qt in range(4):
                            for e in range(2):
                                kbs = list(range(4 * qt + 3, -1, -1))
                                for ii, kb in enumerate(kbs):
                                    items.append(dict(s=s, j=j, qt=qt, e=e, kb=kb, first=(ii == 0), last=(ii == len(kbs) - 1)))
                state = dict(v=None, q=None, ob=None, st=None)

                def stage1(it):
                    s, j, qt, e, kb = it["s"], it["j"], it["qt"], it["e"], it["kb"]
                    if it["first"] and e == 0 and qt == 0:
                        q, Q, qsem = qr.next()
                        P.dma("sp", qsem, dict(out=q[:], in_=qk_d[ts(j, 128), ts(s, SEQ)]), writes=[Q])
                        state["q"] = (q, Q)
                        for e_ in range(2):
                            k, K, ksem = kr.next()
                            P.dma("sp", ksem, dict(out=k[:], in_=kpa_d[ts(2 * j + e_, 128), ts(s, SEQ)]), writes=[K])
                            state[("k", e_)] = (k, K)
                        if j == 0:
                            v, V, vsem = vr.next()
                            P.dma("sp", vsem, dict(out=v[:], in_=vpa_d[ts(s, SEQ), :].rearrange("(b p) n -> p b n", p=128)), writes=[V])
                            state["v"] = (v, V)
                        state["st"] = ost.next()
                    it["v"] = state["v"]
                    it["stg"] = state["st"]
                    q, Q = state["q"]
                    k, K = state[("k", e)]
                    it["qk"] = (q, Q, k, K)
                    p = kb - 4 * qt
                    c0 = 128 * p if p > 0 else 0
                    it["c0"] = c0
                    n = 512 - c0
                    zbk, ZB = zb.next()
                    it["zb"] = (zbk, ZB)
                    t0 = qt * 512
                    P.op("pe", "matmul", dict(out=zbk[:, c0:512], lhsT=k[:, ts(kb, 128)], rhs=q[:, t0 + c0:t0 + 512], start=True, stop=(p < 0)),
                         reads=[Q, K], writes=[ZB])
                    if p >= 0:
                        P.op("pe", "matmul", dict(out=zbk[:, c0:c0 + 128], lhsT=ident, rhs=nma, start=False, stop=True), reads=[CST], writes=[ZB])
                    ee, EE, _ = er.next()
                    P.op("act", "activation", dict(out=ee[:, 0:n], in_=zbk[:, c0:512], func=AF.Exp), reads=[ZB], writes=[EE])
                    sp, SP, _ = spr.next()
                    P.op("act", "activation", dict(out=sp[:, 0:n], in_=ee[:, 0:n], func=AF.Ln, bias=1.0, scale=1.0), reads=[EE], writes=[SP])
                    it["sp"] = (sp, SP)

                def stage2(it):
                    e, qt = it["e"], it["qt"]
                    c0 = it["c0"]
                    n = 512 - c0
                    zbk, ZB = it["zb"]
                    sp, SP = it["sp"]
                    q, Q, k, K = it["qk"]
                    if it["first"]:
                        if e == 0:
                            state["ob"] = obr.next()
                            obk, OB = state["ob"]
                            P.op("pe", "matmul", dict(out=obk[:], lhsT=zer, rhs=q[:, 0:512], start=True, stop=True), reads=[CST, Q], writes=[OB])
                        P.op("pe", "matmul", dict(out=cbk[:], lhsT=zer, rhs=q[:, 0:512], start=True, stop=True), reads=[CST, Q], writes=[CBK])
                    it["ob"] = state["ob"]
                    P.op("pe", "matmul", dict(out=zbk[:, c0:512], lhsT=nlu, rhs=sp[:, 0:n], start=False, stop=it["first"], skip_group_check=True),
                         reads=[SP, CST], writes=[ZB])
                    if not it["first"]:
                        cy, CY = state["carry"]
                        P.op("pe", "matmul", dict(out=zbk[:, c0:512], lhsT=e0m, rhs=cy[:, c0:512], start=False, stop=True, skip_group_check=True),
                             reads=[CY, CST], writes=[ZB])
                    if not it["last"]:
                        P.op("pe", "matmul", dict(out=cbk[:, c0:512], lhsT=neg1, rhs=sp[:, 0:n], start=False, stop=True, skip_group_check=True),
                             reads=[SP, CST], writes=[CBK])
                        cy, CY, _ = cr.next()
                        P.op("dve", "tensor_copy", dict(out=cy[:], in_=cbk[:]), reads=[CBK], writes=[CY])
                        state["carry"] = (cy, CY)
                    pp, PP, _ = pr.next()
                    P.op("act", "activation", dict(out=pp[:, 0:n], in_=zbk[:, c0:512], func=AF.Exp), reads=[ZB], writes=[PP])
                    it["pp"] = (pp, PP)

                def stage3(it):
                    s, j, qt, e, kb = it["s"], it["j"], it["qt"], it["e"], it["kb"]
                    c0 = it["c0"]
                    n = 512 - c0
                    pp, PP = it["pp"]
                    v, V = it["v"]
                    obk, OB = it["ob"]
                    P.op("pe", "matmul", dict(out=obk[:, c0:512], lhsT=v[:, kb, (2 * j + e) * 128:(2 * j + e + 1) * 128], rhs=pp[:, 0:n], start=False, stop=(it["last"] and e == 1), skip_group_check=True),
                         reads=[V, PP], writes=[OB])
                    if it["last"] and e == 1:
                        st, ST, ssem = it["stg"]
                        P.op("dve", "tensor_copy", dict(out=st[:, qt, :], in_=obk[:]), reads=[OB], writes=[ST])
                        if qt == 3:
                            P.dma("pool", ssem, dict(out=o_d[ts(j, 128), ts(s, SEQ)].rearrange("p (a n) -> p a n", a=4), in_=st[:]), reads=[ST])

                n_it = len(items)
                for i in range(n_it + 2):
                    if i < n_it:
                        stage1(items[i])
                    if 0 <= i - 1 < n_it:
                        stage2(items[i - 1])
                    if 0 <= i - 2 < n_it:
                        stage3(items[i - 2])
                P.barrier()

        def phase_attn_b(l):
            with ExitStack() as pes:
                vr = Ring(P, pes, "bv", 2, [128, 16, 512], BF16, dma=True)
                qr = Ring(P, pes, "bq", 4, [68, SEQ], BF16, dma=True)
                kr = Ring(P, pes, "bk", 4, [68, SEQ], BF16, dma=True)
                pr = Ring(P, pes, "bp", 4, [128, 512], BF16)
                rdr = Ring(P, pes, "brd", 2, [128, 512], F32)
                rr = Ring(P, pes, "br", 4, [128, 512], F32)
                o32 = Ring(P, pes, "bo", 2, [128, 512], F32)
                sqr = Ring(P, pes, "bsq", 2, [128, 512], BF16)
                rsr = Ring(P, pes, "brs", 2, [128, 512], F32)
                ost = Ring(P, pes, "bos", 2, [128, 4, 512], BF16, dma="pool")
                zb = BankRing(banks[0:4])
                ndr = BankRing([(banks[4], banks[5]), (banks[6], banks[7])])
                li = lam_init(l)
                items = []
                for s in range(2):
                    for h in range(4):
                        for qt in range(4):
                            for m in range(2):
                                nb = 4 * qt + 4
                                for kb in range(nb):
                                    items.append(dict(s=s, h=h, qt=qt, m=m, kb=kb, first=(kb == 0), last=(kb == nb - 1)))
                state = {}
                deferred = []

                def stage1(it):
                    s, h, qt, m, kb = it["s"], it["h"], it["qt"], it["m"], it["kb"]
                    if it["first"] and qt == 0 and m == 0:
                        for mm in range(2):
                            u = h * 2 + mm
                            q, Q, qsem = qr.next()
                            P.dma("sp", qsem, dict(out=q[0:64, :], in_=qk_d[1536 + u * 64:1536 + (u + 1) * 64, ts(s, SEQ)]), writes=[Q])
                            P.dma("sp", qsem, dict(out=q[64:68, :], in_=augq_d[h]), writes=[Q])
                            k, K, ksem = kr.next()
                            P.dma("sp", ksem, dict(out=k[0:64, :], in_=qk_d[2048 + u * 64:2048 + (u + 1) * 64, ts(s, SEQ)]), writes=[K])
                            P.dma("sp", ksem, dict(out=k[64:68, :], in_=augk_d[h]), writes=[K])
                            state[("q", mm)] = (q, Q)
                            state[("k", mm)] = (k, K)
                        if h == 0:
                            v, V, vsem = vr.next()
                            P.dma("sp", vsem, dict(out=v[:], in_=v_d[1][ts(s, SEQ), :].rearrange("(b p) n -> p b n", p=128)), writes=[V])
                            state["v"] = (v, V)
                        state["st"] = ost.next()
                    it["v"] = state["v"]
                    it["stg"] = state["st"]
                    q, Q = state[("q", m)]
                    k, K = state[("k", m)]
                    p = kb - 4 * qt
                    c0 = 128 * p if p > 0 else 0
                    it["c0"] = c0
                    zbk, ZB = zb.next()
                    t0 = qt * 512
                    P.op("pe", "matmul", dict(out=zbk[:, c0:512], lhsT=k[0:68, ts(kb, 128)], rhs=q[0:68, t0 + c0:t0 + 512], start=True, stop=(p < 0)),
                         reads=[Q, K], writes=[ZB])
                    if p >= 0:
                        P.op("pe", "matmul", dict(out=zbk[:, c0:c0 + 128], lhsT=ident, rhs=cst[:, C_CORR + h * 128:C_CORR + (h + 1) * 128], start=False, stop=True),
                             reads=[CST], writes=[ZB])
                    pp, PP, _ = pr.next()
                    P.op("act", "activation", dict(out=pp[:, 0:512 - c0], in_=zbk[:, c0:512], func=AF.Exp), reads=[ZB], writes=[PP])
                    it["pp"] = (pp, PP)

                def stage2(it):
                    s, h, qt, m, kb = it["s"], it["h"], it["qt"], it["m"], it["kb"]
                    c0 = it["c0"]
                    n = 512 - c0
                    pp, PP = it["pp"]
                    v, V = it["v"]
                    if it["first"]:
                        state["nd"] = ndr.next()
                    (nbk, NB), (dbk, DB) = state["nd"]
                    P.op("pe", "matmul", dict(out=nbk[:, c0:512], lhsT=v[:, kb, ts(h, 128)], rhs=pp[:, 0:n], start=it["first"], stop=it["last"], skip_group_check=True),
                         reads=[V, PP], writes=[NB])
                    P.op("pe", "matmul", dict(out=dbk[:, c0:512], lhsT=ones, rhs=pp[:, 0:n], start=it["first"], stop=it["last"], skip_group_check=True),
                         reads=[CST, PP], writes=[DB])
                    if it["last"]:
                        deferred.append([2, lambda it=it, nbk=nbk, NB=NB, dbk=dbk, DB=DB: combine(it, nbk, NB, dbk, DB)])

                def combine(it, nbk, NB, dbk, DB):
                    s, h, qt, m, kb = it["s"], it["h"], it["qt"], it["m"], it["kb"]
                    if True:
                        rd, RD, _ = rdr.next()
                        P.op("act", "activation", dict(out=rd[:], in_=dbk[:], func=AF.Ln), reads=[DB], writes=[RD])
                        P.op("act", "activation", dict(out=rd[:], in_=rd[:], func=AF.Exp, scale=-1.0), writes=[RD])
                        r, R, _ = rr.next()
                        P.op("dve", "tensor_tensor", dict(out=r[:], in0=nbk[:], in1=rd[:], op=ALU.mult), reads=[NB, RD], writes=[R])
                        state[("r", m)] = (r, R)
                        if m == 1:
                            r0, R0 = state[("r", 0)]
                            o, O, _ = o32.next()
                            P.op("dve", "scalar_tensor_tensor", dict(out=o[:], in0=r[:], scalar=lam_s[:, l:l + 1], in1=r0[:], op0=ALU.mult, op1=ALU.add),
                                 reads=[R, R0, LAM], writes=[O])
                            sq, SQ, _ = sqr.next()
                            P.op("act", "activation", dict(out=sq[:], in_=o[:], func=AF.Square), reads=[O], writes=[SQ])
                            deferred.append([4, lambda it=it, o=o, O=O, sq=sq, SQ=SQ: combine2(it, o, O, sq, SQ)])

                def combine2(it, o, O, sq, SQ):
                    s, h, qt, m, kb = it["s"], it["h"], it["qt"], it["m"], it["kb"]
                    if True:
                        if True:
                            ssb, SSB = zb.next()
                            P.op("pe", "matmul", dict(out=ssb[:], lhsT=ones, rhs=sq[:], start=True, stop=True), reads=[SQ, CST], writes=[SSB])
                            rs, RS, _ = rsr.next()
                            sc = 1.0 / (128.0 * (1 - li) ** 2)
                            P.op("act", "activation", dict(out=rs[:], in_=ssb[:], func=AF.Ln, bias=EPS / (1 - li) ** 2, scale=sc), reads=[SSB], writes=[RS])
                            P.op("act", "activation", dict(out=rs[:], in_=rs[:], func=AF.Exp, scale=-0.5), writes=[RS])
                            st, ST, ssem = it["stg"]
                            P.op("dve", "scalar_tensor_tensor", dict(out=st[:, qt, :], in0=o[:], scalar=prm[:, P_GSUB + l:P_GSUB + l + 1], in1=rs[:], op0=ALU.mult, op1=ALU.mult),
                                 reads=[O, RS, PRM], writes=[ST])
                            if qt == 3:
                                P.dma("pool", ssem, dict(out=o_d[512 + h * 128:512 + (h + 1) * 128, ts(s, SEQ)].rearrange("p (a n) -> p a n", a=4), in_=st[:]), reads=[ST])

                n_it = len(items)
                for i in range(n_it + 2):
                    if i < n_it:
                        stage1(items[i])
                    if 0 <= i - 2 < n_it:
                        stage2(items[i - 2])
                    for d_ in deferred:
                        d_[0] -= 1
                    while deferred and deferred[0][0] <= 0:
                        deferred.pop(0)[1]()
                while deferred:
                    deferred.pop(0)[1]()
                P.barrier()

        def phase_attn_c(l):
            with ExitStack() as pes:
                vr = Ring(P, pes, "cv", 2, [128, 16, 1024], BF16, dma=True)
                qr = Ring(P, pes, "cq", 2, [128, SEQ], BF16, dma=True)
                kr = Ring(P, pes, "ck", 4, [128, SEQ], BF16, dma=True)
                pr = Ring(P, pes, "cp", 4, [128, 512], BF16)
                rdr = Ring(P, pes, "crd", 2, [128, 512], F32)
                ost = Ring(P, pes, "cos", 2, [128, 4, 512], BF16, dma="pool")
                cb = sb("ccb", [128, 8, 640], BF16, pes)
                CB = Buf()
                mk = sb("cmk", [128, 640], F32, pes)
                MK = Buf()
                rbr = Ring(P, pes, "crb", 2, [128, 640], F32, dma=True)
                msem = P.new_sem(pes)
                P.dma("sp", msem, dict(out=mk[:], in_=maskc_d), writes=[MK])
                for h in range(8):
                    rb, RB, rsem = rbr.next()
                    P.dma("sp", rsem, dict(out=rb[:], in_=rbt_d[l][h]), writes=[RB])
                    P.op("dve", "tensor_tensor", dict(out=cb[:, h, :], in0=rb[:], in1=mk[:], op=ALU.add), reads=[RB, MK], writes=[CB])
                zb = BankRing(banks[0:4])
                ndr = BankRing(banks[4:8])
                items = []
                for s in range(2):
                    for j in range(4):
                        for qt in range(4):
                            for e in range(2):
                                ps_ = [p for p in range(-4, 4) if 4 * qt + p >= 0]
                                first_p = -1 if -1 in ps_ else 0
                                order = [first_p] + [p for p in ps_ if p != first_p]
                                for ii, p in enumerate(order):
                                    items.append(dict(s=s, j=j, qt=qt, e=e, p=p, first=(ii == 0), last=(ii == len(order) - 1)))
                state = {}
                deferred = []

                def stage1(it):
                    s, j, qt, e, p = it["s"], it["j"], it["qt"], it["e"], it["p"]
                    if it["first"] and qt == 0 and e == 0:
                        q, Q, qsem = qr.next()
                        P.dma("sp", qsem, dict(out=q[:], in_=qk_d[3072 + j * 128:3072 + (j + 1) * 128, ts(s, SEQ)]), writes=[Q])
                        state["q"] = (q, Q)
                        for e_ in range(2):
                            k, K, ksem = kr.next()
                            P.dma("sp", ksem, dict(out=k[:], in_=kpc_d[ts(2 * j + e_, 128), ts(s, SEQ)]), writes=[K])
                            state[("k", e_)] = (k, K)
                        if j == 0:
                            v, V, vsem = vr.next()
                            P.dma("sp", vsem, dict(out=v[:], in_=vpc_d[ts(s, SEQ), :].rearrange("(b p) n -> p b n", p=128)), writes=[V])
                            state["v"] = (v, V)
                        state["st"] = ost.next()
                    it["v"] = state["v"]
                    it["stg"] = state["st"]
                    q, Q = state["q"]
                    k, K = state[("k", e)]
                    h = 2 * j + e
                    kb = 4 * qt + p
                    qa = max(0, p)
                    qb_ = min(3, p + 4)
                    c0, c1 = 128 * qa, 128 * (qb_ + 1)
                    r0 = qa - p
                    it["c"] = (c0, c1)
                    it["kb"] = kb
                    zbk, ZB = zb.next()
                    t0 = qt * 512
                    P.op("pe", "matmul", dict(out=zbk[:, c0:c1], lhsT=k[:, ts(kb, 128)], rhs=q[:, t0 + c0:t0 + c1], start=True, stop=False),
                         reads=[Q, K], writes=[ZB])
                    P.op("pe", "matmul", dict(out=zbk[:, c0:c1], lhsT=ident, rhs=cb[:, h, r0 * 128:r0 * 128 + (c1 - c0)], start=False, stop=True),
                         reads=[CST, CB], writes=[ZB])
                    pp, PP, _ = pr.next()
                    P.op("act", "activation", dict(out=pp[:, 0:c1 - c0], in_=zbk[:, c0:c1], func=AF.Exp), reads=[ZB], writes=[PP])
                    it["pp"] = (pp, PP)

                def stage2(it):
                    s, j, qt, e, p = it["s"], it["j"], it["qt"], it["e"], it["p"]
                    c0, c1 = it["c"]
                    kb = it["kb"]
                    pp, PP = it["pp"]
                    v, V = it["v"]
                    if it["first"]:
                        state["nd"] = ndr.next()
                    nbk, NB = state["nd"]
                    P.op("pe", "matmul", dict(out=nbk[:, c0:c1], lhsT=v[:, kb, (2 * j + e) * 128:(2 * j + e + 1) * 128], rhs=pp[:, 0:c1 - c0],
                                              start=it["first"], stop=it["last"], skip_group_check=True), reads=[V, PP], writes=[NB])
                    if it["last"]:
                        deferred.append([2, lambda it=it, nbk=nbk, NB=NB: combine(it, nbk, NB)])

                def combine(it, nbk, NB):
                    s, j, qt, e = it["s"], it["j"], it["qt"], it["e"]
                    rd, RD, _ = rdr.next()
                    P.op("act", "activation", dict(out=rd[0:64, :], in_=nbk[64:128, :], func=AF.Ln), reads=[NB], writes=[RD])
                    P.op("act", "activation", dict(out=rd[0:64, :], in_=rd[0:64, :], func=AF.Exp, scale=-1.0), writes=[RD])
                    st, ST, ssem = it["stg"]
                    P.op("dve", "tensor_tensor", dict(out=st[e * 64:(e + 1) * 64, qt, :], in0=nbk[0:64, :], in1=rd[0:64, :], op=ALU.mult), reads=[NB, RD], writes=[ST])
                    if qt == 3 and e == 1:
                        P.dma("pool", ssem, dict(out=o_d[1024 + j * 128:1024 + (j + 1) * 128, ts(s, SEQ)].rearrange("p (a n) -> p a n", a=4), in_=st[:]), reads=[ST])

                n_it = len(items)
                wbv = wbr_d[l].rearrange("i (kc p) n -> p (i kc) n", p=128)
                wov = w_out_d[l].rearrange("(kc p) n -> p kc n", p=128)
                pf = [(wbv, c, c) for c in range(12)] + [(wov, c, 12 + c) for c in range(8)]
                for i in range(n_it + 2):
                    if i % 20 == 10 and pf:
                        src_, cs_, cd_ = pf.pop(0)
                        prefetch(src_, [(cs_, cd_)], engs=("dve",))
                    if i < n_it:
                        stage1(items[i])
                    if 0 <= i - 2 < n_it:
                        stage2(items[i - 2])
                    for d_ in deferred:
                        d_[0] -= 1
                    while deferred and deferred[0][0] <= 0:
                        deferred.pop(0)[1]()
                while deferred:
                    deferred.pop(0)[1]()
                while pf:
                    src_, cs_, cd_ = pf.pop(0)
                    prefetch(src_, [(cs_, cd_)], engs=("dve",))
                P.barrier()

        def phase_merge(l, src_d, dst_d, gcol, hout_d):
            with ExitStack() as pes:
                orr = Ring(P, pes, "mo", 1, [128, 12, 512], BF16, dma=True)
                epr = epilogue_rings(pes, "me")
                gr = Ring(P, pes, "mg", 1, [128, 24, 512], BF16, dma=True)
                xr = Ring(P, pes, "mx", 2, [128, 8, 512], F32, dma=True)
                xst = [P.new_sem(pes, "pool") for _ in range(2)]
                mr = Ring(P, pes, "mm", 2, [128, 512], F32)
                tr = Ring(P, pes, "mt", 2, [128, 512], F32)
                mbr = Ring(P, pes, "mb", 1, [128, 8, 512], BF16)
                br = BankRing(banks)
                xv = src_d.rearrange("(c p) t -> p c t", p=128)
                dv = dst_d.rearrange("(c p) t -> p c t", p=128)
                ov = o_d.rearrange("(c p) t -> p c t", p=128)
                gv = gt_d.rearrange("(c p) t -> p c t", p=128)
                pend = []
                for t in range(8):
                    o, O, osem = orr.next()
                    P.dma("sp", osem, dict(out=o[:], in_=ov[:, :, ts(t, 512)]), writes=[O])
                    g, G, gsem = gr.next()
                    P.dma("sp", gsem, dict(out=g[:], in_=gv[:, :, ts(t, 512)]), writes=[G])
                    x, X, xsem = xr.next()
                    P.dma("sp", xsem, dict(out=x[:], in_=xv[:, :, ts(t, 512)]), writes=[X])
                    mbf, MBF, _ = mbr.next()
                    sq_, SQ_, _ = epr[0].next()
                    for oc in range(8):
                        m, M, _ = mr.next()
                        for i in range(3):
                            bk, BK = br.next()
                            for kc in range(4):
                                P.op("pe", "matmul", dict(out=bk[:], lhsT=arena[:, i * 4 + kc, ts(oc, 128)], rhs=o[:, i * 4 + kc, :], start=(kc == 0), stop=(kc == 3)),
                                     reads=[ARENA[i * 4 + kc], O], writes=[BK])
                            if i == 0:
                                P.op("dve", "tensor_tensor", dict(out=m[:], in0=bk[:], in1=g[:, oc, :], op=ALU.mult), reads=[BK, G], writes=[M])
                            else:
                                tt, TT, _ = tr.next()
                                P.op("dve", "tensor_tensor", dict(out=tt[:], in0=bk[:], in1=g[:, i * 8 + oc, :], op=ALU.mult), reads=[BK, G], writes=[TT])
                                if i == 1:
                                    P.op("pool", "tensor_tensor", dict(out=m[:], in0=m[:], in1=tt[:], op=ALU.add), reads=[TT], writes=[M])
                                else:
                                    P.op("pool", "tensor_tensor", dict(out=mbf[:, oc, :], in0=m[:], in1=tt[:], op=ALU.add), reads=[TT, M], writes=[MBF])
                        if oc == 1 and pend:
                            pend.pop(0)()
                    for oc in range(8):
                        bk, BK = br.next()
                        for kc in range(8):
                            P.op("pe", "matmul", dict(out=bk[:], lhsT=arena[:, 12 + kc, ts(oc, 128)], rhs=mbf[:, kc, :], start=(kc == 0), stop=(kc == 7)),
                                 reads=[ARENA[12 + kc], MBF], writes=[BK])
                        P.op("dve", "tensor_tensor", dict(out=x[:, oc, :], in0=bk[:], in1=x[:, oc, :], op=ALU.add), reads=[BK], writes=[X])
                        P.op("act", "activation", dict(out=sq_[:, oc, :], in_=x[:, oc, :], func=AF.Square), reads=[X], writes=[SQ_])
                    P.dma("pool", xst[t % 2], dict(out=dv[:, :, ts(t, 512)], in_=x[:]), reads=[X])
                    pend.append(lambda t=t, x=x, X=X, sq_=sq_, SQ_=SQ_: norm_epi_b(epr, t, x, X, sq_, SQ_, gcol, hout_d, br))
                while pend:
                    pend.pop(0)()
                P.barrier()

        def phase_ffn_up(l, hT, HT, pre=None):
            with ExitStack() as pes:
                wgf = Ring(P, pes, "fgf", 2, [128, 8, 256], F32, dma=True)
                wuf = Ring(P, pes, "fuf", 2, [128, 8, 256], F32, dma=True)
                wgb = Ring(P, pes, "fgb", 2, [128, 8, 256], BF16)
                wub = Ring(P, pes, "fub", 2, [128, 8, 256], BF16)
                sgr = Ring(P, pes, "fsg", 3, [128, 512], F32)
                stg = Ring(P, pes, "fst", 3, [128, 4, 512], BF16, dma="pool")
                br = BankRing(banks)
                wv = w_gu_d[l].rearrange("(kc p) n -> p kc n", p=128)
                for sl in range(11):
                    wg, WG, gsem = wgf.next()
                    P.dma("sp", gsem, dict(out=wg[:], in_=wv[:, :, ts(sl, 256)]), writes=[WG])
                    wu, WU, usem = wuf.next()
                    P.dma("sp", usem, dict(out=wu[:], in_=wv[:, :, DFF + sl * 256:DFF + (sl + 1) * 256]), writes=[WU])
                    if sl == 0 and pre is not None:
                        pre()
                    gb, GB, _ = wgb.next()
                    P.op("dve", "tensor_copy", dict(out=gb[:], in_=wg[:]), reads=[WG], writes=[GB])
                    ub, UB, _ = wub.next()
                    P.op("dve", "tensor_copy", dict(out=ub[:], in_=wu[:]), reads=[WU], writes=[UB])
                    for j in range(2):
                        fc = sl * 2 + j
                        for half in range(2):
                            st, ST, ssem = stg.next()
                            for tq in range(4):
                                t = half * 4 + tq
                                gk, GK = br.next()
                                for kc in range(8):
                                    P.op("pe", "matmul", dict(out=gk[:], lhsT=gb[:, kc, ts(j, 128)], rhs=hT[:, kc, ts(t, 512)], start=(kc == 0), stop=(kc == 7)),
                                         reads=[HT[t], GB], writes=[GK])
                                uk, UK = br.next()
                                for kc in range(8):
                                    P.op("pe", "matmul", dict(out=uk[:], lhsT=ub[:, kc, ts(j, 128)], rhs=hT[:, kc, ts(t, 512)], start=(kc == 0), stop=(kc == 7)),
                                         reads=[HT[t], UB], writes=[UK])
                                sg, SG, _ = sgr.next()
                                P.op("act", "activation", dict(out=sg[:], in_=gk[:], func=AF.Silu), reads=[GK], writes=[SG])
                                P.op("dve", "tensor_tensor", dict(out=st[:, tq, :], in0=uk[:], in1=sg[:], op=ALU.mult), reads=[UK, SG], writes=[ST])
                            P.dma("pool", ssem, dict(out=ac_d[ts(fc, 128), ts(half, 2048)].rearrange("p (a n) -> p a n", a=4), in_=st[:]), reads=[ST])
                    prefetch(w_dn_d[l].rearrange("(kc p) n -> p kc n", p=128), [(2 * sl, 2 * sl), (2 * sl + 1, 2 * sl + 1)])
                P.barrier()

        def phase_ffn_down(l, src_d, dst_d, gcol, hout_d):
            with ExitStack() as pes:
                ar = Ring(P, pes, "da", 2, [128, 22, 512], BF16, dma=True)
                epr = epilogue_rings(pes, "de") if hout_d is not None else None
                xr = Ring(P, pes, "dx", 2, [128, 8, 512], F32, dma=True)
                xst = [P.new_sem(pes, "pool") for _ in range(2)]
                br = BankRing(banks)
                xv = src_d.rearrange("(c p) t -> p c t", p=128)
                dv = dst_d.rearrange("(c p) t -> p c t", p=128)
                av = ac_d.rearrange("(c p) t -> p c t", p=128)
                pend = []
                for t in range(8):
                    a, A, asem = ar.next()
                    P.dma("sp", asem, dict(out=a[:], in_=av[:, :, ts(t, 512)]), writes=[A])
                    x, X, xsem = xr.next()
                    P.dma("sp", xsem, dict(out=x[:], in_=xv[:, :, ts(t, 512)]), writes=[X])
                    if hout_d is not None:
                        sq_, SQ_, _ = epr[0].next()
                    for oc in range(8):
                        bk, BK = br.next()
                        for kc in range(22):
                            P.op("pe", "matmul", dict(out=bk[:], lhsT=arena[:, kc, ts(oc, 128)], rhs=a[:, kc, :], start=(kc == 0), stop=(kc == 21)),
                                 reads=[ARENA[kc], A], writes=[BK])
                        P.op("dve", "tensor_tensor", dict(out=x[:, oc, :], in0=bk[:], in1=x[:, oc, :], op=ALU.add), reads=[BK], writes=[X])
                        if hout_d is not None:
                            P.op("act", "activation", dict(out=sq_[:, oc, :], in_=x[:, oc, :], func=AF.Square), reads=[X], writes=[SQ_])
                        if oc == 1 and pend:
                            pend.pop(0)()
                    P.dma("pool", xst[t % 2], dict(out=dv[:, :, ts(t, 512)], in_=x[:]), reads=[X])
                    if hout_d is not None:
                        pend.append(lambda t=t, x=x, X=X, sq_=sq_, SQ_=SQ_: norm_epi_b(epr, t, x, X, sq_, SQ_, gcol, hout_d, br))
                while pend:
                    pend.pop(0)()
                P.barrier()

        phases = 0

        def done():
            nonlocal phases
            phases += 1
            return stop_after is not None and phases >= stop_after

        stop = False
        for l in range(n_layers):
            src = xT_d if l == 0 else xs_d
            with ExitStack() as les:
                hT = sb("hT", [128, 8, T], BF16, les)
                HT = [Buf("hT%d" % i) for i in range(8)]
                if l == 0:
                    zt = sb("zt", [128, 4, 1024], BF16, les)
                    ot = sb("ot", [128, 4, 1024], BF16, les)
                    ZT = Buf()
                    OT = Buf()
                    P.op("dve", "memset", dict(ap=zt[:], constant=0.0), writes=[ZT])
                    P.op("dve", "memset", dict(ap=ot[:], constant=1.0), writes=[OT])
                    fsem = P.new_sem()
                    for a_ in range(8):
                        P.dma("pool", fsem, dict(out=kpa_d[ts(a_, 128), :].rearrange("p (a n) -> p a n", a=4), in_=zt[:]), reads=[ZT])
                        P.dma("pool", fsem, dict(out=kpc_d[ts(a_, 128), :].rearrange("p (a n) -> p a n", a=4), in_=zt[:]), reads=[ZT])
                        P.dma("pool", fsem, dict(out=vpa_d[ts(a_, 512), :].rearrange("(a p) n -> p a n", p=128), in_=zt[:]), reads=[ZT])
                        P.dma("pool", fsem, dict(out=vpc_d[ts(a_, 512), :].rearrange("(a p) n -> p a n", p=128), in_=ot[:]), reads=[OT])
                    phase_norm(src, P_GMIX + l * 8, hT, HT)
                    phase_proj(l, hT, HT)
                else:
                    phase_proj(l, hT, HT, pre=lambda: load_h(h1_d, hT, HT, les))
            if done():
                break
            phase_attn_a(l)
            if done():
                break
            phase_attn_b(l)
            if done():
                break
            phase_attn_c(l)
            if done():
                break
            phase_merge(l, src, xs_d, P_GFFN + l * 8, h2_d)
            if done():
                break
            with ExitStack() as les:
                hT = sb("hT2", [128, 8, T], BF16, les)
                HT = [Buf("hT2%d" % i) for i in range(8)]
                phase_ffn_up(l, hT, HT, pre=lambda: load_h(h2_d, hT, HT, les))
            if done():
                break
            last = (l == n_layers - 1)
            phase_ffn_down(l, xs_d, yT_d if last else xs_d, P_GMIX + (l + 1) * 8 if not last else 0, None if last else h1_d)
            if done():
                break
        P.barrier()
        build.last_ninst = dict(P.ninst)
    return nc


def _bf(a):
    return np.asarray(a, dtype=np.float32).astype(ml_dtypes.bfloat16)


def make_consts():
    i = np.arange(128)[:, None]
    j = np.arange(128)[None, :]
    cst = np.zeros((128, NCST), np.float32)
    cst[:, C_ID:C_ID + 128] = (i == j)
    cst[:, C_ONE:C_ONE + 128] = 1.0
    cst[:, C_BLK:C_BLK + 128] = ((i // 64) == (j // 64))
    cst[:, C_NLU:C_NLU + 128] = -1.0 * (i >= j)
    cst[:, C_NMA:C_NMA + 128] = np.where(i < j, 0.0, NEG)
    cst[:, C_NEG1:C_NEG1 + 128] = -1.0
    for h in range(4):
        sl = SLOPES[h]
        ok = (i // 64) <= (j // 64)
        corr = np.where(ok, np.where(i > j, -2.0 * sl * (i - j), 0.0), NEG)
        cst[:, C_CORR + h * 128:C_CORR + (h + 1) * 128] = corr
    cst[:, C_E0:C_E0 + 128] = (i == 0)
    cst[:, C_HM0:C_HM0 + 64] = 1.0
    cst[:, C_HM1 + 64:C_HM1 + 128] = 1.0
    pos = np.arange(SEQ)
    augq = np.zeros((4, 4, SEQ), np.float32)
    augk = np.zeros((4, 4, SEQ), np.float32)
    for h in range(4):
        sl = SLOPES[h]
        augq[h, 0] = -sl * 64.0 * (pos // 64)
        augq[h, 1] = -sl * (pos % 64)
        augq[h, 2] = 1.0
        augq[h, 3] = 1.0
        augk[h, 0] = 1.0
        augk[h, 1] = 1.0
        augk[h, 2] = sl * 64.0 * (pos // 64)
        augk[h, 3] = sl * (pos % 64)
    maskc = np.zeros((128, 640), np.float32)
    a_ = (np.arange(128)[None, :] // 64)
    b_ = (np.arange(128)[:, None] // 64)
    maskc[:, 0:128] = np.where(a_ >= b_, 0.0, NEG)
    maskc[:, 512:640] = np.where(a_ <= b_, 0.0, NEG)
    return _bf(cst), _bf(augq), _bf(augk), maskc


def make_rbt(rel_bias):
    i = np.arange(128)[:, None]
    c = np.arange(640)[None, :]
    idx = np.clip(c - i, -128, 128) + 128
    return np.ascontiguousarray(rel_bias[:, :, idx]).astype(np.float32)


def make_prm(norm_mix_g, norm_ffn_g, b_gate, qk_g_diff, qk_g_ch, subln_g, lambda_qk):
    prm = np.zeros((128, NPRM), np.float32)
    p = np.arange(128)
    for l in range(DEPTH):
        prm[:, P_GMIX + l * 8:P_GMIX + (l + 1) * 8] = norm_mix_g[l].reshape(8, 128).T
        prm[:, P_GFFN + l * 8:P_GFFN + (l + 1) * 8] = norm_ffn_g[l].reshape(8, 128).T
        prm[:, P_BG + l * 24:P_BG + (l + 1) * 24] = b_gate[l].reshape(24, 128).T
        for k in range(2):
            prm[:, P_GQD + 2 * l + k] = qk_g_diff[l, k][p % 64]
            prm[:, P_GQC + 2 * l + k] = qk_g_ch[l, k][p % 64]
        prm[:, P_GSUB + l] = subln_g[l]
        prm[:, P_LQK + l * 256:P_LQK + (l + 1) * 256] = lambda_qk[l].reshape(1, 256)
    return prm


_NC_CACHE = {}


def kernel(x, norm_mix_g, w_in, b_gate, qk_g_diff, lambda_qk, subln_g, qk_g_ch, rel_bias,
           w_branch_sb, w_branch_diff, w_branch_ch, w_out, norm_ffn_g, w_gu, w_down):
    f = lambda a: np.ascontiguousarray(np.asarray(a, dtype=np.float32))
    x = f(x)
    cst, augq, augk, maskc = make_consts()
    shared = {
        "w_in": f(w_in),
        "w_br": np.ascontiguousarray(np.stack([f(w_branch_sb), f(w_branch_diff), f(w_branch_ch)], axis=1)),
        "w_out": f(w_out), "w_gu": f(w_gu), "w_down": f(w_down),
        "rbt": make_rbt(f(rel_bias)),
        "prm": make_prm(f(norm_mix_g), f(norm_ffn_g), f(b_gate), f(qk_g_diff), f(qk_g_ch), f(subln_g), f(lambda_qk)),
        "cst": cst, "augq": augq, "augk": augk, "maskc": maskc,
    }
    in_maps = []
    for c in range(NCORES):
        m = dict(shared)
        m["xT"] = np.ascontiguousarray(x[2 * c:2 * c + 2].reshape(T, D).T)
        in_maps.append(m)
    if "nc" not in _NC_CACHE:
        _NC_CACHE["nc"] = build()
    res = run_bass_kernel_spmd(_NC_CACHE["nc"], in_maps, core_ids=list(range(NCORES)))
    out = np.empty((16, SEQ, D), np.float32)
    for c in range(NCORES):
        out[2 * c:2 * c + 2] = np.asarray(res.results[c]["yT"]).T.reshape(2, SEQ, D)
    return out
```
